# Optimizing a Trainium2 kernel written in Bass

```python
import jax, jax.numpy as jnp
from jax import lax
import numpy as np

D_MODEL = 2048
BATCH = 4
SEQ = 2048
DEPTH = 2

SB_HEADS = 8
SB_HEAD_DIM = 128
SB_WIDTH = SB_HEADS * SB_HEAD_DIM
SB_BLOCK = 128
RW_HEAD_DIM = 64
RW_WIDTH = D_MODEL - SB_WIDTH
RW_HEADS = RW_WIDTH // RW_HEAD_DIM
RW_DECAY_LORA = 64
RW_AAA_LORA = 64
RW_GATE_LORA = 160
RW_LORA = RW_DECAY_LORA + RW_AAA_LORA + RW_GATE_LORA
RW_SHIFTED = 3 * RW_WIDTH + RW_LORA
EVEN_IN = 3 * SB_WIDTH + RW_SHIFTED
RW_GN_EPS = 64e-5
GLA_HEADS = 4
GLA_KEY = D_MODEL // 2
GLA_VAL = D_MODEL
GLA_DK = GLA_KEY // GLA_HEADS
GLA_DV = GLA_VAL // GLA_HEADS
GLA_GATE_RANK = 16
GLA_TAU = 16.0
GLA_CHUNK = 64
ODD_IN = 2 * GLA_KEY + 2 * GLA_VAL + GLA_GATE_RANK
D_FF = 5632
CONV_W = 3
N_EVEN = (DEPTH + 1) // 2
N_ODD = DEPTH // 2
DN_ALPHA = (2 * DEPTH) ** 0.25
DN_BETA = (8 * DEPTH) ** -0.25
LN_EPS = 1e-5

kernel_name = "hybrid_stickbreak_rwkv7_gla_convffn_deepnorm"


def split_last(u, sizes):
    idx, acc = [], 0
    for s in sizes[:-1]:
        acc += s
        idx.append(acc)
    return jnp.split(u, idx, axis=-1)


def layer_norm(x, g, b):
    xf = x.astype(jnp.float32)
    mu = jnp.mean(xf, -1, keepdims=True)
    var = jnp.mean(jnp.square(xf - mu), -1, keepdims=True)
    y = (xf - mu) * lax.rsqrt(var + LN_EPS)
    return (y * g.astype(jnp.float32) + b.astype(jnp.float32)).astype(x.dtype)


def group_norm(x, g, b, n_groups, eps):
    shp = x.shape
    xf = x.astype(jnp.float32).reshape(shp[:-1] + (n_groups, shp[-1] // n_groups))
    mu = jnp.mean(xf, -1, keepdims=True)
    var = jnp.mean(jnp.square(xf - mu), -1, keepdims=True)
    y = ((xf - mu) * lax.rsqrt(var + eps)).reshape(shp)
    return y * g.astype(jnp.float32) + b.astype(jnp.float32)


def token_shift(u, mu):
    prev = jnp.pad(u, ((0, 0), (1, 0), (0, 0)))[:, :-1]
    return u + (prev - u) * mu


def stick_breaking_attention(q, k, v):
    B, H, S, dh = q.shape
    scale = dh ** -0.5
    vf = v.astype(jnp.float32)
    outs = []
    for blk in range(S // SB_BLOCK):
        q0 = blk * SB_BLOCK
        q1 = q0 + SB_BLOCK
        qb = q[:, :, q0:q1].astype(jnp.float32)
        kb = k[:, :, :q1].astype(jnp.float32)
        z = jnp.einsum('bhqd,bhkd->bhqk', qb, kb) * scale
        t_idx = jnp.arange(q0, q1)[:, None]
        s_idx = jnp.arange(q1)[None, :]
        mask = s_idx < t_idx
        log_keep = jnp.where(mask, jax.nn.log_sigmoid(-z), 0.0)
        cum = jnp.cumsum(log_keep, axis=-1)
        suffix = cum[..., -1:] - cum
        w = jnp.where(mask, jnp.exp(jax.nn.log_sigmoid(z) + suffix), 0.0)
        outs.append(jnp.einsum('bhqk,bhkd->bhqd', w, vf[:, :, :q1]))
    return jnp.concatenate(outs, axis=2)


def rwkv7_mix(r, k, v, dw, da, dg, w0, w2, a0, a2, g2, k_k, k_a, r_k, gn_g, gn_b):
    B, S, C = r.shape
    H, N = RW_HEADS, RW_HEAD_DIM
    f32 = jnp.float32
    r, k, v, dw, da, dg = (t.astype(f32) for t in (r, k, v, dw, da, dg))
    w_log = -jax.nn.softplus(-(w0 + jnp.tanh(dw) @ w2)) - 0.5
    decay = jnp.exp(-jnp.exp(w_log))
    a = jax.nn.sigmoid(a0 + da @ a2)
    g = jax.nn.sigmoid(dg) @ g2
    kk = (k * k_k).reshape(B, S, H, N)
    kk = kk * lax.rsqrt(jnp.maximum(jnp.sum(kk * kk, -1, keepdims=True), 1e-24))
    k = k * (1.0 + (a - 1.0) * k_a)
    rh, kh, vh, wh, ah = (t.reshape(B, S, H, N) for t in (r, k, v, decay, a))

    def step(state, inp):
        r_t, k_t, v_t, w_t, kk_t, a_t = inp
        sa = jnp.einsum('bhvk,bhk->bhv', state, -kk_t)
        state = (state * w_t[:, :, None, :]
                 + sa[..., None] * (kk_t * a_t)[:, :, None, :]
                 + v_t[..., None] * k_t[:, :, None, :])
        y = jnp.einsum('bhvk,bhk->bhv', state, r_t)
        return state, y

    xs = tuple(jnp.moveaxis(t, 1, 0) for t in (rh, kh, vh, wh, kk, ah))
    init = jnp.zeros((B, H, N, N), f32)
    _, y = lax.scan(step, init, xs)
    y = jnp.moveaxis(y, 0, 1).reshape(B, S, C)
    y = group_norm(y, gn_g, gn_b, H, RW_GN_EPS)
    bonus = jnp.sum(rh * kh * r_k.astype(f32), -1, keepdims=True) * vh
    return (y + bonus.reshape(B, S, C)) * g


def gla_chunked(q, k, v, log_a):
    B, S, H, dk = q.shape
    dv = v.shape[-1]
    L = GLA_CHUNK
    n = S // L
    f32 = jnp.float32

    def chunk(t):
        return t.astype(f32).reshape(B, n, L, H, t.shape[-1]).transpose(0, 3, 1, 2, 4)

    q, k, v, log_a = chunk(q) * dk ** -0.5, chunk(k), chunk(v), chunk(log_a)
    b = jnp.cumsum(log_a, axis=-2)
    b_last = b[..., -1:, :]
    q_dec = q * jnp.exp(b)
    k_inv = k * jnp.exp(-b)
    k_end = k * jnp.exp(b_last - b)
    causal = jnp.tril(jnp.ones((L, L), bool))
    att = jnp.where(causal, jnp.einsum('bhnid,bhnjd->bhnij', q_dec, k_inv), 0.0)
    o_intra = jnp.einsum('bhnij,bhnjv->bhniv', att, v)

    def step(state, inp):
        qc, kc, vc, dl = inp
        o = jnp.einsum('bhld,bhdv->bhlv', qc, state)
        state = state * dl[..., None] + jnp.einsum('bhld,bhlv->bhdv', kc, vc)
        return state, o

    xs = (jnp.moveaxis(q_dec, 2, 0), jnp.moveaxis(k_end, 2, 0), jnp.moveaxis(v, 2, 0),
          jnp.moveaxis(jnp.exp(b_last[..., 0, :]), 2, 0))
    _, o_inter = lax.scan(step, jnp.zeros((B, H, dk, dv), f32), xs)
    o = o_intra + jnp.moveaxis(o_inter, 0, 2)
    return o.transpose(0, 2, 3, 1, 4).reshape(B, S, H * dv)


def even_mixer(x, w_in, shift_mu, w0, w2, a0, a2, g2, k_k, k_a, r_k, gn_g, gn_b, w_out):
    B, S, _ = x.shape
    u = x @ w_in
    sb, rw = u[..., :3 * SB_WIDTH], u[..., 3 * SB_WIDTH:]
    q, k, v = (t.reshape(B, S, SB_HEADS, SB_HEAD_DIM).transpose(0, 2, 1, 3)
               for t in split_last(sb, [SB_WIDTH] * 3))
    o_sb = stick_breaking_attention(q, k, v).transpose(0, 2, 1, 3).reshape(B, S, SB_WIDTH)
    rw = token_shift(rw, shift_mu)
    r, kr, vr, dw, da, dg = split_last(
        rw, [RW_WIDTH] * 3 + [RW_DECAY_LORA, RW_AAA_LORA, RW_GATE_LORA])
    o_rw = rwkv7_mix(r, kr, vr, dw, da, dg, w0, w2, a0, a2, g2, k_k, k_a, r_k, gn_g, gn_b)
    o = jnp.concatenate([o_sb, o_rw], axis=-1).astype(x.dtype)
    return o @ w_out


def odd_mixer(x, w_in, gate_w2, gate_b, r_b, gn_g, gn_b, w_out):
    B, S, _ = x.shape
    u = x @ w_in
    q, k, v, dg, r = split_last(u, [GLA_KEY, GLA_KEY, GLA_VAL, GLA_GATE_RANK, GLA_VAL])
    log_a = jax.nn.log_sigmoid((dg.astype(jnp.float32) @ gate_w2 + gate_b)) / GLA_TAU
    hd = lambda t, d: t.reshape(B, S, GLA_HEADS, d)
    o = gla_chunked(hd(q, GLA_DK), hd(k, GLA_DK), hd(v, GLA_DV), hd(log_a, GLA_DK))
    o = group_norm(o, gn_g, gn_b, GLA_HEADS, LN_EPS)
    o = o * jax.nn.silu(r.astype(jnp.float32) + r_b)
    return o.astype(x.dtype) @ w_out


def conv_ffn(x, w_up, conv_w, conv_b, w_down):
    S = x.shape[1]
    gate, up = jnp.split(x @ w_up, 2, axis=-1)
    gate = lax.conv_general_dilated(
        gate, conv_w.reshape(CONV_W, 1, D_FF).astype(gate.dtype),
        window_strides=(1,), padding=[(CONV_W - 1, 0)],
        dimension_numbers=('NWC', 'WIO', 'NWC'), feature_group_count=D_FF) + conv_b
    h = jax.nn.gelu(gate, approximate=False) * up
    return h @ w_down


def setup_inputs(seed: int = 0) -> dict:
    key = jax.random.key(seed)
    ks = iter(jax.random.split(key, 48))

    def nrm(shape, scale):
        return jax.random.normal(next(ks), shape, jnp.float32) * scale

    D = D_MODEL
    sd = D ** -0.5
    x = nrm((BATCH, SEQ, D), 1.0)
    even_w_in = jnp.concatenate([
        nrm((N_EVEN, D, 2 * SB_WIDTH), sd),
        nrm((N_EVEN, D, SB_WIDTH), sd * DN_BETA),
        nrm((N_EVEN, D, 2 * RW_WIDTH), sd),
        nrm((N_EVEN, D, RW_WIDTH), sd * DN_BETA),
        nrm((N_EVEN, D, RW_LORA), sd),
    ], axis=-1)
    even_shift_mu = jax.random.uniform(next(ks), (N_EVEN, RW_SHIFTED), jnp.float32)
    rw_w0 = jax.random.uniform(next(ks), (N_EVEN, RW_WIDTH), jnp.float32, -3.0, 1.0)
    rw_w2 = nrm((N_EVEN, RW_DECAY_LORA, RW_WIDTH), 0.1 * RW_DECAY_LORA ** -0.5)
    rw_a0 = nrm((N_EVEN, RW_WIDTH), 0.1)
    rw_a2 = nrm((N_EVEN, RW_AAA_LORA, RW_WIDTH), 0.5 * RW_AAA_LORA ** -0.5)
    rw_g2 = nrm((N_EVEN, RW_GATE_LORA, RW_WIDTH), RW_GATE_LORA ** -0.5)
    rw_k_k = 0.85 + nrm((N_EVEN, RW_WIDTH), 0.02)
    rw_k_a = 1.0 + nrm((N_EVEN, RW_WIDTH), 0.02)
    rw_r_k = nrm((N_EVEN, RW_HEADS, RW_HEAD_DIM), 0.1)
    rw_gn_g = 1.0 + nrm((N_EVEN, RW_WIDTH), 0.02)
    rw_gn_b = nrm((N_EVEN, RW_WIDTH), 0.02)
    even_w_out = nrm((N_EVEN, D, D), sd * DN_BETA)
    odd_w_in = jnp.concatenate([
        nrm((N_ODD, D, 2 * GLA_KEY), sd),
        nrm((N_ODD, D, GLA_VAL), sd * DN_BETA),
        nrm((N_ODD, D, GLA_GATE_RANK), sd),
        nrm((N_ODD, D, GLA_VAL), sd),
    ], axis=-1)
    gla_gate_w2 = nrm((N_ODD, GLA_GATE_RANK, GLA_KEY), GLA_GATE_RANK ** -0.5)
    gla_gate_b = nrm((N_ODD, GLA_KEY), 0.1)
    gla_r_b = nrm((N_ODD, GLA_VAL), 0.02)
    gla_gn_g = 1.0 + nrm((N_ODD, GLA_VAL), 0.02)
    gla_gn_b = nrm((N_ODD, GLA_VAL), 0.02)
    odd_w_out = nrm((N_ODD, GLA_VAL, D), GLA_VAL ** -0.5 * DN_BETA)
    ln_mix_g = 1.0 + nrm((DEPTH, D), 0.02)
    ln_mix_b = nrm((DEPTH, D), 0.02)
    ln_ffn_g = 1.0 + nrm((DEPTH, D), 0.02)
    ln_ffn_b = nrm((DEPTH, D), 0.02)
    ffn_w_up = nrm((DEPTH, D, 2 * D_FF), sd * DN_BETA)
    ffn_conv_w = nrm((DEPTH, CONV_W, D_FF), CONV_W ** -0.5)
    ffn_conv_b = nrm((DEPTH, D_FF), 0.02)
    ffn_w_down = nrm((DEPTH, D_FF, D), D_FF ** -0.5 * DN_BETA)
    return {"x": x, "even_w_in": even_w_in, "even_shift_mu": even_shift_mu,
            "rw_w0": rw_w0, "rw_w2": rw_w2, "rw_a0": rw_a0, "rw_a2": rw_a2,
            "rw_g2": rw_g2, "rw_k_k": rw_k_k, "rw_k_a": rw_k_a, "rw_r_k": rw_r_k,
            "rw_gn_g": rw_gn_g, "rw_gn_b": rw_gn_b, "even_w_out": even_w_out,
            "odd_w_in": odd_w_in, "gla_gate_w2": gla_gate_w2, "gla_gate_b": gla_gate_b,
            "gla_r_b": gla_r_b, "gla_gn_g": gla_gn_g, "gla_gn_b": gla_gn_b,
            "odd_w_out": odd_w_out, "ln_mix_g": ln_mix_g, "ln_mix_b": ln_mix_b,
            "ln_ffn_g": ln_ffn_g, "ln_ffn_b": ln_ffn_b, "ffn_w_up": ffn_w_up,
            "ffn_conv_w": ffn_conv_w, "ffn_conv_b": ffn_conv_b, "ffn_w_down": ffn_w_down}


def reference(x, even_w_in, even_shift_mu, rw_w0, rw_w2, rw_a0, rw_a2, rw_g2, rw_k_k,
              rw_k_a, rw_r_k, rw_gn_g, rw_gn_b, even_w_out, odd_w_in, gla_gate_w2,
              gla_gate_b, gla_r_b, gla_gn_g, gla_gn_b, odd_w_out, ln_mix_g, ln_mix_b,
              ln_ffn_g, ln_ffn_b, ffn_w_up, ffn_conv_w, ffn_conv_b, ffn_w_down):
    h = x
    for layer in range(DEPTH):
        i = layer // 2
        if layer % 2 == 0:
            m = even_mixer(h, even_w_in[i], even_shift_mu[i], rw_w0[i], rw_w2[i], rw_a0[i],
                           rw_a2[i], rw_g2[i], rw_k_k[i], rw_k_a[i], rw_r_k[i],
                           rw_gn_g[i], rw_gn_b[i], even_w_out[i])
        else:
            m = odd_mixer(h, odd_w_in[i], gla_gate_w2[i], gla_gate_b[i], gla_r_b[i],
                          gla_gn_g[i], gla_gn_b[i], odd_w_out[i])
        h = layer_norm(DN_ALPHA * h + m.astype(h.dtype), ln_mix_g[layer], ln_mix_b[layer])
        f = conv_ffn(h, ffn_w_up[layer], ffn_conv_w[layer], ffn_conv_b[layer], ffn_w_down[layer])
        h = layer_norm(DN_ALPHA * h + f.astype(h.dtype), ln_ffn_g[layer], ln_ffn_b[layer])
    return h
```

```python
import contextlib
import numpy as np
import concourse.bass as bass
import concourse.mybir as mybir
from concourse.bass_utils import run_bass_kernel_spmd

F32 = mybir.dt.float32
BF16 = mybir.dt.bfloat16
AF = mybir.ActivationFunctionType
ALU = mybir.AluOpType
AX = mybir.AxisListType

ENGS = ("pe", "act", "dve", "pool", "sp")
SAME_ENG_SYNC = True
NSLOT = 8
SKEW = True


class Res:
    __slots__ = ("name", "lw", "rd")

    def __init__(self, name=""):
        self.name = name
        self.lw = None
        self.rd = []


class Op:
    __slots__ = ("eng", "fn", "reads", "writes", "dma", "deps", "need", "sig", "waits", "idx")

    def __init__(self, eng, fn, reads, writes, dma):
        self.eng = eng
        self.fn = fn
        self.reads = reads
        self.writes = writes
        self.dma = dma
        self.deps = []
        self.need = False
        self.sig = None
        self.waits = []


class _Rec:
    def __init__(self):
        self.call = None

    def __getattr__(self, name):
        def f(*a, **k):
            assert self.call is None
            self.call = (name, a, k)
            return self
        return f

    def replay(self, e):
        name, a, k = self.call
        return getattr(e, name)(*a, **k)


class Prog:
    def __init__(self, nc, stack):
        self.nc = nc
        self.ops = []
        self.sems = {e: stack.enter_context(nc.semaphore("s_" + e)) for e in ENGS}
        self.dsems = {e: [stack.enter_context(nc.semaphore("d_%s%d" % (e, i))) for i in range(NSLOT)]
                      for e in ("sp", "act", "pool")}
        self.sigcount = {e: 0 for e in ENGS}
        self.ndma = {e: 0 for e in ("sp", "act", "pool")}
        self.waited = {e: {} for e in ENGS}
        self.live = set()
        self.nops = 0

    def add(self, eng, fn, reads=(), writes=(), dma=False):
        if fn is not None:
            rec = _Rec()
            fn(rec)
            fn = rec.replay
        op = Op(eng, fn, tuple(reads), tuple(writes), dma)
        deps = []
        for r in op.reads:
            if r.lw is not None:
                deps.append(r.lw)
        for w in op.writes:
            if w.lw is not None:
                deps.append(w.lw)
            deps.extend(w.rd)
        seen = set()
        for d in deps:
            if id(d) in seen or d is op:
                continue
            seen.add(id(d))
            if (not d.dma) and d.eng == eng and not dma:
                if eng == "pe" or not SAME_ENG_SYNC:
                    continue
            op.deps.append(d)
            d.need = True
        for r in op.reads:
            r.rd.append(op)
            self.live.add(r)
        for w in op.writes:
            w.lw = op
            w.rd = []
            self.live.add(w)
        self.ops.append(op)
        return op

    def flush(self):
        nc = self.nc
        ops = self.ops
        self.ops = []
        if not ops:
            return
        for r in self.live:
            if r.lw is not None:
                r.lw.need = True
            for o in r.rd:
                o.need = True
        for op in ops:
            eng = op.eng
            waits = []
            if op.dma:
                i = self.ndma[eng]
                self.ndma[eng] = i + 1
                sem = self.dsems[eng][i % NSLOT]
                val = 16 * (i // NSLOT + 1)
                if val > 16:
                    if self.waited[eng].get(sem, 0) < val - 16:
                        waits.append((sem, val - 16))
                        self.waited[eng][sem] = val - 16
                op.sig = (sem, val)
            elif op.need and op.fn is not None:
                self.sigcount[eng] += 1
                op.sig = (self.sems[eng], self.sigcount[eng])
            for d in op.deps:
                if d.sig is None:
                    continue
                sem, val = d.sig
                if self.waited[eng].get(sem, 0) >= val:
                    continue
                self.waited[eng][sem] = val
                waits.append((sem, val))
            op.waits = waits
        per = {e: [o for o in ops if o.eng == e] for e in ENGS}
        self.nops += len(ops)

        def emit(e, lst):
            for op in lst:
                for sem, val in op.waits:
                    e.wait_ge(sem, val)
                if op.fn is None:
                    continue
                ins = op.fn(e)
                if op.sig is not None:
                    ins.then_inc(op.sig[0], 16 if op.dma else 1)

        with nc.Block() as block:
            @block.tensor
            def _(e):
                emit(e, per["pe"])

            @block.scalar
            def _(e):
                emit(e, per["act"])

            @block.vector
            def _(e):
                emit(e, per["dve"])

            @block.gpsimd
            def _(e):
                emit(e, per["pool"])

            @block.sync
            def _(e):
                emit(e, per["sp"])
        for op in ops:
            op.fn = None

    def dma(self, out, in_, reads=(), writes=(), q="sp"):
        return self.add(q, lambda e: e.dma_start(out=out, in_=in_), reads, writes, dma=True)

    def mm(self, out, lhsT, rhs, start=True, stop=True, reads=(), writes=()):
        return self.add("pe", lambda e: e.matmul(out, lhsT, rhs, start=start, stop=stop), reads, writes)

    def tr(self, out, in_, ident, reads=(), writes=()):
        return self.add("pe", lambda e: e.transpose(out, in_, ident), reads, writes)

    def act(self, out, in_, func, reads=(), writes=(), **kw):
        return self.add("act", lambda e: e.activation(out, in_, func, **kw), reads, writes)

    def wait_all(self, eng, ress):
        return self.add(eng, None, reads=ress)


S = 2048
D = 2048
DFF = 5632
ALPHA = 4.0 ** 0.25
LN_EPS = 1e-5
EXPM05 = float(np.exp(-0.5))


class Ctx:
    pass


DBG_RW = ["r_", "k_", "v_", "lw", "cl", "a_", "g_", "kk", "kt", "bh", "bonus", "RG", "AG", "BI", "KI", "Bend", "Kend", "yT"]
DBG_RW2 = ["Vt0", "BeT0", "KeT0", "M0", "N0", "Aak0", "Arb0", "Ark0", "TT0", "Vt1", "TT1", "P", "Xs", "Us"]


def build_program(phases=None, kinds=None):
    kinds = kinds or {}
    nc = bass.Bass("TRN2", target_bir_lowering=False)
    g = Ctx()
    g.nc = nc

    def dram(name, shape, dt, kind="Internal"):
        return nc.dram_tensor(name, list(shape), dt, kind=kinds.get(name, kind)).ap()

    I = "ExternalInput"
    A = {}
    A["x"] = dram("x", [S, D], F32, I)
    A["even_w_in"] = dram("even_w_in", [D, 6432], F32, I)
    A["even_shift_mu"] = dram("even_shift_mu", [3360], F32, I)
    for n in ("rw_w0", "rw_a0", "rw_k_k", "rw_k_a", "rw_r_k", "rw_gn_g", "rw_gn_b", "gla_gate_b"):
        A[n] = dram(n, [1024], F32, I)
    A["rw_w2"] = dram("rw_w2", [64, 1024], F32, I)
    A["rw_a2"] = dram("rw_a2", [64, 1024], F32, I)
    A["rw_g2"] = dram("rw_g2", [160, 1024], F32, I)
    A["even_w_out"] = dram("even_w_out", [D, D], F32, I)
    A["odd_w_in"] = dram("odd_w_in", [D, 6160], F32, I)
    A["gla_gate_w2"] = dram("gla_gate_w2", [16, 1024], F32, I)
    for n in ("gla_r_b", "gla_gn_g", "gla_gn_b"):
        A[n] = dram(n, [2048], F32, I)
    A["odd_w_out"] = dram("odd_w_out", [D, D], F32, I)
    for n in ("ln_mix_g", "ln_mix_b", "ln_ffn_g", "ln_ffn_b"):
        A[n] = dram(n, [2, D], F32, I)
    A["ffn_w_up"] = dram("ffn_w_up", [2, D, 2 * DFF], F32, I)
    A["ffn_conv_w"] = dram("ffn_conv_w", [2, 3, DFF], F32, I)
    A["ffn_conv_b"] = dram("ffn_conv_b", [2, DFF], F32, I)
    A["ffn_w_down"] = dram("ffn_w_down", [2, DFF, D], F32, I)
    A["y"] = dram("y", [S, D], F32, "ExternalOutput")
    A["qT_s"] = dram("qT_s", [1024, S], BF16)
    A["kT_s"] = dram("kT_s", [1024, S], BF16)
    A["v_s"] = dram("v_s", [S, 1024], BF16)
    A["rwT_s"] = dram("rwT_s", [3360, S], F32)
    A["oT_s"] = dram("oT_s", [D, S], BF16)
    A["h_a"] = dram("h_a", [S, D], F32)
    A["h_b"] = dram("h_b", [S, D], F32)
    A["hffT_s"] = dram("hffT_s", [4, 128, 44, 512], BF16)
    A["qk1_s"] = dram("qk1_s", [2048, S], F32)
    A["v1_s"] = dram("v1_s", [S, 2048], BF16)
    A["dg1_s"] = dram("dg1_s", [16, S], F32)
    A["r1_s"] = dram("r1_s", [S, 2048], F32)
    A["wdn_b"] = dram("wdn_b", [2, 4, 128, 44, 512], BF16)
    if "dbg_rw" in kinds:
        A["dbg_rw"] = dram("dbg_rw", [len(DBG_RW), 64, 8 * 128], F32)
        A["dbg_rw2"] = dram("dbg_rw2", [len(DBG_RW2), 64, 8 * 64], F32)
    g.A = A
    g.R = {n: Res(n) for n in A}

    with contextlib.ExitStack() as st:
        P = Prog(nc, st)
        g.P = P
        g.ident_f = st.enter_context(nc.sbuf_tensor("ident_f", [128, 128], F32))
        g.ident_b = st.enter_context(nc.sbuf_tensor("ident_b", [128, 128], BF16))
        g.r_ident = Res("ident")
        P.add("pool", lambda e: e.memset(g.ident_f[:], 1.0), writes=[g.r_ident])
        P.add("pool", lambda e: e.affine_select(g.ident_f[:], g.ident_f[:], [[-1, 128]], ALU.is_equal, 0.0,
                                                base=0, channel_multiplier=1), reads=[g.r_ident], writes=[g.r_ident])
        P.add("pool", lambda e: e.tensor_copy(g.ident_b[:], g.ident_f[:]), reads=[g.r_ident], writes=[g.r_ident])
        P.flush()
        allp = ["l0_inproj", "sb_attn", "rwkv", "l0_mixout", "l0_ffn", "l1_inproj", "gla", "l1_mixout", "l1_ffn"]
        ph = allp if phases is None else phases
        if "l0_inproj" in ph:
            phase_l0_inproj(g)
        if "sb_attn" in ph:
            phase_sb_attn(g)
        if "rwkv" in ph:
            phase_rwkv(g)
        if "l0_mixout" in ph:
            phase_proj_ln(g, "oT_s", 16, A["even_w_out"], "x", A["ln_mix_g"][0], A["ln_mix_b"][0], "h_a")
        if "l0_ffn" in ph:
            phase_ffn_up(g, 0, "h_a")
            phase_proj_ln(g, "hffT_s", 44, A["ffn_w_down"][0], "h_a", A["ln_ffn_g"][0], A["ln_ffn_b"][0], "h_b", wb=(A["wdn_b"][0], g.R["wdn_b"]))
        if "l1_inproj" in ph:
            phase_l1_inproj(g)
        if "gla" in ph:
            phase_gla(g)
        if "l1_mixout" in ph:
            phase_proj_ln(g, "oT_s", 16, A["odd_w_out"], "h_b", A["ln_mix_g"][1], A["ln_mix_b"][1], "h_a")
        if "l1_ffn" in ph:
            phase_ffn_up(g, 1, "h_a")
            phase_proj_ln(g, "hffT_s", 44, A["ffn_w_down"][1], "h_a", A["ln_ffn_g"][1], A["ln_ffn_b"][1], "y", wb=(A["wdn_b"][1], g.R["wdn_b"]))
        P.wait_all("sp", list(g.R.values()))
        P.flush()
    return nc


_UID = [0]


class Phase:
    def __init__(self, g):
        self.g = g
        self.st = contextlib.ExitStack()
        self.n = 0

    def __enter__(self):
        self.st.__enter__()
        return self

    def __exit__(self, *a):
        self.g.P.flush()
        return self.st.__exit__(*a)

    def sb(self, shape, dt, name=None):
        _UID[0] += 1
        t = self.st.enter_context(self.g.nc.sbuf_tensor("%s_%d" % (name or "t", _UID[0]), list(shape), dt))
        return t

    def ps(self, shape, dt, name=None):
        _UID[0] += 1
        t = self.st.enter_context(self.g.nc.psum_tensor("%s_%d" % (name or "p", _UID[0]), list(shape), dt))
        return t


def make_xT(g, ph, src_name, xT, xT_res):
    P, A, R = g.P, g.A, g.R
    src = A[src_name]
    xin = [ph.sb([128, D], BF16, "xin") for _ in range(2)]
    r_xin = [Res("xin0"), Res("xin1")]
    ptr = [ph.ps([128, 8, 128], BF16, "ptr") for _ in range(2)]
    r_ptr = [Res("ptr0"), Res("ptr1")]
    k = 0
    for tt in range(16):
        b = tt % 2
        P.dma(xin[b][:], src[tt * 128:(tt + 1) * 128, :], reads=[R[src_name]], writes=[r_xin[b]], q="pool")
        for gi in range(2):
            pb = k % 2
            k += 1
            for j in range(8):
                kc = gi * 8 + j
                P.tr(ptr[pb][:, j, :], xin[b][:, kc * 128:(kc + 1) * 128], g.ident_b[:],
                     reads=[r_xin[b], g.r_ident], writes=[r_ptr[pb]])
            dst = xT[:, gi * 8:(gi + 1) * 8, tt * 128:(tt + 1) * 128]
            if pb == 0:
                P.add("dve", lambda e, dst=dst, s=ptr[pb]: e.tensor_copy(dst, s[:]), reads=[r_ptr[pb]], writes=[xT_res[tt // 4]])
            else:
                P.add("act", lambda e, dst=dst, s=ptr[pb]: e.copy(dst, s[:]), reads=[r_ptr[pb]], writes=[xT_res[tt // 4]])


def load_vec_fm(g, ph, vec, n, dst, dst_res, pt, pt_res, parts=128):
    P = g.P
    nfull = n // parts
    rem = n - nfull * parts
    nch = nfull + (1 if rem else 0)
    tmp = ph.sb([128, parts], F32, "vtmp")
    r_tmp = Res("vtmp")
    P.add("pool", lambda e: e.memset(tmp[:], 0.0), writes=[r_tmp])
    if nfull:
        P.dma(tmp[0:nfull, :], vec[0:nfull * parts].rearrange("(c p) -> c p", p=parts), writes=[r_tmp])
    if rem:
        P.dma(tmp[nfull:nfull + 1, 0:rem], vec[nfull * parts:n].rearrange("(c p) -> c p", c=1), writes=[r_tmp])
    P.tr(pt[0:parts, 0:nch], tmp[0:nch, 0:parts], g.ident_f[0:nch, 0:nch], reads=[r_tmp, g.r_ident], writes=[pt_res])
    P.add("dve", lambda e: e.tensor_copy(dst, pt[0:parts, 0:nch]), reads=[pt_res], writes=[dst_res])
    return nch


def phase_l0_inproj(g):
    P, A, R, nc = g.P, g.A, g.R, g.nc
    with Phase(g) as ph:
        xT = ph.sb([128, 16, S], BF16, "xT")
        xT_res = [Res("xT%d" % i) for i in range(4)]
        with Phase(g) as ph2:
            make_xT(g, ph2, "x", xT, xT_res)
        w = [ph.sb([128, 16, 512], BF16, "w") for _ in range(2)]
        r_w = [Res("w0"), Res("w1")]
        pacc = [ph.ps([128, 512], F32, "pacc") for _ in range(4)]
        r_pacc = [Res("pacc%d" % i) for i in range(4)]
        mu = ph.sb([128, 27], F32, "mu")
        omu = ph.sb([128, 27], F32, "omu")
        r_mu = Res("mu")
        ptv = ph.ps([128, 128], F32, "ptv")
        r_ptv = Res("ptv")
        load_vec_fm(g, ph, A["even_shift_mu"], 3360, mu[:, 0:27], r_mu, ptv, r_ptv)
        P.add("dve", lambda e: e.tensor_scalar(omu[:], mu[:], -1.0, 1.0, ALU.mult, ALU.add), reads=[r_mu], writes=[r_mu])
        stage_b = [ph.sb([128, S], BF16, "stb") for _ in range(2)]
        r_stb = [Res("stb0"), Res("stb1")]
        stage_v = [ph.sb([128, 512], BF16, "stv") for _ in range(2)]
        r_stv = [Res("stv0"), Res("stv1")]
        ubuf = [ph.sb([128, S + 1], F32, "ubuf") for _ in range(2)]
        r_ub = [Res("ub0"), Res("ub1")]
        shf = [ph.sb([128, S], F32, "shf") for _ in range(2)]
        r_shf = [Res("shf0"), Res("shf1")]
        for b in range(2):
            P.add("pool", lambda e, b=b: e.memset(ubuf[b][:, 0:1], 0.0), writes=[r_ub[b]])
        W = A["even_w_in"]
        ngroups = (6432 + 511) // 512
        cnt = {"pacc": 0, "chunk": 0, "v": 0}

        def load_w(gi):
            c0 = gi * 512
            gw = min(512, 6432 - c0)
            b = gi % 2
            P.dma(w[b][:, :, 0:gw], W[:, c0:c0 + gw].rearrange("(kc p) c -> p kc c", p=128), writes=[r_w[b]], q="pool")

        load_w(0)
        for gi in range(ngroups):
            c0 = gi * 512
            gw = min(512, 6432 - c0)
            b = gi % 2
            if gi + 1 < ngroups:
                load_w(gi + 1)
            if 2048 <= c0 < 3072:
                for tt in range(16):
                    pi = cnt["pacc"] % 4
                    cnt["pacc"] += 1
                    for kc in range(16):
                        P.mm(pacc[pi][:, :], xT[:, kc, tt * 128:(tt + 1) * 128], w[b][:, kc, 0:512], start=(kc == 0), stop=(kc == 15),
                             reads=[xT_res[tt // 4], r_w[b]], writes=[r_pacc[pi]])
                    sv = cnt["v"] % 2
                    cnt["v"] += 1
                    P.add("act", lambda e, sv=sv, pi=pi: e.copy(stage_v[sv][:], pacc[pi][:]), reads=[r_pacc[pi]], writes=[r_stv[sv]])
                    P.dma(A["v_s"][tt * 128:(tt + 1) * 128, c0 - 2048:c0 - 2048 + 512], stage_v[sv][:], reads=[r_stv[sv]], writes=[R["v_s"]])
                continue
            nchunk = (gw + 127) // 128
            for j in range(nchunk):
                cw = min(128, gw - j * 128)
                col = c0 + j * 128
                ci = cnt["chunk"] % 2
                cnt["chunk"] += 1
                for qt in range(4):
                    pi = cnt["pacc"] % 4
                    cnt["pacc"] += 1
                    for kc in range(16):
                        P.mm(pacc[pi][0:cw, :], w[b][:, kc, j * 128:j * 128 + cw], xT[:, kc, qt * 512:(qt + 1) * 512], start=(kc == 0), stop=(kc == 15),
                             reads=[xT_res[qt], r_w[b]], writes=[r_pacc[pi]])
                    if col < 2048:
                        sc = (128.0 ** -0.5) if col < 1024 else 1.0
                        P.add("act", lambda e, ci=ci, pi=pi, qt=qt, sc=sc: e.activation(stage_b[ci][:, qt * 512:(qt + 1) * 512], pacc[pi][:], AF.Copy, scale=sc),
                              reads=[r_pacc[pi]], writes=[r_stb[ci]])
                    else:
                        P.add("act", lambda e, ci=ci, pi=pi, qt=qt, cw=cw: e.copy(ubuf[ci][0:cw, 1 + qt * 512:1 + (qt + 1) * 512], pacc[pi][0:cw, :]),
                              reads=[r_pacc[pi]], writes=[r_ub[ci]])
                if col < 2048:
                    dst = A["qT_s"] if col < 1024 else A["kT_s"]
                    dn = "qT_s" if col < 1024 else "kT_s"
                    r0 = col % 1024
                    P.dma(dst[r0:r0 + 128, :], stage_b[ci][:], reads=[r_stb[ci]], writes=[R[dn]])
                else:
                    rc = (col - 3072) // 128
                    P.add("dve", lambda e, ci=ci, rc=rc, cw=cw: e.tensor_scalar(shf[ci][0:cw, :], ubuf[ci][0:cw, 0:S], mu[0:cw, rc:rc + 1], None, ALU.mult),
                          reads=[r_ub[ci], r_mu], writes=[r_shf[ci]])
                    P.add("dve", lambda e, ci=ci, rc=rc, cw=cw: e.scalar_tensor_tensor(shf[ci][0:cw, :], ubuf[ci][0:cw, 1:S + 1], omu[0:cw, rc:rc + 1], shf[ci][0:cw, :], ALU.mult, ALU.add),
                          reads=[r_ub[ci], r_mu, r_shf[ci]], writes=[r_shf[ci]])
                    P.dma(A["rwT_s"][col - 3072:col - 3072 + cw, :], shf[ci][0:cw, :], reads=[r_shf[ci]], writes=[R["rwT_s"]])


def phase_sb_attn(g, NH=8):
    P, A, R, nc = g.P, g.A, g.R, g.nc
    NA = 3
    with Phase(g) as ph:
        qT = [ph.sb([128, S], BF16, "qT") for _ in range(2)]
        kT = [ph.sb([128, S], BF16, "kT") for _ in range(2)]
        vv = [ph.sb([128, 16, 128], BF16, "vv") for _ in range(2)]
        r_in = [Res("sbin0"), Res("sbin1")]
        tri = ph.sb([128, 128], F32, "tri")
        ones = ph.sb([128, 128], F32, "ones")
        r_c = Res("sbconst")
        P.add("pool", lambda e: e.memset(tri[:], 1.0), writes=[r_c])
        P.add("pool", lambda e: e.memset(ones[:], 1.0), writes=[r_c])
        P.add("pool", lambda e: e.affine_select(tri[:], tri[:], [[-1, 128]], ALU.is_ge, 0.0, base=0, channel_multiplier=1),
              reads=[r_c], writes=[r_c])
        pz = [ph.ps([128, 512], F32, "pz") for _ in range(NA)]
        r_pz = [Res("pz%d" % i) for i in range(NA)]
        zs = [ph.sb([128, 512], F32, "zs") for _ in range(NA)]
        r_zs = [Res("zs%d" % i) for i in range(NA)]
        sp = [ph.sb([128, 512], F32, "sp") for _ in range(NA)]
        r_sp = [Res("sp%d" % i) for i in range(NA)]
        pt = [ph.ps([128, 512], F32, "pt") for _ in range(2)]
        r_pt = [Res("pt0"), Res("pt1")]
        po = [ph.ps([128, 512], F32, "po") for _ in range(2)]
        r_po = [Res("po0"), Res("po1")]
        sacc = [ph.sb([128, 512], F32, "sacc") for _ in range(2)]
        r_sacc = [Res("sacc0"), Res("sacc1")]
        ee = [ph.sb([128, 512], F32, "ee") for _ in range(2)]
        r_ee = [Res("ee0"), Res("ee1")]
        ww = [ph.sb([128, 512], BF16, "ww") for _ in range(3)]
        r_ww = [Res("ww0"), Res("ww1"), Res("ww2")]
        osb = [ph.sb([128, S], BF16, "osb") for _ in range(2)]
        r_osb = [Res("osb0"), Res("osb1")]

        def load(h):
            b = h % 2
            P.dma(qT[b][:], A["qT_s"][h * 128:(h + 1) * 128, :], reads=[R["qT_s"]], writes=[r_in[b]])
            P.dma(kT[b][:], A["kT_s"][h * 128:(h + 1) * 128, :], reads=[R["kT_s"]], writes=[r_in[b]])
            P.dma(vv[b][:], A["v_s"][:, h * 128:(h + 1) * 128].rearrange("(t p) d -> p t d", p=128), reads=[R["v_s"]], writes=[r_in[b]])

        steps = []
        gq = 0
        for h in range(NH):
            for qt in range(4):
                kbs = list(range(4 * qt + 3, -1, -1))
                for ki, kb in enumerate(kbs):
                    steps.append((h, qt, ki, kb, len(kbs), gq))
                gq += 1

        def stageA(i):
            h, qt, ki, kb, nk, gq = steps[i]
            a = i % NA
            b = h % 2
            t0, s0 = qt * 512, kb * 128
            if i == 0:
                load(0)
            P.mm(pz[a][:], kT[b][:, s0:s0 + 128], qT[b][:, t0:t0 + 512], reads=[r_in[b]], writes=[r_pz[a]])
            P.act(sp[a][:], pz[a][:], AF.Exp, reads=[r_pz[a]], writes=[r_sp[a]])
            P.act(sp[a][:], sp[a][:], AF.Ln, reads=[r_sp[a]], writes=[r_sp[a]], bias=1.0)
            P.add("dve", lambda e: e.tensor_copy(zs[a][:], pz[a][:]), reads=[r_pz[a], r_sp[a]], writes=[r_zs[a]])
            if kb >= 4 * qt:
                P.add("pool", lambda e: e.affine_select(sp[a][:], sp[a][:], [[1, 512]], ALU.is_gt, 0.0, base=t0 - s0, channel_multiplier=-1),
                      reads=[r_sp[a]], writes=[r_sp[a]])

        def stageB(i):
            h, qt, ki, kb, nk, gq = steps[i]
            a = i % NA
            i2 = i % 2
            b = h % 2
            t0, s0 = qt * 512, kb * 128
            ob = gq % 2
            sa, r_sa = sacc[gq % 2], r_sacc[gq % 2]
            P.mm(pt[i2][:], tri[:], sp[a][:], start=True, stop=(ki == 0), reads=[r_sp[a], r_c], writes=[r_pt[i2]])
            if ki > 0:
                P.mm(pt[i2][:], ones[:], sa[:], start=False, stop=True, reads=[r_sa, r_c], writes=[r_pt[i2]])
            P.add("dve", lambda e: e.tensor_tensor(ee[i2][:], zs[a][:], pt[i2][:], ALU.subtract), reads=[r_zs[a], r_pt[i2]], writes=[r_ee[i2]])
            i3 = i % 3
            P.act(ww[i3][:], ee[i2][:], AF.Exp, reads=[r_ee[i2]], writes=[r_ww[i3]])
            if kb >= 4 * qt:
                P.add("pool", lambda e: e.affine_select(ww[i3][:], ww[i3][:], [[1, 512]], ALU.is_gt, 0.0, base=t0 - s0, channel_multiplier=-1),
                      reads=[r_ww[i3]], writes=[r_ww[i3]])
            if ki == 0:
                P.add("pool", lambda e: e.tensor_copy(sa[:], sp[a][:]), reads=[r_sp[a]], writes=[r_sa])
            elif ki + 1 < nk:
                P.add("pool", lambda e: e.tensor_tensor(sa[:], sa[:], sp[a][:], ALU.add), reads=[r_sp[a], r_sa], writes=[r_sa])

        def stageC(i):
            h, qt, ki, kb, nk, gq = steps[i]
            i3 = i % 3
            b = h % 2
            t0 = qt * 512
            ob = gq % 2
            P.mm(po[ob][:], vv[b][:, kb, :], ww[i3][:], start=(ki == 0), stop=(ki == nk - 1), reads=[r_ww[i3], r_in[b]], writes=[r_po[ob]])
            if ki == nk - 1:
                P.add("act", lambda e: e.copy(osb[b][:, t0:t0 + 512], po[ob][:]), reads=[r_po[ob]], writes=[r_osb[b]])
                if qt == 3:
                    P.dma(A["oT_s"][h * 128:(h + 1) * 128, :], osb[b][:], reads=[r_osb[b]], writes=[R["oT_s"]])

        for i in range(len(steps) + 2):
            if i < len(steps):
                stageA(i)
            if 1 <= i <= len(steps):
                stageB(i - 1)
            if i >= 2:
                stageC(i - 2)
            j = i - 1
            if 0 <= j < len(steps) and steps[j][1] == 0 and steps[j][2] == 0 and steps[j][0] + 1 < NH:
                load(steps[j][0] + 1)


def bcast_vec(g, ph, vec, n, name):
    t = ph.sb([128, n], F32, name)
    r = Res(name)
    g.P.dma(t[:], vec.partition_broadcast(128), writes=[r])
    return t, r


def layer_norm_tile(g, pre, r_pre, gam, bet, r_gb, out, r_out, stats, mv, rstd, r_tmp, negh, n=D, eps=LN_EPS):
    P = g.P
    nch = n // 512
    for c in range(nch):
        P.add("dve", lambda e, c=c: e.bn_stats(stats[:, c, :], pre[:, c * 512:(c + 1) * 512]), reads=[r_pre], writes=[r_tmp])
    P.add("dve", lambda e: e.bn_aggr(mv[:], stats[:, 0:nch, :]), reads=[r_tmp], writes=[r_tmp])
    P.add("pool", lambda e: e.tensor_scalar(rstd[:], mv[:, 1:2], eps, None, ALU.add), reads=[r_tmp], writes=[r_tmp])
    P.add("pool", lambda e: e.tensor_tensor(rstd[:], rstd[:], negh, ALU.pow), reads=[r_tmp, r_gb], writes=[r_tmp])
    P.add("dve", lambda e: e.tensor_scalar(out, pre, mv[:, 0:1], rstd[:, 0:1], ALU.subtract, ALU.mult), reads=[r_pre, r_tmp], writes=[r_out])
    P.add("dve", lambda e: e.tensor_tensor(out, out, gam, ALU.mult), reads=[r_gb, r_out], writes=[r_out])
    P.add("pool", lambda e: e.tensor_tensor(out, out, bet, ALU.add), reads=[r_gb, r_out], writes=[r_out])


def phase_proj_ln(g, a_name, KC, Wd, resid_name, ln_g, ln_b, out_name, wb=None):
    P, A, R, nc = g.P, g.A, g.R, g.nc
    CG = 512
    ncg = D // CG
    resident = KC <= 16
    with Phase(g) as ph:
        naT = 2 if resident else 1
        aT = [ph.sb([128, KC, 512], BF16, "aT") for _ in range(naT)]
        r_aT = [Res("aT%d" % i) for i in range(naT)]
        nw = ncg if resident else 2
        w = [ph.sb([128, KC, CG], BF16, "wp") for _ in range(nw)]
        r_w = [Res("wp%d" % i) for i in range(nw)]
        npre = 2 if resident else 1
        pre2 = [ph.sb([128, 4, D], F32, "pre") for _ in range(npre)]
        r_pre2 = [[Res("pre%d_%d" % (j, i)) for i in range(4)] for j in range(npre)]
        gam, r_g = bcast_vec(g, ph, ln_g, D, "gam")
        bet, r_b = bcast_vec(g, ph, ln_b, D, "bet")
        r_gb = Res("gb")
        negh = ph.sb([128, 1], F32, "negh")
        P.add("pool", lambda e: e.memset(negh[:], -0.5), reads=[r_g, r_b], writes=[r_gb])
        P.add("dve", None, reads=[r_g, r_b, r_gb])
        stats = [ph.sb([128, 4, 6], F32, "stats") for _ in range(4)]
        mv = [ph.sb([128, 2], F32, "mv") for _ in range(4)]
        rstd = [ph.sb([128, 1], F32, "rstd") for _ in range(4)]
        r_tmp = [Res("lntmp%d" % i) for i in range(4)]
        pacc = [ph.ps([128, 512], F32, "pp") for _ in range(4)]
        r_pacc = [Res("pp%d" % i) for i in range(4)]
        cnt = 0

        def load_w(i):
            cg = i % ncg
            b = cg if resident else i % 2
            if wb is None:
                P.dma(w[b][:], Wd[:, cg * CG:(cg + 1) * CG].rearrange("(kc p) c -> p kc c", p=128), writes=[r_w[b]], q="pool")
            else:
                P.dma(w[b][:], wb[0][cg], reads=[wb[1]], writes=[r_w[b]], q="act")

        def load_a(tg):
            if a_name == "hffT_s":
                P.dma(aT[tg % naT][:], A[a_name][tg], reads=[R[a_name]], writes=[r_aT[tg % naT]])
            else:
                P.dma(aT[tg % naT][:], A[a_name][:, tg * 512:(tg + 1) * 512].rearrange("(kc p) t -> p kc t", p=128), reads=[R[a_name]], writes=[r_aT[tg % naT]])

        total = 4 * ncg
        if resident:
            for i in range(ncg):
                load_w(i)
        else:
            load_w(0)
        load_a(0)
        for tg in range(4):
            ab = tg % naT
            pre = pre2[tg % npre]
            r_pre = r_pre2[tg % npre]
            if naT > 1 and tg + 1 < 4:
                load_a(tg + 1)
            for cg in range(ncg):
                i = tg * ncg + cg
                b = cg if resident else i % 2
                if not resident and i + 1 < total:
                    load_w(i + 1)
                for tt in range(4):
                    pi = cnt % 4
                    cnt += 1
                    for kc in range(KC):
                        P.mm(pacc[pi][:, 0:CG], aT[ab][:, kc, tt * 128:(tt + 1) * 128], w[b][:, kc, :], start=(kc == 0), stop=(kc == KC - 1),
                             reads=[r_aT[ab], r_w[b]], writes=[r_pacc[pi]])
                    P.add("act", lambda e: e.activation(pre[:, tt, cg * CG:(cg + 1) * CG], pacc[pi][:, 0:CG], AF.Copy, scale=1.0 / ALPHA),
                          reads=[r_pacc[pi]], writes=[r_pre[tt]])
            if naT == 1 and tg + 1 < 4:
                load_a(tg + 1)
            for tt in range(4):
                tok = tg * 4 + tt
                P.add("pool", lambda e: e.dma_start(out=pre[:, tt, :], in_=A[resid_name][tok * 128:(tok + 1) * 128, :], accum_op=ALU.add),
                      reads=[R[resid_name]], writes=[r_pre[tt]], dma=True)
            for tt in range(4):
                tok = tg * 4 + tt
                layer_norm_tile(g, pre[:, tt, :], r_pre[tt], gam[:], bet[:], r_gb, pre[:, tt, :], r_pre[tt], stats[tt], mv[tt], rstd[tt], r_tmp[tt], negh[:],
                                eps=LN_EPS / (ALPHA * ALPHA))
                P.dma(A[out_name][tok * 128:(tok + 1) * 128, :], pre[:, tt, :], reads=[r_pre[tt]], writes=[R[out_name]], q="pool")


def phase_ffn_up(g, layer, h_name):
    P, A, R, nc = g.P, g.A, g.R, g.nc
    Wu = A["ffn_w_up"][layer]
    with Phase(g) as ph:
        xT = ph.sb([128, 16, S], BF16, "hT")
        xT_res = [Res("hT%d" % i) for i in range(4)]
        with Phase(g) as ph2:
            make_xT(g, ph2, h_name, xT, xT_res)
        for cg in range(4):
            P.dma(A["wdn_b"][layer, cg], A["ffn_w_down"][layer][:, cg * 512:(cg + 1) * 512].rearrange("(kc p) c -> p kc c", p=128),
                  writes=[R["wdn_b"]], q="pool")
        w = [ph.sb([128, 16, 256], BF16, "wu") for _ in range(2)]
        r_w = [Res("wu0"), Res("wu1")]
        cw = ph.sb([128, 3, 44], F32, "cw")
        cb = ph.sb([128, 44], F32, "cb")
        r_cw = Res("cw")
        ptv = ph.ps([128, 128], F32, "ptv")
        r_ptv = Res("ptv")
        for j in range(3):
            load_vec_fm(g, ph, A["ffn_conv_w"][layer, j], DFF, cw[:, j, :], r_cw, ptv, r_ptv)
        load_vec_fm(g, ph, A["ffn_conv_b"][layer], DFF, cb[:, :], r_cw, ptv, r_ptv)
        pg = [ph.ps([128, 512], F32, "pg") for _ in range(2)]
        r_pg = [Res("pg0"), Res("pg1")]
        pu = [ph.ps([128, 512], F32, "pu") for _ in range(2)]
        r_pu = [Res("pu0"), Res("pu1")]
        gbuf = [ph.sb([128, S + 2], F32, "gbuf") for _ in range(2)]
        r_gb = [Res("gbuf0"), Res("gbuf1")]
        ubuf = [ph.sb([128, S], F32, "ubuf") for _ in range(2)]
        r_ub = [Res("fub0"), Res("fub1")]
        t1 = [ph.sb([128, S], F32, "t1") for _ in range(2)]
        r_t1 = [Res("t10"), Res("t11")]
        hf = [ph.sb([128, S], BF16, "hf") for _ in range(2)]
        r_hf = [Res("hf0"), Res("hf1")]
        for b in range(2):
            P.add("pool", lambda e, b=b: e.memset(gbuf[b][:, 0:2], 0.0), writes=[r_gb[b]])

        def load_w(c):
            b = c % 2
            P.dma(w[b][:, :, 0:128], Wu[:, c * 128:(c + 1) * 128].rearrange("(kc p) c -> p kc c", p=128), writes=[r_w[b]], q="pool")
            P.dma(w[b][:, :, 128:256], Wu[:, DFF + c * 128:DFF + (c + 1) * 128].rearrange("(kc p) c -> p kc c", p=128), writes=[r_w[b]], q="pool")

        load_w(0)
        k = 0
        for c in range(44):
            b = c % 2
            if c + 1 < 44:
                load_w(c + 1)
            for qt in range(4):
                pi = k % 2
                k += 1
                for kc in range(16):
                    P.mm(pg[pi][:], w[b][:, kc, 0:128], xT[:, kc, qt * 512:(qt + 1) * 512], start=(kc == 0), stop=(kc == 15),
                         reads=[xT_res[qt], r_w[b]], writes=[r_pg[pi]])
                for kc in range(16):
                    P.mm(pu[pi][:], w[b][:, kc, 128:256], xT[:, kc, qt * 512:(qt + 1) * 512], start=(kc == 0), stop=(kc == 15),
                         reads=[xT_res[qt], r_w[b]], writes=[r_pu[pi]])
                P.add("act", lambda e, b=b, pi=pi, qt=qt: e.copy(gbuf[b][:, 2 + qt * 512:2 + (qt + 1) * 512], pg[pi][:]), reads=[r_pg[pi]], writes=[r_gb[b]])
                P.add("dve", lambda e, b=b, pi=pi, qt=qt: e.tensor_copy(ubuf[b][:, qt * 512:(qt + 1) * 512], pu[pi][:]), reads=[r_pu[pi]], writes=[r_ub[b]])
            P.add("pool", lambda e, b=b, c=c: e.tensor_scalar(t1[b][:], gbuf[b][:, 0:S], cw[:, 0, c:c + 1], cb[:, c:c + 1], ALU.mult, ALU.add),
                  reads=[r_gb[b], r_cw], writes=[r_t1[b]])
            P.add("dve", lambda e, b=b, c=c: e.scalar_tensor_tensor(t1[b][:], gbuf[b][:, 1:S + 1], cw[:, 1, c:c + 1], t1[b][:], ALU.mult, ALU.add),
                  reads=[r_gb[b], r_cw, r_t1[b]], writes=[r_t1[b]])
            P.add("dve", lambda e, b=b, c=c: e.scalar_tensor_tensor(t1[b][:], gbuf[b][:, 2:S + 2], cw[:, 2, c:c + 1], t1[b][:], ALU.mult, ALU.add),
                  reads=[r_gb[b], r_cw, r_t1[b]], writes=[r_t1[b]])
            P.act(t1[b][:], t1[b][:], AF.Gelu, reads=[r_t1[b]], writes=[r_t1[b]])
            P.add("pool", lambda e, b=b: e.tensor_tensor(hf[b][:], t1[b][:], ubuf[b][:], ALU.mult), reads=[r_t1[b], r_ub[b]], writes=[r_hf[b]])
            for tg in range(4):
                P.dma(A["hffT_s"][tg, :, c, :], hf[b][:, tg * 512:(tg + 1) * 512], reads=[r_hf[b]], writes=[R["hffT_s"]])


def phase_l1_inproj(g):
    P, A, R, nc = g.P, g.A, g.R, g.nc
    W = A["odd_w_in"]
    groups = [(c, 512, "fm") for c in range(0, 2048, 512)] + [(c, 512, "v") for c in range(2048, 4096, 512)] \
        + [(4096, 16, "dg")] + [(c, 512, "r") for c in range(4112, 6160, 512)]
    with Phase(g) as ph:
        xT = ph.sb([128, 16, S], BF16, "xT1")
        xT_res = [Res("xT1%d" % i) for i in range(4)]
        with Phase(g) as ph2:
            make_xT(g, ph2, "h_b", xT, xT_res)
        w = [ph.sb([128, 16, 512], BF16, "w1") for _ in range(2)]
        r_w = [Res("w10"), Res("w11")]
        pacc = [ph.ps([128, 512], F32, "pacc") for _ in range(4)]
        r_pacc = [Res("pacc%d" % i) for i in range(4)]
        stf = [ph.sb([128, S], F32, "stf") for _ in range(2)]
        r_stf = [Res("stf0"), Res("stf1")]
        stv = [ph.sb([128, 512], BF16, "stv") for _ in range(2)]
        r_stv = [Res("stv0"), Res("stv1")]
        strr = [ph.sb([128, 512], F32, "str") for _ in range(2)]
        r_str = [Res("str0"), Res("str1")]
        cnt = {"pacc": 0, "chunk": 0, "v": 0}

        def load_w(gi):
            c0, gw, _ = groups[gi]
            b = gi % 2
            P.dma(w[b][:, :, 0:gw], W[:, c0:c0 + gw].rearrange("(kc p) c -> p kc c", p=128), writes=[r_w[b]], q="pool")

        load_w(0)
        for gi, (c0, gw, kind) in enumerate(groups):
            b = gi % 2
            if gi + 1 < len(groups):
                load_w(gi + 1)
            if kind in ("v", "r"):
                for tt in range(16):
                    pi = cnt["pacc"] % 4
                    cnt["pacc"] += 1
                    for kc in range(16):
                        P.mm(pacc[pi][:, :], xT[:, kc, tt * 128:(tt + 1) * 128], w[b][:, kc, 0:512], start=(kc == 0), stop=(kc == 15),
                             reads=[xT_res[tt // 4], r_w[b]], writes=[r_pacc[pi]])
                    sv = cnt["v"] % 2
                    cnt["v"] += 1
                    if kind == "v":
                        P.add("act", lambda e, sv=sv, pi=pi: e.copy(stv[sv][:], pacc[pi][:]), reads=[r_pacc[pi]], writes=[r_stv[sv]])
                        P.dma(A["v1_s"][tt * 128:(tt + 1) * 128, c0 - 2048:c0 - 2048 + 512], stv[sv][:], reads=[r_stv[sv]], writes=[R["v1_s"]])
                    else:
                        P.add("dve", lambda e, sv=sv, pi=pi: e.tensor_copy(strr[sv][:], pacc[pi][:]), reads=[r_pacc[pi]], writes=[r_str[sv]])
                        P.dma(A["r1_s"][tt * 128:(tt + 1) * 128, c0 - 4112:c0 - 4112 + 512], strr[sv][:], reads=[r_str[sv]], writes=[R["r1_s"]])
                continue
            nchunk = (gw + 127) // 128
            for j in range(nchunk):
                cw = min(128, gw - j * 128)
                col = c0 + j * 128
                ci = cnt["chunk"] % 2
                cnt["chunk"] += 1
                for qt in range(4):
                    pi = cnt["pacc"] % 4
                    cnt["pacc"] += 1
                    for kc in range(16):
                        P.mm(pacc[pi][0:cw, :], w[b][:, kc, j * 128:j * 128 + cw], xT[:, kc, qt * 512:(qt + 1) * 512], start=(kc == 0), stop=(kc == 15),
                             reads=[xT_res[qt], r_w[b]], writes=[r_pacc[pi]])
                    P.add("act", lambda e, ci=ci, pi=pi, qt=qt, cw=cw: e.copy(stf[ci][0:cw, qt * 512:(qt + 1) * 512], pacc[pi][0:cw, :]),
                          reads=[r_pacc[pi]], writes=[r_stf[ci]])
                if kind == "fm":
                    P.dma(A["qk1_s"][col:col + 128, :], stf[ci][:], reads=[r_stf[ci]], writes=[R["qk1_s"]])
                else:
                    P.dma(A["dg1_s"][0:16, :], stf[ci][0:16, :], reads=[r_stf[ci]], writes=[R["dg1_s"]])


def phase_gla(g):
    P, A, R, nc = g.P, g.A, g.R, g.nc
    SC = 1.0 / 16.0
    with Phase(g) as ph:
        gw2 = ph.sb([16, 1024], F32, "gw2")
        dgT = ph.sb([16, S], F32, "dgT")
        r_c = Res("glac")
        P.dma(gw2[:], A["gla_gate_w2"], writes=[r_c])
        P.dma(dgT[:], A["dg1_s"], reads=[R["dg1_s"]], writes=[r_c])
        ngb = ph.sb([128, 8], F32, "ngb")
        ptv = ph.ps([128, 128], F32, "ptv")
        r_ptv = Res("ptv")
        load_vec_fm(g, ph, A["gla_gate_b"], 1024, ngb[:, :], r_c, ptv, r_ptv)
        P.add("dve", lambda e: e.tensor_scalar(ngb[:], ngb[:], -1.0, None, ALU.mult), reads=[r_c], writes=[r_c])
        msk = ph.sb([128, 128], F32, "msk")
        P.add("pool", lambda e: e.memset(msk[:], 1.0), writes=[r_c])
        P.add("pool", lambda e: e.affine_select(msk[:], msk[:], [[1, 128]], ALU.is_ge, 0.0, base=0, channel_multiplier=-1),
              reads=[r_c], writes=[r_c])
        gnegh = ph.sb([128, 1], F32, "gnegh")
        P.add("pool", lambda e: e.memset(gnegh[:], -0.5), writes=[r_c])
        rst = ph.sb([128, 16, 128], F32, "rst")
        P.add("pool", lambda e: e.memset(rst[:], 1.0), writes=[r_c])
        P.add("pool", lambda e: e.memset(rst[:, :, 0:1], 0.0), reads=[r_c], writes=[r_c])
        rstf = rst[:].rearrange("p c l -> p (c l)")
        qf = ph.sb([128, S], F32, "qf")
        kf = ph.sb([128, S], F32, "kf")
        r_qk = Res("qkf")
        csp = ph.sb([128, 16, 128], F32, "csp")
        cspf = csp[:].rearrange("p c l -> p (c l)")
        r_csp = Res("csp")
        tmp = ph.sb([128, 16, 128], F32, "gtmp")
        tmpf = tmp[:].rearrange("p c l -> p (c l)")
        r_tmp = Res("gtmp")
        qd = ph.sb([128, 2, S], BF16, "qd")
        kinv = ph.sb([128, 2, S], BF16, "kinv")
        kendT = ph.sb([128, 2, S], BF16, "kendT")
        r_qd, r_kinv, r_kendT = Res("qd"), Res("kinv"), Res("kendT")
        kend = ph.sb([128, 16, 256], BF16, "kend")
        r_kend = Res("kend")
        dec = ph.sb([128, 2, 16], F32, "dec")
        r_dec = Res("dec")
        vv = ph.sb([128, 16, 512], BF16, "vv1")
        r_vv = Res("vv1")
        S_f = ph.sb([128, 2, 512], F32, "S_f")
        S_b = ph.sb([128, 2, 512], BF16, "S_b")
        r_Sf, r_Sb = Res("S_f"), Res("S_b")
        oT_h = ph.sb([128, 4, S], BF16, "oT_h")
        r_oTh = Res("oT_h")
        gng = ph.sb([128, 512], F32, "gng")
        gnb = ph.sb([128, 512], F32, "gnb")
        rbb = ph.sb([128, 512], F32, "rbb")
        r_hc = Res("headc")
        pA = [ph.ps([128, 512], F32, "pA") for _ in range(2)]
        r_pA = [Res("pA0"), Res("pA1")]
        patt = ph.ps([128, 128], F32, "patt")
        r_patt = Res("patt")
        po = [ph.ps([128, 512], F32, "po1") for _ in range(2)]
        r_po = [Res("po10"), Res("po11")]
        ptr = ph.ps([128, 8, 128], BF16, "ptr1")
        r_ptr = Res("ptr1")
        att_b = ph.sb([128, 128], BF16, "att_b")
        r_att = Res("att_b")
        NSL = 4
        o_sb = [ph.sb([128, 512], F32, "o_sb") for _ in range(NSL)]
        r_osb = [Res("o_sb%d" % i) for i in range(NSL)]
        rt = [ph.sb([128, 512], F32, "rt") for _ in range(NSL)]
        r_rt = [Res("rt%d" % i) for i in range(NSL)]
        og = [ph.sb([128, 512], BF16, "og") for _ in range(NSL)]
        r_og = [Res("og%d" % i) for i in range(NSL)]
        stats = [ph.sb([128, 1, 6], F32, "gstats") for _ in range(NSL)]
        mv = [ph.sb([128, 2], F32, "gmv") for _ in range(NSL)]
        rstd = [ph.sb([128, 1], F32, "grstd") for _ in range(NSL)]
        r_st = [Res("gst%d" % i) for i in range(NSL)]
        k = 0
        for h in range(4):
            P.dma(gng[:], A["gla_gn_g"][h * 512:(h + 1) * 512].partition_broadcast(128), writes=[r_hc])
            P.dma(gnb[:], A["gla_gn_b"][h * 512:(h + 1) * 512].partition_broadcast(128), writes=[r_hc])
            P.dma(rbb[:], A["gla_r_b"][h * 512:(h + 1) * 512].partition_broadcast(128), writes=[r_hc])
            P.dma(vv[:], A["v1_s"][:, h * 512:(h + 1) * 512].rearrange("(c p) v -> p c v", p=128), reads=[R["v1_s"]], writes=[r_vv])
            for dc in range(2):
                ch = h * 2 + dc
                P.dma(qf[:], A["qk1_s"][ch * 128:(ch + 1) * 128, :], reads=[R["qk1_s"]], writes=[r_qk])
                P.dma(kf[:], A["qk1_s"][1024 + ch * 128:1024 + (ch + 1) * 128, :], reads=[R["qk1_s"]], writes=[r_qk])
                for qt in range(4):
                    pi = k % 2
                    k += 1
                    P.mm(pA[pi][:], gw2[0:16, ch * 128:(ch + 1) * 128], dgT[0:16, qt * 512:(qt + 1) * 512], reads=[r_c], writes=[r_pA[pi]])
                    P.act(tmpf[:, qt * 512:(qt + 1) * 512], pA[pi][:], AF.Exp, reads=[r_pA[pi], r_c], writes=[r_tmp], scale=-1.0, bias=ngb[:, ch:ch + 1])
                P.act(tmpf, tmpf, AF.Ln, reads=[r_tmp], writes=[r_tmp], bias=1.0)
                P.add("dve", lambda e: e.tensor_tensor_scan(cspf, rstf, tmpf, 0.0, ALU.mult, ALU.add), reads=[r_tmp, r_c], writes=[r_csp])
                P.act(tmpf, cspf, AF.Exp, reads=[r_csp], writes=[r_tmp], scale=-SC)
                P.add("dve", lambda e, dc=dc: e.scalar_tensor_tensor(qd[:, dc, :], tmpf, SC, qf[:], ALU.mult, ALU.mult), reads=[r_tmp, r_qk], writes=[r_qd])
                P.act(dec[:, dc, :], csp[:, :, 127], AF.Exp, reads=[r_csp], writes=[r_dec], scale=-SC)
                P.act(tmpf, cspf, AF.Exp, reads=[r_csp], writes=[r_tmp], scale=SC)
                P.add("dve", lambda e, dc=dc: e.tensor_tensor(kinv[:, dc, :], tmpf, kf[:], ALU.mult), reads=[r_tmp, r_qk], writes=[r_kinv])
                P.add("dve", lambda e: e.tensor_tensor(tmp[:], csp[:], csp[:, :, 127:128].to_broadcast([128, 16, 128]), ALU.subtract),
                      reads=[r_csp], writes=[r_tmp])
                P.act(tmpf, tmpf, AF.Exp, reads=[r_tmp], writes=[r_tmp], scale=SC)
                P.add("dve", lambda e, dc=dc: e.tensor_tensor(kendT[:, dc, :], tmpf, kf[:], ALU.mult), reads=[r_tmp, r_qk], writes=[r_kendT])
                for half in range(2):
                    for j in range(8):
                        c = half * 8 + j
                        P.tr(ptr[:, j, :], kendT[:, dc, c * 128:(c + 1) * 128], g.ident_b[:], reads=[r_kendT, g.r_ident], writes=[r_ptr])
                    P.add("dve", lambda e, dc=dc, half=half: e.tensor_copy(kend[:, half * 8:(half + 1) * 8, dc * 128:(dc + 1) * 128], ptr[:]),
                          reads=[r_ptr], writes=[r_kend])
            kctr = [0]

            def gla_stage_a(c):
                cs = slice(c * 128, (c + 1) * 128)
                ob = c % 2
                for dc in range(2):
                    P.mm(patt[:], kinv[:, dc, cs], qd[:, dc, cs], start=(dc == 0), stop=(dc == 1), reads=[r_kinv, r_qd], writes=[r_patt])
                P.add("dve", lambda e: e.tensor_tensor(att_b[:], patt[:], msk[:], ALU.mult), reads=[r_patt, r_c], writes=[r_att])
                P.mm(po[ob][:], att_b[:], vv[:, c, :], start=True, stop=(c == 0), reads=[r_att, r_vv], writes=[r_po[ob]])
                if c > 0:
                    for dc in range(2):
                        P.mm(po[ob][:], qd[:, dc, cs], S_b[:, dc, :], start=False, stop=(dc == 1), reads=[r_qd, r_Sb], writes=[r_po[ob]])
                if c < 15:
                    for dc in range(2):
                        pi = kctr[0] % 2
                        kctr[0] += 1
                        P.mm(pA[pi][:], kend[:, c, dc * 128:(dc + 1) * 128], vv[:, c, :], reads=[r_kend, r_vv], writes=[r_pA[pi]])
                        if c == 0:
                            P.add("dve", lambda e: e.tensor_copy(S_b[:, dc, :], pA[pi][:]), reads=[r_pA[pi]], writes=[r_Sb])
                            P.add("dve", lambda e: e.tensor_copy(S_f[:, dc, :], pA[pi][:]), reads=[r_pA[pi]], writes=[r_Sf])
                        else:
                            P.add("dve", lambda e: e.scalar_tensor_tensor(S_b[:, dc, :], S_f[:, dc, :], dec[:, dc, c:c + 1], pA[pi][:], ALU.mult, ALU.add),
                                  reads=[r_pA[pi], r_Sf, r_dec], writes=[r_Sb])
                            P.add("dve", lambda e: e.scalar_tensor_tensor(S_f[:, dc, :], S_f[:, dc, :], dec[:, dc, c:c + 1], pA[pi][:], ALU.mult, ALU.add),
                                  reads=[r_pA[pi], r_Sf, r_dec], writes=[r_Sf])

            def gla_s0(c):
                cs = slice(c * 128, (c + 1) * 128)
                ob = c % 2
                sl = c % NSL
                P.add("dve", lambda e: e.tensor_copy(o_sb[sl][:], po[ob][:]), reads=[r_po[ob]], writes=[r_osb[sl]])
                P.dma(rt[sl][:], A["r1_s"][cs, h * 512:(h + 1) * 512], reads=[R["r1_s"]], writes=[r_rt[sl]])
                P.add("pool", lambda e: e.tensor_tensor(rt[sl][:], rt[sl][:], rbb[:], ALU.add), reads=[r_rt[sl], r_hc], writes=[r_rt[sl]])
                P.act(rt[sl][:], rt[sl][:], AF.Silu, reads=[r_rt[sl]], writes=[r_rt[sl]])
                P.add("dve", lambda e: e.bn_stats(stats[sl][:, 0, :], o_sb[sl][:]), reads=[r_osb[sl]], writes=[r_st[sl]])
                P.add("dve", lambda e: e.bn_aggr(mv[sl][:], stats[sl][:, 0:1, :]), reads=[r_st[sl]], writes=[r_st[sl]])

            def gla_s1(c):
                sl = c % NSL
                P.add("pool", lambda e: e.tensor_scalar(rstd[sl][:], mv[sl][:, 1:2], LN_EPS, None, ALU.add), reads=[r_st[sl]], writes=[r_st[sl]])
                P.add("pool", lambda e: e.tensor_tensor(rstd[sl][:], rstd[sl][:], gnegh[:], ALU.pow), reads=[r_st[sl], r_c], writes=[r_st[sl]])
                P.add("dve", lambda e: e.tensor_scalar(o_sb[sl][:], o_sb[sl][:], mv[sl][:, 0:1], rstd[sl][:, 0:1], ALU.subtract, ALU.mult),
                      reads=[r_st[sl], r_osb[sl]], writes=[r_osb[sl]])

            def gla_s2(c):
                sl = c % NSL
                P.add("pool", lambda e: e.tensor_tensor(o_sb[sl][:], o_sb[sl][:], gng[:], ALU.mult), reads=[r_hc, r_osb[sl]], writes=[r_osb[sl]])
                P.add("pool", lambda e: e.tensor_tensor(o_sb[sl][:], o_sb[sl][:], gnb[:], ALU.add), reads=[r_hc, r_osb[sl]], writes=[r_osb[sl]])
                P.add("dve", lambda e: e.tensor_tensor(og[sl][:], o_sb[sl][:], rt[sl][:], ALU.mult), reads=[r_osb[sl], r_rt[sl]], writes=[r_og[sl]])

            def gla_s3(c):
                cs = slice(c * 128, (c + 1) * 128)
                sl = c % NSL
                for vc in range(4):
                    P.tr(ptr[:, vc, :], og[sl][:, vc * 128:(vc + 1) * 128], g.ident_b[:], reads=[r_og[sl], g.r_ident], writes=[r_ptr])
                P.add("dve", lambda e: e.tensor_copy(oT_h[:, :, cs], ptr[:, 0:4, :]), reads=[r_ptr], writes=[r_oTh])

            for c in range(16 + 4):
                if c < 16:
                    gla_stage_a(c)
                for k_, fn_ in enumerate((gla_s0, gla_s1, gla_s2, gla_s3)):
                    cc_ = c - 1 - k_
                    if 0 <= cc_ < 16:
                        fn_(cc_)
            P.dma(A["oT_s"][h * 512:(h + 1) * 512, :].rearrange("(vc p) t -> p vc t", p=128), oT_h[:], reads=[r_oTh], writes=[R["oT_s"]])


def phase_rwkv(g):
    P, A, R, nc = g.P, g.A, g.R, g.nc
    H = 8
    HP = 4
    RW = A["rwT_s"]
    with Phase(g) as ph:
        T = {}

        def mk(name, shape, dt=F32):
            t = ph.sb(shape, dt, name)
            T[name] = (t, Res(name))
            return t

        def tl(name):
            return T[name][0]

        def rs(*names):
            return [T[n][1] for n in names]

        def op(eng, fn, reads, writes):
            P.add(eng, fn, rs(*reads), rs(*writes))

        def hsl(h):
            return slice((h % 2) * 64, (h % 2) * 64 + 64), h // 2

        mk("w2", [64, 1024]); mk("a2", [64, 1024]); mk("g2a", [128, 1024]); mk("g2b", [32, 1024])
        P.dma(tl("w2")[:], A["rw_w2"], writes=rs("w2"))
        P.dma(tl("a2")[:], A["rw_a2"], writes=rs("a2"))
        P.dma(tl("g2a")[:], A["rw_g2"][0:128, :], writes=rs("g2a"))
        P.dma(tl("g2b")[:], A["rw_g2"][128:160, :], writes=rs("g2b"))
        psA = [ph.ps([128, HP, 128], F32, "psA") for _ in range(2)]
        r_psA = [Res("psA0"), Res("psA1")]
        for n in ("rw_w0", "rw_a0", "rw_k_k", "rw_k_a", "rw_r_k", "rw_gn_g", "rw_gn_b"):
            mk(n, [128, 8])
            load_vec_fm(g, ph, A[n], 1024, tl(n)[:, :], T[n][1], psA[0][:, 0, :], r_psA[0], parts=128)
        mk("ones", [128, 128]); mk("onesm", [128, 128]); mk("mS", [128, 64]); mk("mI", [128, 64]); mk("mL", [128, 64])
        mk("identS", [128, 64])
        mk("rst", [128, HP, 2, 64])
        for nm, val in (("ones", 1.0), ("onesm", 1.0 / 64.0)):
            op("pool", lambda e: e.memset(tl(nm)[:], 0.0), [], [nm])
            for h2 in range(2):
                q = slice(h2 * 64, h2 * 64 + 64)
                op("pool", lambda e: e.memset(tl(nm)[q, q], val), [nm], [nm])
        for nm, cmp_, cm, st_ in (("mS", ALU.is_gt, -1, 1), ("mI", ALU.is_ge, -1, 1), ("mL", ALU.is_gt, 1, -1)):
            op("pool", lambda e: e.memset(tl(nm)[:], 1.0), [], [nm])
            for h2 in range(2):
                q = slice(h2 * 64, h2 * 64 + 64)
                op("pool", lambda e: e.affine_select(tl(nm)[q, :], tl(nm)[q, :], [[st_, 64]], cmp_, 0.0, base=0, channel_multiplier=cm), [nm], [nm])
        for h2 in range(2):
            q = slice(h2 * 64, h2 * 64 + 64)
            P.add("pool", lambda e: e.tensor_copy(tl("identS")[q, :], g.ident_f[q, q]), reads=[g.r_ident], writes=rs("identS"))
        op("pool", lambda e: e.memset(tl("rst")[:], 1.0), [], ["rst"])
        op("pool", lambda e: e.memset(tl("rst")[:, :, :, 0:1], 0.0), ["rst"], ["rst"])

        def bcm(name):
            return tl(name)[:].unsqueeze(1).to_broadcast([128, HP, 64])

        for n in ("r_", "k_", "v_", "lw", "cl", "a_", "g_", "ep", "en", "ex", "kk", "kt", "bh", "t1", "bonus",
                  "yT", "t2"):
            mk(n, [128, HP, 128])
        for n in ("RG", "AG", "BI", "KI", "Bend", "Kend", "vb16"):
            mk(n, [128, HP, 128], BF16)
        mk("dwT", [64, 128]); mk("daT", [64, 128]); mk("dgT", [128, 128]); mk("dgT2", [32, 128])
        mk("gL", [128, HP, 2])
        mk("ob", [128, HP, 128], BF16)
        for cc in range(2):
            for n in ("Vt", "BeT", "KeT", "M", "N", "Aak", "Arb", "Ark", "TT", "M2", "N2"):
                mk("%s%d" % (n, cc), [128, HP, 64], BF16)
        mk("P", [128, HP, 64]); mk("Xs", [128, HP, 64], BF16); mk("Us", [128, HP, 64], BF16); mk("Pt", [128, HP, 64])
        mk("Pb", [128, HP, 64], BF16)
        psB = [ph.ps([128, HP, 64], F32, "psB") for _ in range(4)]
        r_psB = [Res("psB%d" % i) for i in range(4)]
        psC = ph.ps([128, HP * 128], F32, "psC")
        r_psC = Res("psC")
        cnt = {"A": 0, "B": 0}

        def nextB():
            i = cnt["B"] % 4
            cnt["B"] += 1
            return psB[i], r_psB[i]

        def flat(name):
            return tl(name)[:].rearrange("p h t -> p (h t)")

        def ones_reduce(lhs_name, src_name):
            P.mm(psC[:, :], tl(lhs_name)[:], flat(src_name), reads=rs(lhs_name, src_name), writes=[r_psC])

        psC3 = psC[:].rearrange("p (h t) -> p h t", h=HP)

        for hg in range(2):
            prs = slice(hg * HP, (hg + 1) * HP)

            def vb(name, n=128):
                return tl(name)[:, prs].unsqueeze(2).to_broadcast([128, HP, n])

            op("pool", lambda e: e.memset(tl("P")[:], 0.0), [], ["P"])
            op("pool", lambda e: e.memset(tl("Pb")[:], 0.0), [], ["Pb"])
            for blk in range(16):
                t0 = blk * 128
                ts_ = slice(t0, t0 + 128)
                for nm, base in (("r_", 0), ("k_", 1024), ("v_", 2048)):
                    P.dma(tl(nm)[:], RW[base + hg * 512:base + (hg + 1) * 512, ts_].rearrange("(hp p) t -> p hp t", p=128),
                          reads=[R["rwT_s"]], writes=rs(nm))
                P.dma(tl("dwT")[:], RW[3072:3136, ts_], reads=[R["rwT_s"]], writes=rs("dwT"))
                P.dma(tl("daT")[:], RW[3136:3200, ts_], reads=[R["rwT_s"]], writes=rs("daT"))
                P.dma(tl("dgT")[:], RW[3200:3328, ts_], reads=[R["rwT_s"]], writes=rs("dgT"))
                P.dma(tl("dgT2")[:], RW[3328:3360, ts_], reads=[R["rwT_s"]], writes=rs("dgT2"))
                P.act(tl("dwT")[:], tl("dwT")[:], AF.Tanh, reads=rs("dwT"), writes=rs("dwT"))
                P.act(tl("dgT")[:], tl("dgT")[:], AF.Sigmoid, reads=rs("dgT"), writes=rs("dgT"))
                P.act(tl("dgT2")[:], tl("dgT2")[:], AF.Sigmoid, reads=rs("dgT2"), writes=rs("dgT2"))
                for (wn, src, dst, bn) in (("w2", "dwT", "lw", "rw_w0"), ("a2", "daT", "a_", "rw_a0"), ("g2", "dgT", "g_", None)):
                    ai = cnt["A"] % 2
                    cnt["A"] += 1
                    for hp in range(HP):
                        pr = hg * HP + hp
                        cols = slice(pr * 128, (pr + 1) * 128)
                        if wn == "g2":
                            P.mm(psA[ai][:, hp, :], tl("g2a")[:, cols], tl("dgT")[:], start=True, stop=False, reads=rs("g2a", "dgT"), writes=[r_psA[ai]])
                            P.mm(psA[ai][:, hp, :], tl("g2b")[:, cols], tl("dgT2")[:], start=False, stop=True, reads=rs("g2b", "dgT2"), writes=[r_psA[ai]])
                        else:
                            P.mm(psA[ai][:, hp, :], tl(wn)[:, cols], tl(src)[:], reads=rs(wn, src), writes=[r_psA[ai]])
                    if bn is None:
                        P.add("dve", lambda e: e.tensor_copy(tl(dst)[:], psA[ai][:]), reads=[r_psA[ai]], writes=rs(dst))
                    else:
                        for hp in range(HP):
                            pr = hg * HP + hp
                            P.act(tl(dst)[:, hp, :], psA[ai][:, hp, :], AF.Sigmoid, reads=[r_psA[ai]] + rs(bn), writes=rs(dst), bias=tl(bn)[:, pr:pr + 1])
                op("dve", lambda e: e.tensor_scalar(tl("lw")[:], tl("lw")[:], -EXPM05, None, ALU.mult), ["lw"], ["lw"])
                op("dve", lambda e: e.tensor_tensor_scan(flat("cl"), tl("rst")[:].rearrange("p h c l -> p (h c l)"), flat("lw"), 0.0, ALU.mult, ALU.add),
                   ["lw", "rst"], ["cl"])
                P.act(tl("ep")[:], tl("cl")[:], AF.Exp, reads=rs("cl"), writes=rs("ep"))
                P.act(tl("en")[:], tl("cl")[:], AF.Exp, reads=rs("cl"), writes=rs("en"), scale=-1.0)
                op("dve", lambda e: e.tensor_tensor(tl("ex")[:], tl("cl")[:], tl("lw")[:], ALU.subtract), ["cl", "lw"], ["ex"])
                P.act(tl("ex")[:], tl("ex")[:], AF.Exp, reads=rs("ex"), writes=rs("ex"))
                op("pool", lambda e: e.tensor_copy(tl("gL")[:], tl("ep")[:].rearrange("p h (c l) -> p h c l", c=2)[:, :, :, 63]), ["ep"], ["gL"])
                op("dve", lambda e: e.tensor_tensor(tl("kk")[:], tl("k_")[:], vb("rw_k_k"), ALU.mult), ["k_", "rw_k_k"], ["kk"])
                op("pool", lambda e: e.tensor_tensor(tl("t1")[:], tl("kk")[:], tl("kk")[:], ALU.mult), ["kk"], ["t1"])
                ones_reduce("ones", "t1")
                P.add("dve", lambda e: e.tensor_scalar_max(tl("t1")[:], psC3, 1e-24), reads=[r_psC], writes=rs("t1"))
                P.act(tl("t1")[:], tl("t1")[:], AF.Sqrt, reads=rs("t1"), writes=rs("t1"))
                op("dve", lambda e: e.reciprocal(tl("t1")[:], tl("t1")[:]), ["t1"], ["t1"])
                op("dve", lambda e: e.tensor_tensor(tl("kk")[:], tl("kk")[:], tl("t1")[:], ALU.mult), ["kk", "t1"], ["kk"])
                op("dve", lambda e: e.scalar_tensor_tensor(tl("t2")[:], tl("a_")[:], -1.0, vb("rw_k_a"), ALU.add, ALU.mult), ["a_", "rw_k_a"], ["t2"])
                op("dve", lambda e: e.scalar_tensor_tensor(tl("kt")[:], tl("t2")[:], 1.0, tl("k_")[:], ALU.add, ALU.mult), ["t2", "k_"], ["kt"])
                op("pool", lambda e: e.tensor_tensor(tl("bh")[:], tl("kk")[:], tl("a_")[:], ALU.mult), ["kk", "a_"], ["bh"])
                op("pool", lambda e: e.tensor_tensor(tl("t2")[:], tl("r_")[:], tl("kt")[:], ALU.mult), ["r_", "kt"], ["t2"])
                op("pool", lambda e: e.tensor_tensor(tl("t2")[:], tl("t2")[:], vb("rw_r_k"), ALU.mult), ["t2", "rw_r_k"], ["t2"])
                ones_reduce("ones", "t2")
                P.add("dve", lambda e: e.tensor_tensor(tl("bonus")[:], psC3, tl("v_")[:], ALU.mult), reads=[r_psC] + rs("v_"), writes=rs("bonus"))
                op("pool", lambda e: e.tensor_tensor(tl("RG")[:], tl("r_")[:], tl("ep")[:], ALU.mult), ["r_", "ep"], ["RG"])
                op("dve", lambda e: e.scalar_tensor_tensor(tl("AG")[:], tl("kk")[:], -1.0, tl("ex")[:], ALU.mult, ALU.mult), ["kk", "ex"], ["AG"])
                gLb = tl("gL")[:].unsqueeze(3).to_broadcast([128, HP, 2, 64])
                op("pool", lambda e: e.tensor_tensor(tl("BI")[:], tl("bh")[:], tl("en")[:], ALU.mult), ["bh", "en"], ["BI"])
                op("pool", lambda e: e.tensor_tensor(tl("KI")[:], tl("kt")[:], tl("en")[:], ALU.mult), ["kt", "en"], ["KI"])
                op("dve", lambda e: e.tensor_tensor(tl("Bend")[:].rearrange("p h (c l) -> p h c l", c=2), tl("BI")[:].rearrange("p h (c l) -> p h c l", c=2), gLb, ALU.mult),
                   ["BI", "gL"], ["Bend"])
                op("dve", lambda e: e.tensor_tensor(tl("Kend")[:].rearrange("p h (c l) -> p h c l", c=2), tl("KI")[:].rearrange("p h (c l) -> p h c l", c=2), gLb, ALU.mult),
                   ["KI", "gL"], ["Kend"])
                op("act", lambda e: e.copy(tl("vb16")[:], tl("v_")[:]), ["v_"], ["vb16"])
                for cc in range(2):
                    c_ = slice(cc * 64, (cc + 1) * 64)
                    sfx = str(cc)
                    for k3, (src, dst) in enumerate((("vb16", "Vt"), ("Bend", "BeT"), ("Kend", "KeT"))):
                        pb, r_pb = nextB()
                        for h in range(H):
                            q, hp = hsl(h)
                            P.mm(pb[q, hp, :], tl(src)[q, hp, c_], g.ident_b[q, q], reads=rs(src) + [g.r_ident], writes=[r_pb])
                        if k3 % 2 == 0:
                            P.add("act", lambda e: e.copy(tl(dst + sfx)[:], pb[:]), reads=[r_pb], writes=rs(dst + sfx))
                        else:
                            P.add("dve", lambda e: e.tensor_copy(tl(dst + sfx)[:], pb[:]), reads=[r_pb], writes=rs(dst + sfx))
                    for (lh, rh, dst, msk) in (("BI", "AG", "M", "mS"), ("KI", "AG", "Aak", "mS"), ("BI", "RG", "Arb", "mI"), ("KI", "RG", "Ark", "mI"),
                                              ("AG", "BI", "N", "mL")):
                        pb, r_pb = nextB()
                        for h in range(H):
                            q, hp = hsl(h)
                            P.mm(pb[q, hp, :], tl(lh)[q, hp, c_], tl(rh)[q, hp, c_], reads=rs(lh, rh), writes=[r_pb])
                        P.add("dve", lambda e: e.tensor_tensor(tl(dst + sfx)[:], pb[:], bcm(msk), ALU.mult), reads=[r_pb] + rs(msk), writes=rs(dst + sfx))
                    op("pool", lambda e: e.tensor_tensor(tl("TT" + sfx)[:], tl("M" + sfx)[:], bcm("identS"), ALU.add), ["M" + sfx, "identS"], ["TT" + sfx])
                    cur_m, cur_n = "M" + sfx, "N" + sfx
                    oth_m, oth_n = "M2" + sfx, "N2" + sfx
                    for lvl in range(5):
                        last = (lvl == 4)
                        pn, r_pn = nextB()
                        for h in range(H):
                            q, hp = hsl(h)
                            P.mm(pn[q, hp, :], tl(cur_m)[q, hp, :], tl(cur_n)[q, hp, :], reads=rs(cur_m, cur_n), writes=[r_pn])
                        if not last:
                            pm, r_pm = nextB()
                            for h in range(H):
                                q, hp = hsl(h)
                                P.mm(pm[q, hp, :], tl(cur_n)[q, hp, :], tl(cur_m)[q, hp, :], reads=rs(cur_m, cur_n), writes=[r_pm])
                        P.add("act", lambda e: e.copy(tl(oth_n)[:], pn[:]), reads=[r_pn], writes=rs(oth_n))
                        if not last:
                            P.add("dve", lambda e: e.tensor_copy(tl(oth_m)[:], pm[:]), reads=[r_pm], writes=rs(oth_m))
                        pt_, r_pt = nextB()
                        for h in range(H):
                            q, hp = hsl(h)
                            P.mm(pt_[q, hp, :], tl(oth_n)[q, hp, :], tl("TT" + sfx)[q, hp, :], reads=rs(oth_n, "TT" + sfx), writes=[r_pt])
                        P.add("dve", lambda e: e.tensor_tensor(tl("TT" + sfx)[:], tl("TT" + sfx)[:], pt_[:], ALU.add), reads=[r_pt] + rs("TT" + sfx), writes=rs("TT" + sfx))
                        cur_m, oth_m = oth_m, cur_m
                        cur_n, oth_n = oth_n, cur_n
                for cc in range(2):
                    c_ = slice(cc * 64, (cc + 1) * 64)
                    sfx = str(cc)
                    px, r_px = nextB()
                    for h in range(H):
                        q, hp = hsl(h)
                        P.mm(px[q, hp, :], tl("AG")[q, hp, c_], tl("Pb")[q, hp, :], start=True, stop=False, reads=rs("AG", "Pb"), writes=[r_px])
                        P.mm(px[q, hp, :], tl("Aak" + sfx)[q, hp, :], tl("Vt" + sfx)[q, hp, :], start=False, stop=True, reads=rs("Aak" + sfx, "Vt" + sfx), writes=[r_px])
                    P.add("act", lambda e: e.copy(tl("Xs")[:], px[:]), reads=[r_px], writes=rs("Xs"))
                    pu, r_pu = nextB()
                    for h in range(H):
                        q, hp = hsl(h)
                        P.mm(pu[q, hp, :], tl("TT" + sfx)[q, hp, :], tl("Xs")[q, hp, :], reads=rs("TT" + sfx, "Xs"), writes=[r_pu])
                    P.add("dve", lambda e: e.tensor_copy(tl("Us")[:], pu[:]), reads=[r_pu], writes=rs("Us"))
                    py, r_py = nextB()
                    for h in range(H):
                        q, hp = hsl(h)
                        P.mm(py[q, hp, :], tl("Pb")[q, hp, :], tl("RG")[q, hp, c_], start=True, stop=False, reads=rs("Pb", "RG"), writes=[r_py])
                        P.mm(py[q, hp, :], tl("Us")[q, hp, :], tl("Arb" + sfx)[q, hp, :], start=False, stop=False, reads=rs("Us", "Arb" + sfx), writes=[r_py])
                        P.mm(py[q, hp, :], tl("Vt" + sfx)[q, hp, :], tl("Ark" + sfx)[q, hp, :], start=False, stop=True, reads=rs("Vt" + sfx, "Ark" + sfx), writes=[r_py])
                    P.add("act", lambda e: e.copy(tl("yT")[:, :, c_], py[:]), reads=[r_py], writes=rs("yT"))
                    pp, r_pp = nextB()
                    for h in range(H):
                        q, hp = hsl(h)
                        P.mm(pp[q, hp, :], tl("BeT" + sfx)[q, hp, :], tl("Us")[q, hp, :], start=True, stop=False, reads=rs("BeT" + sfx, "Us"), writes=[r_pp])
                        P.mm(pp[q, hp, :], tl("KeT" + sfx)[q, hp, :], tl("Vt" + sfx)[q, hp, :], start=False, stop=True, reads=rs("KeT" + sfx, "Vt" + sfx), writes=[r_pp])
                    op("pool", lambda e: e.tensor_tensor(tl("Pt")[:], tl("P")[:], tl("gL")[:, :, cc:cc + 1].to_broadcast([128, HP, 64]), ALU.mult), ["P", "gL"], ["Pt"])
                    P.add("dve", lambda e: e.tensor_tensor(tl("P")[:], tl("Pt")[:], pp[:], ALU.add), reads=[r_pp] + rs("Pt"), writes=rs("P"))
                    op("act", lambda e: e.copy(tl("Pb")[:], tl("P")[:]), ["P"], ["Pb"])
                ones_reduce("onesm", "yT")
                P.add("dve", lambda e: e.tensor_tensor(tl("yT")[:], tl("yT")[:], psC3, ALU.subtract), reads=[r_psC] + rs("yT"), writes=rs("yT"))
                op("pool", lambda e: e.tensor_tensor(tl("t1")[:], tl("yT")[:], tl("yT")[:], ALU.mult), ["yT"], ["t1"])
                ones_reduce("onesm", "t1")
                P.act(tl("t1")[:], psC3, AF.Sqrt, reads=[r_psC], writes=rs("t1"), bias=64e-5)
                op("dve", lambda e: e.reciprocal(tl("t1")[:], tl("t1")[:]), ["t1"], ["t1"])
                op("dve", lambda e: e.tensor_tensor(tl("yT")[:], tl("yT")[:], tl("t1")[:], ALU.mult), ["yT", "t1"], ["yT"])
                op("pool", lambda e: e.tensor_tensor(tl("yT")[:], tl("yT")[:], vb("rw_gn_g"), ALU.mult), ["yT", "rw_gn_g"], ["yT"])
                op("pool", lambda e: e.tensor_tensor(tl("yT")[:], tl("yT")[:], vb("rw_gn_b"), ALU.add), ["yT", "rw_gn_b"], ["yT"])
                op("dve", lambda e: e.tensor_tensor(tl("yT")[:], tl("yT")[:], tl("bonus")[:], ALU.add), ["yT", "bonus"], ["yT"])
                op("dve", lambda e: e.tensor_tensor(tl("ob")[:], tl("yT")[:], tl("g_")[:], ALU.mult), ["yT", "g_"], ["ob"])
                P.dma(A["oT_s"][1024 + hg * 512:1024 + (hg + 1) * 512, ts_].rearrange("(hp p) t -> p hp t", p=128), tl("ob")[:],
                      reads=rs("ob"), writes=[R["oT_s"]])


N_CORES = 4
_NC_CACHE = {}


def _core_inputs(inp, b):
    m = {"x": np.ascontiguousarray(inp["x"][b], dtype=np.float32)}
    for k, v in inp.items():
        if k == "x":
            continue
        v = np.asarray(v, dtype=np.float32)
        if k.startswith("ffn_") or k.startswith("ln_"):
            m[k] = np.ascontiguousarray(v)
        else:
            v0 = v[0]
            if k == "rw_r_k":
                v0 = v0.reshape(-1)
            m[k] = np.ascontiguousarray(v0)
    return m


def kernel(**inputs):
    if "nc" not in _NC_CACHE:
        _NC_CACHE["nc"] = build_program()
    nc = _NC_CACHE["nc"]
    in_maps = [_core_inputs(inputs, b) for b in range(N_CORES)]
    res = run_bass_kernel_spmd(nc, in_maps, core_ids=list(range(N_CORES)))
    out = np.stack([np.asarray(res.results[b]["y"], dtype=np.float32) for b in range(N_CORES)], axis=0)
    return out
```

```python
import contextlib
import numpy as np
import concourse.bass as bass
import concourse.mybir as mybir
from concourse.bass_utils import run_bass_kernel_spmd

F32 = mybir.dt.float32
BF16 = mybir.dt.bfloat16
AF = mybir.ActivationFunctionType
ALU = mybir.AluOpType
AX = mybir.AxisListType

ENGS = ("pe", "act", "dve", "pool", "sp")
SAME_ENG_SYNC = True
NSLOT = 8
SKEW = True


class Res:
    __slots__ = ("name", "lw", "rd")

    def __init__(self, name=""):
        self.name = name
        self.lw = None
        self.rd = []


class Op:
    __slots__ = ("eng", "fn", "reads", "writes", "dma", "deps", "need", "sig", "waits", "idx")

    def __init__(self, eng, fn, reads, writes, dma):
        self.eng = eng
        self.fn = fn
        self.reads = reads
        self.writes = writes
        self.dma = dma
        self.deps = []
        self.need = False
        self.sig = None
        self.waits = []


class _Rec:
    def __init__(self):
        self.call = None

    def __getattr__(self, name):
        def f(*a, **k):
            assert self.call is None
            self.call = (name, a, k)
            return self
        return f

    def replay(self, e):
        name, a, k = self.call
        return getattr(e, name)(*a, **k)


class Prog:
    def __init__(self, nc, stack):
        self.nc = nc
        self.ops = []
        self.sems = {e: stack.enter_context(nc.semaphore("s_" + e)) for e in ENGS}
        self.dsems = {e: [stack.enter_context(nc.semaphore("d_%s%d" % (e, i))) for i in range(NSLOT)]
                      for e in ("sp", "act", "pool")}
        self.sigcount = {e: 0 for e in ENGS}
        self.ndma = {e: 0 for e in ("sp", "act", "pool")}
        self.waited = {e: {} for e in ENGS}
        self.live = set()
        self.nops = 0

    def add(self, eng, fn, reads=(), writes=(), dma=False):
        if fn is not None:
            rec = _Rec()
            fn(rec)
            fn = rec.replay
        op = Op(eng, fn, tuple(reads), tuple(writes), dma)
        deps = []
        for r in op.reads:
            if r.lw is not None:
                deps.append(r.lw)
        for w in op.writes:
            if w.lw is not None:
                deps.append(w.lw)
            deps.extend(w.rd)
        seen = set()
        for d in deps:
            if id(d) in seen or d is op:
                continue
            seen.add(id(d))
            if (not d.dma) and d.eng == eng and not dma:
                if eng == "pe" or not SAME_ENG_SYNC:
                    continue
            op.deps.append(d)
            d.need = True
        for r in op.reads:
            r.rd.append(op)
            self.live.add(r)
        for w in op.writes:
            w.lw = op
            w.rd = []
            self.live.add(w)
        self.ops.append(op)
        return op

    def flush(self):
        nc = self.nc
        ops = self.ops
        self.ops = []
        if not ops:
            return
        for r in self.live:
            if r.lw is not None:
                r.lw.need = True
            for o in r.rd:
                o.need = True
        for op in ops:
            eng = op.eng
            waits = []
            if op.dma:
                i = self.ndma[eng]
                self.ndma[eng] = i + 1
                sem = self.dsems[eng][i % NSLOT]
                val = 16 * (i // NSLOT + 1)
                if val > 16:
                    if self.waited[eng].get(sem, 0) < val - 16:
                        waits.append((sem, val - 16))
                        self.waited[eng][sem] = val - 16
                op.sig = (sem, val)
            elif op.need and op.fn is not None:
                self.sigcount[eng] += 1
                op.sig = (self.sems[eng], self.sigcount[eng])
            for d in op.deps:
                if d.sig is None:
                    continue
                sem, val = d.sig
                if self.waited[eng].get(sem, 0) >= val:
                    continue
                self.waited[eng][sem] = val
                waits.append((sem, val))
            op.waits = waits
        per = {e: [o for o in ops if o.eng == e] for e in ENGS}
        self.nops += len(ops)

        def emit(e, lst):
            for op in lst:
                for sem, val in op.waits:
                    e.wait_ge(sem, val)
                if op.fn is None:
                    continue
                ins = op.fn(e)
                if op.sig is not None:
                    ins.then_inc(op.sig[0], 16 if op.dma else 1)

        with nc.Block() as block:
            @block.tensor
            def _(e):
                emit(e, per["pe"])

            @block.scalar
            def _(e):
                emit(e, per["act"])

            @block.vector
            def _(e):
                emit(e, per["dve"])

            @block.gpsimd
            def _(e):
                emit(e, per["pool"])

            @block.sync
            def _(e):
                emit(e, per["sp"])
        for op in ops:
            op.fn = None

    def dma(self, out, in_, reads=(), writes=(), q="sp"):
        return self.add(q, lambda e: e.dma_start(out=out, in_=in_), reads, writes, dma=True)

    def mm(self, out, lhsT, rhs, start=True, stop=True, reads=(), writes=()):
        return self.add("pe", lambda e: e.matmul(out, lhsT, rhs, start=start, stop=stop), reads, writes)

    def tr(self, out, in_, ident, reads=(), writes=()):
        return self.add("pe", lambda e: e.transpose(out, in_, ident), reads, writes)

    def act(self, out, in_, func, reads=(), writes=(), **kw):
        return self.add("act", lambda e: e.activation(out, in_, func, **kw), reads, writes)

    def wait_all(self, eng, ress):
        return self.add(eng, None, reads=ress)


S = 2048
D = 2048
DFF = 5632
ALPHA = 4.0 ** 0.25
LN_EPS = 1e-5
EXPM05 = float(np.exp(-0.5))


class Ctx:
    pass


DBG_RW = ["r_", "k_", "v_", "lw", "cl", "a_", "g_", "kk", "kt", "bh", "bonus", "RG", "AG", "BI", "KI", "Bend", "Kend", "yT"]
DBG_RW2 = ["Vt0", "BeT0", "KeT0", "M0", "N0", "Aak0", "Arb0", "Ark0", "TT0", "Vt1", "TT1", "P", "Xs", "Us"]


def build_program(phases=None, kinds=None):
    kinds = kinds or {}
    nc = bass.Bass("TRN2", target_bir_lowering=False)
    g = Ctx()
    g.nc = nc

    def dram(name, shape, dt, kind="Internal"):
        return nc.dram_tensor(name, list(shape), dt, kind=kinds.get(name, kind)).ap()

    I = "ExternalInput"
    A = {}
    A["x"] = dram("x", [S, D], F32, I)
    A["even_w_in"] = dram("even_w_in", [D, 6432], F32, I)
    A["even_shift_mu"] = dram("even_shift_mu", [3360], F32, I)
    for n in ("rw_w0", "rw_a0", "rw_k_k", "rw_k_a", "rw_r_k", "rw_gn_g", "rw_gn_b", "gla_gate_b"):
        A[n] = dram(n, [1024], F32, I)
    A["rw_w2"] = dram("rw_w2", [64, 1024], F32, I)
    A["rw_a2"] = dram("rw_a2", [64, 1024], F32, I)
    A["rw_g2"] = dram("rw_g2", [160, 1024], F32, I)
    A["even_w_out"] = dram("even_w_out", [D, D], F32, I)
    A["odd_w_in"] = dram("odd_w_in", [D, 6160], F32, I)
    A["gla_gate_w2"] = dram("gla_gate_w2", [16, 1024], F32, I)
    for n in ("gla_r_b", "gla_gn_g", "gla_gn_b"):
        A[n] = dram(n, [2048], F32, I)
    A["odd_w_out"] = dram("odd_w_out", [D, D], F32, I)
    for n in ("ln_mix_g", "ln_mix_b", "ln_ffn_g", "ln_ffn_b"):
        A[n] = dram(n, [2, D], F32, I)
    A["ffn_w_up"] = dram("ffn_w_up", [2, D, 2 * DFF], F32, I)
    A["ffn_conv_w"] = dram("ffn_conv_w", [2, 3, DFF], F32, I)
    A["ffn_conv_b"] = dram("ffn_conv_b", [2, DFF], F32, I)
    A["ffn_w_down"] = dram("ffn_w_down", [2, DFF, D], F32, I)
    A["y"] = dram("y", [S, D], F32, "ExternalOutput")
    A["qT_s"] = dram("qT_s", [1024, S], BF16)
    A["kT_s"] = dram("kT_s", [1024, S], BF16)
    A["v_s"] = dram("v_s", [S, 1024], BF16)
    A["rwT_s"] = dram("rwT_s", [3360, S], F32)
    A["oT_s"] = dram("oT_s", [D, S], BF16)
    A["h_a"] = dram("h_a", [S, D], F32)
    A["h_b"] = dram("h_b", [S, D], F32)
    A["hffT_s"] = dram("hffT_s", [4, 128, 44, 512], BF16)
    A["qk1_s"] = dram("qk1_s", [2048, S], F32)
    A["v1_s"] = dram("v1_s", [S, 2048], BF16)
    A["dg1_s"] = dram("dg1_s", [16, S], F32)
    A["r1_s"] = dram("r1_s", [S, 2048], F32)
    A["wdn_b"] = dram("wdn_b", [2, 4, 128, 44, 512], BF16)
    if "dbg_rw" in kinds:
        A["dbg_rw"] = dram("dbg_rw", [len(DBG_RW), 64, 8 * 128], F32)
        A["dbg_rw2"] = dram("dbg_rw2", [len(DBG_RW2), 64, 8 * 64], F32)
    g.A = A
    g.R = {n: Res(n) for n in A}

    with contextlib.ExitStack() as st:
        P = Prog(nc, st)
        g.P = P
        g.ident_f = st.enter_context(nc.sbuf_tensor("ident_f", [128, 128], F32))
        g.ident_b = st.enter_context(nc.sbuf_tensor("ident_b", [128, 128], BF16))
        g.r_ident = Res("ident")
        P.add("pool", lambda e: e.memset(g.ident_f[:], 1.0), writes=[g.r_ident])
        P.add("pool", lambda e: e.affine_select(g.ident_f[:], g.ident_f[:], [[-1, 128]], ALU.is_equal, 0.0,
                                                base=0, channel_multiplier=1), reads=[g.r_ident], writes=[g.r_ident])
        P.add("pool", lambda e: e.tensor_copy(g.ident_b[:], g.ident_f[:]), reads=[g.r_ident], writes=[g.r_ident])
        P.flush()
        allp = ["l0_inproj", "sb_attn", "rwkv", "l0_mixout", "l0_ffn", "l1_inproj", "gla", "l1_mixout", "l1_ffn"]
        ph = allp if phases is None else phases
        if "l0_inproj" in ph:
            phase_l0_inproj(g)
        if "sb_attn" in ph:
            phase_sb_attn(g)
        if "rwkv" in ph:
            phase_rwkv(g)
        if "l0_mixout" in ph:
            phase_proj_ln(g, "oT_s", 16, A["even_w_out"], "x", A["ln_mix_g"][0], A["ln_mix_b"][0], "h_a")
        if "l0_ffn" in ph:
            phase_ffn_up(g, 0, "h_a")
            phase_proj_ln(g, "hffT_s", 44, A["ffn_w_down"][0], "h_a", A["ln_ffn_g"][0], A["ln_ffn_b"][0], "h_b", wb=(A["wdn_b"][0], g.R["wdn_b"]))
        if "l1_inproj" in ph:
            phase_l1_inproj(g)
        if "gla" in ph:
            phase_gla(g)
        if "l1_mixout" in ph:
            phase_proj_ln(g, "oT_s", 16, A["odd_w_out"], "h_b", A["ln_mix_g"][1], A["ln_mix_b"][1], "h_a")
        if "l1_ffn" in ph:
            phase_ffn_up(g, 1, "h_a")
            phase_proj_ln(g, "hffT_s", 44, A["ffn_w_down"][1], "h_a", A["ln_ffn_g"][1], A["ln_ffn_b"][1], "y", wb=(A["wdn_b"][1], g.R["wdn_b"]))
        P.wait_all("sp", list(g.R.values()))
        P.flush()
    return nc


_UID = [0]


class Phase:
    def __init__(self, g):
        self.g = g
        self.st = contextlib.ExitStack()
        self.n = 0

    def __enter__(self):
        self.st.__enter__()
        return self

    def __exit__(self, *a):
        self.g.P.flush()
        return self.st.__exit__(*a)

    def sb(self, shape, dt, name=None):
        _UID[0] += 1
        t = self.st.enter_context(self.g.nc.sbuf_tensor("%s_%d" % (name or "t", _UID[0]), list(shape), dt))
        return t

    def ps(self, shape, dt, name=None):
        _UID[0] += 1
        t = self.st.enter_context(self.g.nc.psum_tensor("%s_%d" % (name or "p", _UID[0]), list(shape), dt))
        return t


def make_xT(g, ph, src_name, xT, xT_res):
    P, A, R = g.P, g.A, g.R
    src = A[src_name]
    xin = [ph.sb([128, D], BF16, "xin") for _ in range(2)]
    r_xin = [Res("xin0"), Res("xin1")]
    ptr = [ph.ps([128, 8, 128], BF16, "ptr") for _ in range(2)]
    r_ptr = [Res("ptr0"), Res("ptr1")]
    k = 0
    for tt in range(16):
        b = tt % 2
        P.dma(xin[b][:], src[tt * 128:(tt + 1) * 128, :], reads=[R[src_name]], writes=[r_xin[b]], q="pool")
        for gi in range(2):
            pb = k % 2
            k += 1
            for j in range(8):
                kc = gi * 8 + j
                P.tr(ptr[pb][:, j, :], xin[b][:, kc * 128:(kc + 1) * 128], g.ident_b[:],
                     reads=[r_xin[b], g.r_ident], writes=[r_ptr[pb]])
            dst = xT[:, gi * 8:(gi + 1) * 8, tt * 128:(tt + 1) * 128]
            if pb == 0:
                P.add("dve", lambda e, dst=dst, s=ptr[pb]: e.tensor_copy(dst, s[:]), reads=[r_ptr[pb]], writes=[xT_res[tt // 4]])
            else:
                P.add("act", lambda e, dst=dst, s=ptr[pb]: e.copy(dst, s[:]), reads=[r_ptr[pb]], writes=[xT_res[tt // 4]])


def load_vec_fm(g, ph, vec, n, dst, dst_res, pt, pt_res, parts=128):
    P = g.P
    nfull = n // parts
    rem = n - nfull * parts
    nch = nfull + (1 if rem else 0)
    tmp = ph.sb([128, parts], F32, "vtmp")
    r_tmp = Res("vtmp")
    P.add("pool", lambda e: e.memset(tmp[:], 0.0), writes=[r_tmp])
    if nfull:
        P.dma(tmp[0:nfull, :], vec[0:nfull * parts].rearrange("(c p) -> c p", p=parts), writes=[r_tmp])
    if rem:
        P.dma(tmp[nfull:nfull + 1, 0:rem], vec[nfull * parts:n].rearrange("(c p) -> c p", c=1), writes=[r_tmp])
    P.tr(pt[0:parts, 0:nch], tmp[0:nch, 0:parts], g.ident_f[0:nch, 0:nch], reads=[r_tmp, g.r_ident], writes=[pt_res])
    P.add("dve", lambda e: e.tensor_copy(dst, pt[0:parts, 0:nch]), reads=[pt_res], writes=[dst_res])
    return nch


def phase_l0_inproj(g):
    P, A, R, nc = g.P, g.A, g.R, g.nc
    with Phase(g) as ph:
        xT = ph.sb([128, 16, S], BF16, "xT")
        xT_res = [Res("xT%d" % i) for i in range(4)]
        with Phase(g) as ph2:
            make_xT(g, ph2, "x", xT, xT_res)
        w = [ph.sb([128, 16, 512], BF16, "w") for _ in range(2)]
        r_w = [Res("w0"), Res("w1")]
        pacc = [ph.ps([128, 512], F32, "pacc") for _ in range(4)]
        r_pacc = [Res("pacc%d" % i) for i in range(4)]
        mu = ph.sb([128, 27], F32, "mu")
        omu = ph.sb([128, 27], F32, "omu")
        r_mu = Res("mu")
        ptv = ph.ps([128, 128], F32, "ptv")
        r_ptv = Res("ptv")
        load_vec_fm(g, ph, A["even_shift_mu"], 3360, mu[:, 0:27], r_mu, ptv, r_ptv)
        P.add("dve", lambda e: e.tensor_scalar(omu[:], mu[:], -1.0, 1.0, ALU.mult, ALU.add), reads=[r_mu], writes=[r_mu])
        stage_b = [ph.sb([128, S], BF16, "stb") for _ in range(2)]
        r_stb = [Res("stb0"), Res("stb1")]
        stage_v = [ph.sb([128, 512], BF16, "stv") for _ in range(2)]
        r_stv = [Res("stv0"), Res("stv1")]
        ubuf = [ph.sb([128, S + 1], F32, "ubuf") for _ in range(2)]
        r_ub = [Res("ub0"), Res("ub1")]
        shf = [ph.sb([128, S], F32, "shf") for _ in range(2)]
        r_shf = [Res("shf0"), Res("shf1")]
        for b in range(2):
            P.add("pool", lambda e, b=b: e.memset(ubuf[b][:, 0:1], 0.0), writes=[r_ub[b]])
        W = A["even_w_in"]
        ngroups = (6432 + 511) // 512
        cnt = {"pacc": 0, "chunk": 0, "v": 0}

        def load_w(gi):
            c0 = gi * 512
            gw = min(512, 6432 - c0)
            b = gi % 2
            P.dma(w[b][:, :, 0:gw], W[:, c0:c0 + gw].rearrange("(kc p) c -> p kc c", p=128), writes=[r_w[b]], q="pool")

        load_w(0)
        for gi in range(ngroups):
            c0 = gi * 512
            gw = min(512, 6432 - c0)
            b = gi % 2
            if gi + 1 < ngroups:
                load_w(gi + 1)
            if 2048 <= c0 < 3072:
                for tt in range(16):
                    pi = cnt["pacc"] % 4
                    cnt["pacc"] += 1
                    for kc in range(16):
                        P.mm(pacc[pi][:, :], xT[:, kc, tt * 128:(tt + 1) * 128], w[b][:, kc, 0:512], start=(kc == 0), stop=(kc == 15),
                             reads=[xT_res[tt // 4], r_w[b]], writes=[r_pacc[pi]])
                    sv = cnt["v"] % 2
                    cnt["v"] += 1
                    P.add("act", lambda e, sv=sv, pi=pi: e.copy(stage_v[sv][:], pacc[pi][:]), reads=[r_pacc[pi]], writes=[r_stv[sv]])
                    P.dma(A["v_s"][tt * 128:(tt + 1) * 128, c0 - 2048:c0 - 2048 + 512], stage_v[sv][:], reads=[r_stv[sv]], writes=[R["v_s"]])
                continue
            nchunk = (gw + 127) // 128
            for j in range(nchunk):
                cw = min(128, gw - j * 128)
                col = c0 + j * 128
                ci = cnt["chunk"] % 2
                cnt["chunk"] += 1
                for qt in range(4):
                    pi = cnt["pacc"] % 4
                    cnt["pacc"] += 1
                    for kc in range(16):
                        P.mm(pacc[pi][0:cw, :], w[b][:, kc, j * 128:j * 128 + cw], xT[:, kc, qt * 512:(qt + 1) * 512], start=(kc == 0), stop=(kc == 15),
                             reads=[xT_res[qt], r_w[b]], writes=[r_pacc[pi]])
                    if col < 2048:
                        sc = (128.0 ** -0.5) if col < 1024 else 1.0
                        P.add("act", lambda e, ci=ci, pi=pi, qt=qt, sc=sc: e.activation(stage_b[ci][:, qt * 512:(qt + 1) * 512], pacc[pi][:], AF.Copy, scale=sc),
                              reads=[r_pacc[pi]], writes=[r_stb[ci]])
                    else:
                        P.add("act", lambda e, ci=ci, pi=pi, qt=qt, cw=cw: e.copy(ubuf[ci][0:cw, 1 + qt * 512:1 + (qt + 1) * 512], pacc[pi][0:cw, :]),
                              reads=[r_pacc[pi]], writes=[r_ub[ci]])
                if col < 2048:
                    dst = A["qT_s"] if col < 1024 else A["kT_s"]
                    dn = "qT_s" if col < 1024 else "kT_s"
                    r0 = col % 1024
                    P.dma(dst[r0:r0 + 128, :], stage_b[ci][:], reads=[r_stb[ci]], writes=[R[dn]])
                else:
                    rc = (col - 3072) // 128
                    P.add("dve", lambda e, ci=ci, rc=rc, cw=cw: e.tensor_scalar(shf[ci][0:cw, :], ubuf[ci][0:cw, 0:S], mu[0:cw, rc:rc + 1], None, ALU.mult),
                          reads=[r_ub[ci], r_mu], writes=[r_shf[ci]])
                    P.add("dve", lambda e, ci=ci, rc=rc, cw=cw: e.scalar_tensor_tensor(shf[ci][0:cw, :], ubuf[ci][0:cw, 1:S + 1], omu[0:cw, rc:rc + 1], shf[ci][0:cw, :], ALU.mult, ALU.add),
                          reads=[r_ub[ci], r_mu, r_shf[ci]], writes=[r_shf[ci]])
                    P.dma(A["rwT_s"][col - 3072:col - 3072 + cw, :], shf[ci][0:cw, :], reads=[r_shf[ci]], writes=[R["rwT_s"]])


def phase_sb_attn(g, NH=8):
    P, A, R, nc = g.P, g.A, g.R, g.nc
    NA = 3
    with Phase(g) as ph:
        qT = [ph.sb([128, S], BF16, "qT") for _ in range(2)]
        kT = [ph.sb([128, S], BF16, "kT") for _ in range(2)]
        vv = [ph.sb([128, 16, 128], BF16, "vv") for _ in range(2)]
        r_in = [Res("sbin0"), Res("sbin1")]
        tri = ph.sb([128, 128], F32, "tri")
        ones = ph.sb([128, 128], F32, "ones")
        r_c = Res("sbconst")
        P.add("pool", lambda e: e.memset(tri[:], 1.0), writes=[r_c])
        P.add("pool", lambda e: e.memset(ones[:], 1.0), writes=[r_c])
        P.add("pool", lambda e: e.affine_select(tri[:], tri[:], [[-1, 128]], ALU.is_ge, 0.0, base=0, channel_multiplier=1),
              reads=[r_c], writes=[r_c])
        pz = [ph.ps([128, 512], F32, "pz") for _ in range(NA)]
        r_pz = [Res("pz%d" % i) for i in range(NA)]
        zs = [ph.sb([128, 512], F32, "zs") for _ in range(NA)]
        r_zs = [Res("zs%d" % i) for i in range(NA)]
        sp = [ph.sb([128, 512], F32, "sp") for _ in range(NA)]
        r_sp = [Res("sp%d" % i) for i in range(NA)]
        pt = [ph.ps([128, 512], F32, "pt") for _ in range(2)]
        r_pt = [Res("pt0"), Res("pt1")]
        po = [ph.ps([128, 512], F32, "po") for _ in range(2)]
        r_po = [Res("po0"), Res("po1")]
        sacc = [ph.sb([128, 512], F32, "sacc") for _ in range(2)]
        r_sacc = [Res("sacc0"), Res("sacc1")]
        ee = [ph.sb([128, 512], F32, "ee") for _ in range(2)]
        r_ee = [Res("ee0"), Res("ee1")]
        ww = [ph.sb([128, 512], BF16, "ww") for _ in range(3)]
        r_ww = [Res("ww0"), Res("ww1"), Res("ww2")]
        osb = [ph.sb([128, S], BF16, "osb") for _ in range(2)]
        r_osb = [Res("osb0"), Res("osb1")]

        def load(h):
            b = h % 2
            P.dma(qT[b][:], A["qT_s"][h * 128:(h + 1) * 128, :], reads=[R["qT_s"]], writes=[r_in[b]])
            P.dma(kT[b][:], A["kT_s"][h * 128:(h + 1) * 128, :], reads=[R["kT_s"]], writes=[r_in[b]])
            P.dma(vv[b][:], A["v_s"][:, h * 128:(h + 1) * 128].rearrange("(t p) d -> p t d", p=128), reads=[R["v_s"]], writes=[r_in[b]])

        steps = []
        gq = 0
        for h in range(NH):
            for qt in range(4):
                kbs = list(range(4 * qt + 3, -1, -1))
                for ki, kb in enumerate(kbs):
                    steps.append((h, qt, ki, kb, len(kbs), gq))
                gq += 1

        def stageA(i):
            h, qt, ki, kb, nk, gq = steps[i]
            a = i % NA
            b = h % 2
            t0, s0 = qt * 512, kb * 128
            if i == 0:
                load(0)
            P.mm(pz[a][:], kT[b][:, s0:s0 + 128], qT[b][:, t0:t0 + 512], reads=[r_in[b]], writes=[r_pz[a]])
            P.act(sp[a][:], pz[a][:], AF.Exp, reads=[r_pz[a]], writes=[r_sp[a]])
            P.add("act", lambda e: e.copy(zs[a][:], pz[a][:]), reads=[r_pz[a]], writes=[r_zs[a]])
            P.act(sp[a][:], sp[a][:], AF.Ln, reads=[r_sp[a]], writes=[r_sp[a]], bias=1.0)
            if kb >= 4 * qt:
                P.add("pool", lambda e: e.affine_select(sp[a][:], sp[a][:], [[1, 512]], ALU.is_gt, 0.0, base=t0 - s0, channel_multiplier=-1),
                      reads=[r_sp[a]], writes=[r_sp[a]])

        def stageB(i):
            h, qt, ki, kb, nk, gq = steps[i]
            a = i % NA
            i2 = i % 2
            b = h % 2
            t0, s0 = qt * 512, kb * 128
            ob = gq % 2
            sa, r_sa = sacc[gq % 2], r_sacc[gq % 2]
            P.mm(pt[i2][:], tri[:], sp[a][:], start=True, stop=(ki == 0), reads=[r_sp[a], r_c], writes=[r_pt[i2]])
            if ki > 0:
                P.mm(pt[i2][:], ones[:], sa[:], start=False, stop=True, reads=[r_sa, r_c], writes=[r_pt[i2]])
            P.add("dve", lambda e: e.tensor_tensor(ee[i2][:], zs[a][:], pt[i2][:], ALU.subtract), reads=[r_zs[a], r_pt[i2]], writes=[r_ee[i2]])
            i3 = i % 3
            P.act(ww[i3][:], ee[i2][:], AF.Exp, reads=[r_ee[i2]], writes=[r_ww[i3]])
            if kb >= 4 * qt:
                P.add("pool", lambda e: e.affine_select(ww[i3][:], ww[i3][:], [[1, 512]], ALU.is_gt, 0.0, base=t0 - s0, channel_multiplier=-1),
                      reads=[r_ww[i3]], writes=[r_ww[i3]])
            if ki == 0:
                P.add("pool", lambda e: e.tensor_copy(sa[:], sp[a][:]), reads=[r_sp[a]], writes=[r_sa])
            elif ki + 1 < nk:
                P.add("pool", lambda e: e.tensor_tensor(sa[:], sa[:], sp[a][:], ALU.add), reads=[r_sp[a], r_sa], writes=[r_sa])

        def stageC(i):
            h, qt, ki, kb, nk, gq = steps[i]
            i3 = i % 3
            b = h % 2
            t0 = qt * 512
            ob = gq % 2
            P.mm(po[ob][:], vv[b][:, kb, :], ww[i3][:], start=(ki == 0), stop=(ki == nk - 1), reads=[r_ww[i3], r_in[b]], writes=[r_po[ob]])
            if ki == nk - 1:
                P.add("act", lambda e: e.copy(osb[b][:, t0:t0 + 512], po[ob][:]), reads=[r_po[ob]], writes=[r_osb[b]])
                if qt == 3:
                    P.dma(A["oT_s"][h * 128:(h + 1) * 128, :], osb[b][:], reads=[r_osb[b]], writes=[R["oT_s"]])

        for i in range(len(steps) + 2):
            if i < len(steps):
                stageA(i)
            if 1 <= i <= len(steps):
                stageB(i - 1)
            if i >= 2:
                stageC(i - 2)
            j = i - 1
            if 0 <= j < len(steps) and steps[j][1] == 0 and steps[j][2] == 0 and steps[j][0] + 1 < NH:
                load(steps[j][0] + 1)


def bcast_vec(g, ph, vec, n, name):
    t = ph.sb([128, n], F32, name)
    r = Res(name)
    g.P.dma(t[:], vec.partition_broadcast(128), writes=[r])
    return t, r


def layer_norm_tile(g, pre, r_pre, gam, bet, r_gb, out, r_out, stats, mv, rstd, r_tmp, negh, n=D, eps=LN_EPS):
    P = g.P
    nch = n // 512
    for c in range(nch):
        P.add("dve", lambda e, c=c: e.bn_stats(stats[:, c, :], pre[:, c * 512:(c + 1) * 512]), reads=[r_pre], writes=[r_tmp])
    P.add("dve", lambda e: e.bn_aggr(mv[:], stats[:, 0:nch, :]), reads=[r_tmp], writes=[r_tmp])
    P.add("pool", lambda e: e.tensor_scalar(rstd[:], mv[:, 1:2], eps, None, ALU.add), reads=[r_tmp], writes=[r_tmp])
    P.add("pool", lambda e: e.tensor_tensor(rstd[:], rstd[:], negh, ALU.pow), reads=[r_tmp, r_gb], writes=[r_tmp])
    P.add("dve", lambda e: e.tensor_scalar(out, pre, mv[:, 0:1], rstd[:, 0:1], ALU.subtract, ALU.mult), reads=[r_pre, r_tmp], writes=[r_out])
    P.add("dve", lambda e: e.tensor_tensor(out, out, gam, ALU.mult), reads=[r_gb, r_out], writes=[r_out])
    P.add("pool", lambda e: e.tensor_tensor(out, out, bet, ALU.add), reads=[r_gb, r_out], writes=[r_out])


def phase_proj_ln(g, a_name, KC, Wd, resid_name, ln_g, ln_b, out_name, wb=None):
    P, A, R, nc = g.P, g.A, g.R, g.nc
    CG = 512
    ncg = D // CG
    resident = KC <= 16
    with Phase(g) as ph:
        naT = 2 if resident else 1
        aT = [ph.sb([128, KC, 512], BF16, "aT") for _ in range(naT)]
        r_aT = [Res("aT%d" % i) for i in range(naT)]
        nw = ncg if resident else 2
        w = [ph.sb([128, KC, CG], BF16, "wp") for _ in range(nw)]
        r_w = [Res("wp%d" % i) for i in range(nw)]
        npre = 2 if resident else 1
        pre2 = [ph.sb([128, 4, D], F32, "pre") for _ in range(npre)]
        r_pre2 = [[Res("pre%d_%d" % (j, i)) for i in range(4)] for j in range(npre)]
        gam, r_g = bcast_vec(g, ph, ln_g, D, "gam")
        bet, r_b = bcast_vec(g, ph, ln_b, D, "bet")
        r_gb = Res("gb")
        negh = ph.sb([128, 1], F32, "negh")
        P.add("pool", lambda e: e.memset(negh[:], -0.5), reads=[r_g, r_b], writes=[r_gb])
        P.add("dve", None, reads=[r_g, r_b, r_gb])
        stats = [ph.sb([128, 4, 6], F32, "stats") for _ in range(4)]
        mv = [ph.sb([128, 2], F32, "mv") for _ in range(4)]
        rstd = [ph.sb([128, 1], F32, "rstd") for _ in range(4)]
        r_tmp = [Res("lntmp%d" % i) for i in range(4)]
        pacc = [ph.ps([128, 512], F32, "pp") for _ in range(4)]
        r_pacc = [Res("pp%d" % i) for i in range(4)]
        cnt = 0

        def load_w(i):
            cg = i % ncg
            b = cg if resident else i % 2
            if wb is None:
                P.dma(w[b][:], Wd[:, cg * CG:(cg + 1) * CG].rearrange("(kc p) c -> p kc c", p=128), writes=[r_w[b]], q="pool")
            else:
                P.dma(w[b][:], wb[0][cg], reads=[wb[1]], writes=[r_w[b]], q="act")

        def load_a(tg):
            if a_name == "hffT_s":
                P.dma(aT[tg % naT][:], A[a_name][tg], reads=[R[a_name]], writes=[r_aT[tg % naT]])
            else:
                P.dma(aT[tg % naT][:], A[a_name][:, tg * 512:(tg + 1) * 512].rearrange("(kc p) t -> p kc t", p=128), reads=[R[a_name]], writes=[r_aT[tg % naT]])

        total = 4 * ncg
        if resident:
            for i in range(ncg):
                load_w(i)
        else:
            load_w(0)
        load_a(0)
        for tg in range(4):
            ab = tg % naT
            pre = pre2[tg % npre]
            r_pre = r_pre2[tg % npre]
            if naT > 1 and tg + 1 < 4:
                load_a(tg + 1)
            for cg in range(ncg):
                i = tg * ncg + cg
                b = cg if resident else i % 2
                if not resident and i + 1 < total:
                    load_w(i + 1)
                for tt in range(4):
                    pi = cnt % 4
                    cnt += 1
                    for kc in range(KC):
                        P.mm(pacc[pi][:, 0:CG], aT[ab][:, kc, tt * 128:(tt + 1) * 128], w[b][:, kc, :], start=(kc == 0), stop=(kc == KC - 1),
                             reads=[r_aT[ab], r_w[b]], writes=[r_pacc[pi]])
                    P.add("act", lambda e: e.activation(pre[:, tt, cg * CG:(cg + 1) * CG], pacc[pi][:, 0:CG], AF.Copy, scale=1.0 / ALPHA),
                          reads=[r_pacc[pi]], writes=[r_pre[tt]])
            if naT == 1 and tg + 1 < 4:
                load_a(tg + 1)
            for tt in range(4):
                tok = tg * 4 + tt
                P.add("pool", lambda e: e.dma_start(out=pre[:, tt, :], in_=A[resid_name][tok * 128:(tok + 1) * 128, :], accum_op=ALU.add),
                      reads=[R[resid_name]], writes=[r_pre[tt]], dma=True)
            for tt in range(4):
                tok = tg * 4 + tt
                layer_norm_tile(g, pre[:, tt, :], r_pre[tt], gam[:], bet[:], r_gb, pre[:, tt, :], r_pre[tt], stats[tt], mv[tt], rstd[tt], r_tmp[tt], negh[:],
                                eps=LN_EPS / (ALPHA * ALPHA))
                P.dma(A[out_name][tok * 128:(tok + 1) * 128, :], pre[:, tt, :], reads=[r_pre[tt]], writes=[R[out_name]], q="pool")


def phase_ffn_up(g, layer, h_name):
    P, A, R, nc = g.P, g.A, g.R, g.nc
    Wu = A["ffn_w_up"][layer]
    with Phase(g) as ph:
        xT = ph.sb([128, 16, S], BF16, "hT")
        xT_res = [Res("hT%d" % i) for i in range(4)]
        with Phase(g) as ph2:
            make_xT(g, ph2, h_name, xT, xT_res)
        for cg in range(4):
            P.dma(A["wdn_b"][layer, cg], A["ffn_w_down"][layer][:, cg * 512:(cg + 1) * 512].rearrange("(kc p) c -> p kc c", p=128),
                  writes=[R["wdn_b"]], q="pool")
        w = [ph.sb([128, 16, 256], BF16, "wu") for _ in range(2)]
        r_w = [Res("wu0"), Res("wu1")]
        cw = ph.sb([128, 3, 44], F32, "cw")
        cb = ph.sb([128, 44], F32, "cb")
        r_cw = Res("cw")
        ptv = ph.ps([128, 128], F32, "ptv")
        r_ptv = Res("ptv")
        for j in range(3):
            load_vec_fm(g, ph, A["ffn_conv_w"][layer, j], DFF, cw[:, j, :], r_cw, ptv, r_ptv)
        load_vec_fm(g, ph, A["ffn_conv_b"][layer], DFF, cb[:, :], r_cw, ptv, r_ptv)
        pg = [ph.ps([128, 512], F32, "pg") for _ in range(2)]
        r_pg = [Res("pg0"), Res("pg1")]
        pu = [ph.ps([128, 512], F32, "pu") for _ in range(2)]
        r_pu = [Res("pu0"), Res("pu1")]
        gbuf = [ph.sb([128, S + 2], F32, "gbuf") for _ in range(2)]
        r_gb = [Res("gbuf0"), Res("gbuf1")]
        ubuf = [ph.sb([128, S], F32, "ubuf") for _ in range(2)]
        r_ub = [Res("fub0"), Res("fub1")]
        t1 = [ph.sb([128, S], F32, "t1") for _ in range(2)]
        r_t1 = [Res("t10"), Res("t11")]
        hf = [ph.sb([128, S], BF16, "hf") for _ in range(2)]
        r_hf = [Res("hf0"), Res("hf1")]
        for b in range(2):
            P.add("pool", lambda e, b=b: e.memset(gbuf[b][:, 0:2], 0.0), writes=[r_gb[b]])

        def load_w(c):
            b = c % 2
            P.dma(w[b][:, :, 0:128], Wu[:, c * 128:(c + 1) * 128].rearrange("(kc p) c -> p kc c", p=128), writes=[r_w[b]], q="pool")
            P.dma(w[b][:, :, 128:256], Wu[:, DFF + c * 128:DFF + (c + 1) * 128].rearrange("(kc p) c -> p kc c", p=128), writes=[r_w[b]], q="pool")

        load_w(0)
        k = 0
        for c in range(44):
            b = c % 2
            if c + 1 < 44:
                load_w(c + 1)
            for qt in range(4):
                pi = k % 2
                k += 1
                for kc in range(16):
                    P.mm(pg[pi][:], w[b][:, kc, 0:128], xT[:, kc, qt * 512:(qt + 1) * 512], start=(kc == 0), stop=(kc == 15),
                         reads=[xT_res[qt], r_w[b]], writes=[r_pg[pi]])
                for kc in range(16):
                    P.mm(pu[pi][:], w[b][:, kc, 128:256], xT[:, kc, qt * 512:(qt + 1) * 512], start=(kc == 0), stop=(kc == 15),
                         reads=[xT_res[qt], r_w[b]], writes=[r_pu[pi]])
                P.add("act", lambda e, b=b, pi=pi, qt=qt: e.copy(gbuf[b][:, 2 + qt * 512:2 + (qt + 1) * 512], pg[pi][:]), reads=[r_pg[pi]], writes=[r_gb[b]])
                P.add("dve", lambda e, b=b, pi=pi, qt=qt: e.tensor_copy(ubuf[b][:, qt * 512:(qt + 1) * 512], pu[pi][:]), reads=[r_pu[pi]], writes=[r_ub[b]])
            P.add("pool", lambda e, b=b, c=c: e.tensor_scalar(t1[b][:], gbuf[b][:, 0:S], cw[:, 0, c:c + 1], cb[:, c:c + 1], ALU.mult, ALU.add),
                  reads=[r_gb[b], r_cw], writes=[r_t1[b]])
            P.add("dve", lambda e, b=b, c=c: e.scalar_tensor_tensor(t1[b][:], gbuf[b][:, 1:S + 1], cw[:, 1, c:c + 1], t1[b][:], ALU.mult, ALU.add),
                  reads=[r_gb[b], r_cw, r_t1[b]], writes=[r_t1[b]])
            P.add("dve", lambda e, b=b, c=c: e.scalar_tensor_tensor(t1[b][:], gbuf[b][:, 2:S + 2], cw[:, 2, c:c + 1], t1[b][:], ALU.mult, ALU.add),
                  reads=[r_gb[b], r_cw, r_t1[b]], writes=[r_t1[b]])
            P.act(t1[b][:], t1[b][:], AF.Gelu, reads=[r_t1[b]], writes=[r_t1[b]])
            P.add("pool", lambda e, b=b: e.tensor_tensor(hf[b][:], t1[b][:], ubuf[b][:], ALU.mult), reads=[r_t1[b], r_ub[b]], writes=[r_hf[b]])
            for tg in range(4):
                P.dma(A["hffT_s"][tg, :, c, :], hf[b][:, tg * 512:(tg + 1) * 512], reads=[r_hf[b]], writes=[R["hffT_s"]])


def phase_l1_inproj(g):
    P, A, R, nc = g.P, g.A, g.R, g.nc
    W = A["odd_w_in"]
    groups = [(c, 512, "fm") for c in range(0, 2048, 512)] + [(c, 512, "v") for c in range(2048, 4096, 512)] \
        + [(4096, 16, "dg")] + [(c, 512, "r") for c in range(4112, 6160, 512)]
    with Phase(g) as ph:
        xT = ph.sb([128, 16, S], BF16, "xT1")
        xT_res = [Res("xT1%d" % i) for i in range(4)]
        with Phase(g) as ph2:
            make_xT(g, ph2, "h_b", xT, xT_res)
        w = [ph.sb([128, 16, 512], BF16, "w1") for _ in range(2)]
        r_w = [Res("w10"), Res("w11")]
        pacc = [ph.ps([128, 512], F32, "pacc") for _ in range(4)]
        r_pacc = [Res("pacc%d" % i) for i in range(4)]
        stf = [ph.sb([128, S], F32, "stf") for _ in range(2)]
        r_stf = [Res("stf0"), Res("stf1")]
        stv = [ph.sb([128, 512], BF16, "stv") for _ in range(2)]
        r_stv = [Res("stv0"), Res("stv1")]
        strr = [ph.sb([128, 512], F32, "str") for _ in range(2)]
        r_str = [Res("str0"), Res("str1")]
        cnt = {"pacc": 0, "chunk": 0, "v": 0}

        def load_w(gi):
            c0, gw, _ = groups[gi]
            b = gi % 2
            P.dma(w[b][:, :, 0:gw], W[:, c0:c0 + gw].rearrange("(kc p) c -> p kc c", p=128), writes=[r_w[b]], q="pool")

        load_w(0)
        for gi, (c0, gw, kind) in enumerate(groups):
            b = gi % 2
            if gi + 1 < len(groups):
                load_w(gi + 1)
            if kind in ("v", "r"):
                for tt in range(16):
                    pi = cnt["pacc"] % 4
                    cnt["pacc"] += 1
                    for kc in range(16):
                        P.mm(pacc[pi][:, :], xT[:, kc, tt * 128:(tt + 1) * 128], w[b][:, kc, 0:512], start=(kc == 0), stop=(kc == 15),
                             reads=[xT_res[tt // 4], r_w[b]], writes=[r_pacc[pi]])
                    sv = cnt["v"] % 2
                    cnt["v"] += 1
                    if kind == "v":
                        P.add("act", lambda e, sv=sv, pi=pi: e.copy(stv[sv][:], pacc[pi][:]), reads=[r_pacc[pi]], writes=[r_stv[sv]])
                        P.dma(A["v1_s"][tt * 128:(tt + 1) * 128, c0 - 2048:c0 - 2048 + 512], stv[sv][:], reads=[r_stv[sv]], writes=[R["v1_s"]])
                    else:
                        P.add("dve", lambda e, sv=sv, pi=pi: e.tensor_copy(strr[sv][:], pacc[pi][:]), reads=[r_pacc[pi]], writes=[r_str[sv]])
                        P.dma(A["r1_s"][tt * 128:(tt + 1) * 128, c0 - 4112:c0 - 4112 + 512], strr[sv][:], reads=[r_str[sv]], writes=[R["r1_s"]])
                continue
            nchunk = (gw + 127) // 128
            for j in range(nchunk):
                cw = min(128, gw - j * 128)
                col = c0 + j * 128
                ci = cnt["chunk"] % 2
                cnt["chunk"] += 1
                for qt in range(4):
                    pi = cnt["pacc"] % 4
                    cnt["pacc"] += 1
                    for kc in range(16):
                        P.mm(pacc[pi][0:cw, :], w[b][:, kc, j * 128:j * 128 + cw], xT[:, kc, qt * 512:(qt + 1) * 512], start=(kc == 0), stop=(kc == 15),
                             reads=[xT_res[qt], r_w[b]], writes=[r_pacc[pi]])
                    P.add("act", lambda e, ci=ci, pi=pi, qt=qt, cw=cw: e.copy(stf[ci][0:cw, qt * 512:(qt + 1) * 512], pacc[pi][0:cw, :]),
                          reads=[r_pacc[pi]], writes=[r_stf[ci]])
                if kind == "fm":
                    P.dma(A["qk1_s"][col:col + 128, :], stf[ci][:], reads=[r_stf[ci]], writes=[R["qk1_s"]])
                else:
                    P.dma(A["dg1_s"][0:16, :], stf[ci][0:16, :], reads=[r_stf[ci]], writes=[R["dg1_s"]])


def phase_gla(g):
    P, A, R, nc = g.P, g.A, g.R, g.nc
    SC = 1.0 / 16.0
    with Phase(g) as ph:
        gw2 = ph.sb([16, 1024], F32, "gw2")
        dgT = ph.sb([16, S], F32, "dgT")
        r_c = Res("glac")
        P.dma(gw2[:], A["gla_gate_w2"], writes=[r_c])
        P.dma(dgT[:], A["dg1_s"], reads=[R["dg1_s"]], writes=[r_c])
        ngb = ph.sb([128, 8], F32, "ngb")
        ptv = ph.ps([128, 128], F32, "ptv")
        r_ptv = Res("ptv")
        load_vec_fm(g, ph, A["gla_gate_b"], 1024, ngb[:, :], r_c, ptv, r_ptv)
        P.add("dve", lambda e: e.tensor_scalar(ngb[:], ngb[:], -1.0, None, ALU.mult), reads=[r_c], writes=[r_c])
        msk = ph.sb([128, 128], F32, "msk")
        P.add("pool", lambda e: e.memset(msk[:], 1.0), writes=[r_c])
        P.add("pool", lambda e: e.affine_select(msk[:], msk[:], [[1, 128]], ALU.is_ge, 0.0, base=0, channel_multiplier=-1),
              reads=[r_c], writes=[r_c])
        gnegh = ph.sb([128, 1], F32, "gnegh")
        P.add("pool", lambda e: e.memset(gnegh[:], -0.5), writes=[r_c])
        rst = ph.sb([128, 16, 128], F32, "rst")
        P.add("pool", lambda e: e.memset(rst[:], 1.0), writes=[r_c])
        P.add("pool", lambda e: e.memset(rst[:, :, 0:1], 0.0), reads=[r_c], writes=[r_c])
        rstf = rst[:].rearrange("p c l -> p (c l)")
        qf = ph.sb([128, S], F32, "qf")
        kf = ph.sb([128, S], F32, "kf")
        r_qk = Res("qkf")
        csp = ph.sb([128, 16, 128], F32, "csp")
        cspf = csp[:].rearrange("p c l -> p (c l)")
        r_csp = Res("csp")
        tmp = ph.sb([128, 16, 128], F32, "gtmp")
        tmpf = tmp[:].rearrange("p c l -> p (c l)")
        r_tmp = Res("gtmp")
        qd = ph.sb([128, 2, S], BF16, "qd")
        kinv = ph.sb([128, 2, S], BF16, "kinv")
        kendT = ph.sb([128, 2, S], BF16, "kendT")
        r_qd, r_kinv, r_kendT = Res("qd"), Res("kinv"), Res("kendT")
        kend = ph.sb([128, 16, 256], BF16, "kend")
        r_kend = Res("kend")
        dec = ph.sb([128, 2, 16], F32, "dec")
        r_dec = Res("dec")
        vv = ph.sb([128, 16, 512], BF16, "vv1")
        r_vv = Res("vv1")
        S_f = ph.sb([128, 2, 512], F32, "S_f")
        S_b = ph.sb([128, 2, 512], BF16, "S_b")
        r_Sf, r_Sb = Res("S_f"), Res("S_b")
        oT_h = ph.sb([128, 4, S], BF16, "oT_h")
        r_oTh = Res("oT_h")
        gng = ph.sb([128, 512], F32, "gng")
        gnb = ph.sb([128, 512], F32, "gnb")
        rbb = ph.sb([128, 512], F32, "rbb")
        r_hc = Res("headc")
        pA = [ph.ps([128, 512], F32, "pA") for _ in range(2)]
        r_pA = [Res("pA0"), Res("pA1")]
        patt = ph.ps([128, 128], F32, "patt")
        r_patt = Res("patt")
        po = [ph.ps([128, 512], F32, "po1") for _ in range(2)]
        r_po = [Res("po10"), Res("po11")]
        ptr = ph.ps([128, 8, 128], BF16, "ptr1")
        r_ptr = Res("ptr1")
        att_b = ph.sb([128, 128], BF16, "att_b")
        r_att = Res("att_b")
        NSL = 4
        o_sb = [ph.sb([128, 512], F32, "o_sb") for _ in range(NSL)]
        r_osb = [Res("o_sb%d" % i) for i in range(NSL)]
        rt = [ph.sb([128, 512], F32, "rt") for _ in range(NSL)]
        r_rt = [Res("rt%d" % i) for i in range(NSL)]
        og = [ph.sb([128, 512], BF16, "og") for _ in range(NSL)]
        r_og = [Res("og%d" % i) for i in range(NSL)]
        stats = [ph.sb([128, 1, 6], F32, "gstats") for _ in range(NSL)]
        mv = [ph.sb([128, 2], F32, "gmv") for _ in range(NSL)]
        rstd = [ph.sb([128, 1], F32, "grstd") for _ in range(NSL)]
        r_st = [Res("gst%d" % i) for i in range(NSL)]
        k = 0
        for h in range(4):
            P.dma(gng[:], A["gla_gn_g"][h * 512:(h + 1) * 512].partition_broadcast(128), writes=[r_hc])
            P.dma(gnb[:], A["gla_gn_b"][h * 512:(h + 1) * 512].partition_broadcast(128), writes=[r_hc])
            P.dma(rbb[:], A["gla_r_b"][h * 512:(h + 1) * 512].partition_broadcast(128), writes=[r_hc])
            P.dma(vv[:], A["v1_s"][:, h * 512:(h + 1) * 512].rearrange("(c p) v -> p c v", p=128), reads=[R["v1_s"]], writes=[r_vv])
            for dc in range(2):
                ch = h * 2 + dc
                P.dma(qf[:], A["qk1_s"][ch * 128:(ch + 1) * 128, :], reads=[R["qk1_s"]], writes=[r_qk])
                P.dma(kf[:], A["qk1_s"][1024 + ch * 128:1024 + (ch + 1) * 128, :], reads=[R["qk1_s"]], writes=[r_qk])
                for qt in range(4):
                    pi = k % 2
                    k += 1
                    P.mm(pA[pi][:], gw2[0:16, ch * 128:(ch + 1) * 128], dgT[0:16, qt * 512:(qt + 1) * 512], reads=[r_c], writes=[r_pA[pi]])
                    P.act(tmpf[:, qt * 512:(qt + 1) * 512], pA[pi][:], AF.Exp, reads=[r_pA[pi], r_c], writes=[r_tmp], scale=-1.0, bias=ngb[:, ch:ch + 1])
                P.act(tmpf, tmpf, AF.Ln, reads=[r_tmp], writes=[r_tmp], bias=1.0)
                P.add("dve", lambda e: e.tensor_tensor_scan(cspf, rstf, tmpf, 0.0, ALU.mult, ALU.add), reads=[r_tmp, r_c], writes=[r_csp])
                P.act(tmpf, cspf, AF.Exp, reads=[r_csp], writes=[r_tmp], scale=-SC)
                P.add("dve", lambda e, dc=dc: e.scalar_tensor_tensor(qd[:, dc, :], tmpf, SC, qf[:], ALU.mult, ALU.mult), reads=[r_tmp, r_qk], writes=[r_qd])
                P.act(dec[:, dc, :], csp[:, :, 127], AF.Exp, reads=[r_csp], writes=[r_dec], scale=-SC)
                P.act(tmpf, cspf, AF.Exp, reads=[r_csp], writes=[r_tmp], scale=SC)
                P.add("dve", lambda e, dc=dc: e.tensor_tensor(kinv[:, dc, :], tmpf, kf[:], ALU.mult), reads=[r_tmp, r_qk], writes=[r_kinv])
                P.add("dve", lambda e: e.tensor_tensor(tmp[:], csp[:], csp[:, :, 127:128].to_broadcast([128, 16, 128]), ALU.subtract),
                      reads=[r_csp], writes=[r_tmp])
                P.act(tmpf, tmpf, AF.Exp, reads=[r_tmp], writes=[r_tmp], scale=SC)
                P.add("dve", lambda e, dc=dc: e.tensor_tensor(kendT[:, dc, :], tmpf, kf[:], ALU.mult), reads=[r_tmp, r_qk], writes=[r_kendT])
                for half in range(2):
                    for j in range(8):
                        c = half * 8 + j
                        P.tr(ptr[:, j, :], kendT[:, dc, c * 128:(c + 1) * 128], g.ident_b[:], reads=[r_kendT, g.r_ident], writes=[r_ptr])
                    P.add("dve", lambda e, dc=dc, half=half: e.tensor_copy(kend[:, half * 8:(half + 1) * 8, dc * 128:(dc + 1) * 128], ptr[:]),
                          reads=[r_ptr], writes=[r_kend])
            kctr = [0]

            def gla_stage_a(c):
                cs = slice(c * 128, (c + 1) * 128)
                ob = c % 2
                for dc in range(2):
                    P.mm(patt[:], kinv[:, dc, cs], qd[:, dc, cs], start=(dc == 0), stop=(dc == 1), reads=[r_kinv, r_qd], writes=[r_patt])
                P.add("dve", lambda e: e.tensor_tensor(att_b[:], patt[:], msk[:], ALU.mult), reads=[r_patt, r_c], writes=[r_att])
                P.mm(po[ob][:], att_b[:], vv[:, c, :], start=True, stop=(c == 0), reads=[r_att, r_vv], writes=[r_po[ob]])
                if c > 0:
                    for dc in range(2):
                        P.mm(po[ob][:], qd[:, dc, cs], S_b[:, dc, :], start=False, stop=(dc == 1), reads=[r_qd, r_Sb], writes=[r_po[ob]])
                if c < 15:
                    for dc in range(2):
                        pi = kctr[0] % 2
                        kctr[0] += 1
                        P.mm(pA[pi][:], kend[:, c, dc * 128:(dc + 1) * 128], vv[:, c, :], reads=[r_kend, r_vv], writes=[r_pA[pi]])
                        if c == 0:
                            P.add("dve", lambda e: e.tensor_copy(S_b[:, dc, :], pA[pi][:]), reads=[r_pA[pi]], writes=[r_Sb])
                            P.add("dve", lambda e: e.tensor_copy(S_f[:, dc, :], pA[pi][:]), reads=[r_pA[pi]], writes=[r_Sf])
                        else:
                            P.add("dve", lambda e: e.scalar_tensor_tensor(S_b[:, dc, :], S_f[:, dc, :], dec[:, dc, c:c + 1], pA[pi][:], ALU.mult, ALU.add),
                                  reads=[r_pA[pi], r_Sf, r_dec], writes=[r_Sb])
                            P.add("dve", lambda e: e.scalar_tensor_tensor(S_f[:, dc, :], S_f[:, dc, :], dec[:, dc, c:c + 1], pA[pi][:], ALU.mult, ALU.add),
                                  reads=[r_pA[pi], r_Sf, r_dec], writes=[r_Sf])

            def gla_s0(c):
                cs = slice(c * 128, (c + 1) * 128)
                ob = c % 2
                sl = c % NSL
                P.add("dve", lambda e: e.tensor_copy(o_sb[sl][:], po[ob][:]), reads=[r_po[ob]], writes=[r_osb[sl]])
                P.dma(rt[sl][:], A["r1_s"][cs, h * 512:(h + 1) * 512], reads=[R["r1_s"]], writes=[r_rt[sl]])
                P.add("pool", lambda e: e.tensor_tensor(rt[sl][:], rt[sl][:], rbb[:], ALU.add), reads=[r_rt[sl], r_hc], writes=[r_rt[sl]])
                P.act(rt[sl][:], rt[sl][:], AF.Silu, reads=[r_rt[sl]], writes=[r_rt[sl]])
                P.add("dve", lambda e: e.bn_stats(stats[sl][:, 0, :], o_sb[sl][:]), reads=[r_osb[sl]], writes=[r_st[sl]])
                P.add("dve", lambda e: e.bn_aggr(mv[sl][:], stats[sl][:, 0:1, :]), reads=[r_st[sl]], writes=[r_st[sl]])

            def gla_s1(c):
                sl = c % NSL
                P.add("pool", lambda e: e.tensor_scalar(rstd[sl][:], mv[sl][:, 1:2], LN_EPS, None, ALU.add), reads=[r_st[sl]], writes=[r_st[sl]])
                P.add("pool", lambda e: e.tensor_tensor(rstd[sl][:], rstd[sl][:], gnegh[:], ALU.pow), reads=[r_st[sl], r_c], writes=[r_st[sl]])
                P.add("dve", lambda e: e.tensor_scalar(o_sb[sl][:], o_sb[sl][:], mv[sl][:, 0:1], rstd[sl][:, 0:1], ALU.subtract, ALU.mult),
                      reads=[r_st[sl], r_osb[sl]], writes=[r_osb[sl]])

            def gla_s2(c):
                sl = c % NSL
                P.add("pool", lambda e: e.tensor_tensor(o_sb[sl][:], o_sb[sl][:], gng[:], ALU.mult), reads=[r_hc, r_osb[sl]], writes=[r_osb[sl]])
                P.add("pool", lambda e: e.tensor_tensor(o_sb[sl][:], o_sb[sl][:], gnb[:], ALU.add), reads=[r_hc, r_osb[sl]], writes=[r_osb[sl]])
                P.add("dve", lambda e: e.tensor_tensor(og[sl][:], o_sb[sl][:], rt[sl][:], ALU.mult), reads=[r_osb[sl], r_rt[sl]], writes=[r_og[sl]])

            def gla_s3(c):
                cs = slice(c * 128, (c + 1) * 128)
                sl = c % NSL
                for vc in range(4):
                    P.tr(ptr[:, vc, :], og[sl][:, vc * 128:(vc + 1) * 128], g.ident_b[:], reads=[r_og[sl], g.r_ident], writes=[r_ptr])
                P.add("dve", lambda e: e.tensor_copy(oT_h[:, :, cs], ptr[:, 0:4, :]), reads=[r_ptr], writes=[r_oTh])

            for c in range(16 + 4):
                if c < 16:
                    gla_stage_a(c)
                for k_, fn_ in enumerate((gla_s0, gla_s1, gla_s2, gla_s3)):
                    cc_ = c - 1 - k_
                    if 0 <= cc_ < 16:
                        fn_(cc_)
            P.dma(A["oT_s"][h * 512:(h + 1) * 512, :].rearrange("(vc p) t -> p vc t", p=128), oT_h[:], reads=[r_oTh], writes=[R["oT_s"]])


def phase_rwkv(g):
    P, A, R, nc = g.P, g.A, g.R, g.nc
    H = 8
    HP = 4
    RW = A["rwT_s"]
    with Phase(g) as ph:
        T = {}

        def mk(name, shape, dt=F32):
            t = ph.sb(shape, dt, name)
            T[name] = (t, Res(name))
            return t

        def tl(name):
            return T[name][0]

        def rs(*names):
            return [T[n][1] for n in names]

        def op(eng, fn, reads, writes):
            P.add(eng, fn, rs(*reads), rs(*writes))

        def hsl(h):
            return slice((h % 2) * 64, (h % 2) * 64 + 64), h // 2

        mk("w2", [64, 1024]); mk("a2", [64, 1024]); mk("g2a", [128, 1024]); mk("g2b", [32, 1024])
        P.dma(tl("w2")[:], A["rw_w2"], writes=rs("w2"))
        P.dma(tl("a2")[:], A["rw_a2"], writes=rs("a2"))
        P.dma(tl("g2a")[:], A["rw_g2"][0:128, :], writes=rs("g2a"))
        P.dma(tl("g2b")[:], A["rw_g2"][128:160, :], writes=rs("g2b"))
        psA = [ph.ps([128, HP, 128], F32, "psA") for _ in range(2)]
        r_psA = [Res("psA0"), Res("psA1")]
        for n in ("rw_w0", "rw_a0", "rw_k_k", "rw_k_a", "rw_r_k", "rw_gn_g", "rw_gn_b"):
            mk(n, [128, 8])
            load_vec_fm(g, ph, A[n], 1024, tl(n)[:, :], T[n][1], psA[0][:, 0, :], r_psA[0], parts=128)
        mk("ones", [128, 128]); mk("onesm", [128, 128]); mk("mS", [128, 64]); mk("mI", [128, 64]); mk("mL", [128, 64])
        mk("identS", [128, 64])
        mk("rst", [128, HP, 2, 64])
        for nm, val in (("ones", 1.0), ("onesm", 1.0 / 64.0)):
            op("pool", lambda e: e.memset(tl(nm)[:], 0.0), [], [nm])
            for h2 in range(2):
                q = slice(h2 * 64, h2 * 64 + 64)
                op("pool", lambda e: e.memset(tl(nm)[q, q], val), [nm], [nm])
        for nm, cmp_, cm, st_ in (("mS", ALU.is_gt, -1, 1), ("mI", ALU.is_ge, -1, 1), ("mL", ALU.is_gt, 1, -1)):
            op("pool", lambda e: e.memset(tl(nm)[:], 1.0), [], [nm])
            for h2 in range(2):
                q = slice(h2 * 64, h2 * 64 + 64)
                op("pool", lambda e: e.affine_select(tl(nm)[q, :], tl(nm)[q, :], [[st_, 64]], cmp_, 0.0, base=0, channel_multiplier=cm), [nm], [nm])
        for h2 in range(2):
            q = slice(h2 * 64, h2 * 64 + 64)
            P.add("pool", lambda e: e.tensor_copy(tl("identS")[q, :], g.ident_f[q, q]), reads=[g.r_ident], writes=rs("identS"))
        op("pool", lambda e: e.memset(tl("rst")[:], 1.0), [], ["rst"])
        op("pool", lambda e: e.memset(tl("rst")[:, :, :, 0:1], 0.0), ["rst"], ["rst"])

        def bcm(name):
            return tl(name)[:].unsqueeze(1).to_broadcast([128, HP, 64])

        for n in ("r_", "k_", "v_", "lw", "cl", "a_", "g_", "ep", "en", "ex", "kk", "kt", "bh", "t1", "bonus",
                  "yT", "t2"):
            mk(n, [128, HP, 128])
        for n in ("RG", "AG", "BI", "KI", "Bend", "Kend", "vb16"):
            mk(n, [128, HP, 128], BF16)
        mk("dwT", [64, 128]); mk("daT", [64, 128]); mk("dgT", [128, 128]); mk("dgT2", [32, 128])
        mk("gL", [128, HP, 2])
        mk("ob", [128, HP, 128], BF16)
        for cc in range(2):
            for n in ("Vt", "BeT", "KeT", "M", "N", "Aak", "Arb", "Ark", "TT", "M2", "N2"):
                mk("%s%d" % (n, cc), [128, HP, 64], BF16)
        mk("P", [128, HP, 64]); mk("Xs", [128, HP, 64], BF16); mk("Us", [128, HP, 64], BF16); mk("Pt", [128, HP, 64])
        mk("Pb", [128, HP, 64], BF16)
        psB = [ph.ps([128, HP, 64], F32, "psB") for _ in range(4)]
        r_psB = [Res("psB%d" % i) for i in range(4)]
        psC = ph.ps([128, HP * 128], F32, "psC")
        r_psC = Res("psC")
        cnt = {"A": 0, "B": 0}

        def nextB():
            i = cnt["B"] % 4
            cnt["B"] += 1
            return psB[i], r_psB[i]

        def flat(name):
            return tl(name)[:].rearrange("p h t -> p (h t)")

        def ones_reduce(lhs_name, src_name):
            P.mm(psC[:, :], tl(lhs_name)[:], flat(src_name), reads=rs(lhs_name, src_name), writes=[r_psC])

        psC3 = psC[:].rearrange("p (h t) -> p h t", h=HP)

        for hg in range(2):
            prs = slice(hg * HP, (hg + 1) * HP)

            def vb(name, n=128):
                return tl(name)[:, prs].unsqueeze(2).to_broadcast([128, HP, n])

            op("pool", lambda e: e.memset(tl("P")[:], 0.0), [], ["P"])
            op("pool", lambda e: e.memset(tl("Pb")[:], 0.0), [], ["Pb"])
            for blk in range(16):
                t0 = blk * 128
                ts_ = slice(t0, t0 + 128)
                for nm, base in (("r_", 0), ("k_", 1024), ("v_", 2048)):
                    P.dma(tl(nm)[:], RW[base + hg * 512:base + (hg + 1) * 512, ts_].rearrange("(hp p) t -> p hp t", p=128),
                          reads=[R["rwT_s"]], writes=rs(nm))
                P.dma(tl("dwT")[:], RW[3072:3136, ts_], reads=[R["rwT_s"]], writes=rs("dwT"))
                P.dma(tl("daT")[:], RW[3136:3200, ts_], reads=[R["rwT_s"]], writes=rs("daT"))
                P.dma(tl("dgT")[:], RW[3200:3328, ts_], reads=[R["rwT_s"]], writes=rs("dgT"))
                P.dma(tl("dgT2")[:], RW[3328:3360, ts_], reads=[R["rwT_s"]], writes=rs("dgT2"))
                P.act(tl("dwT")[:], tl("dwT")[:], AF.Tanh, reads=rs("dwT"), writes=rs("dwT"))
                P.act(tl("dgT")[:], tl("dgT")[:], AF.Sigmoid, reads=rs("dgT"), writes=rs("dgT"))
                P.act(tl("dgT2")[:], tl("dgT2")[:], AF.Sigmoid, reads=rs("dgT2"), writes=rs("dgT2"))
                for (wn, src, dst, bn) in (("w2", "dwT", "lw", "rw_w0"), ("a2", "daT", "a_", "rw_a0"), ("g2", "dgT", "g_", None)):
                    ai = cnt["A"] % 2
                    cnt["A"] += 1
                    for hp in range(HP):
                        pr = hg * HP + hp
                        cols = slice(pr * 128, (pr + 1) * 128)
                        if wn == "g2":
                            P.mm(psA[ai][:, hp, :], tl("g2a")[:, cols], tl("dgT")[:], start=True, stop=False, reads=rs("g2a", "dgT"), writes=[r_psA[ai]])
                            P.mm(psA[ai][:, hp, :], tl("g2b")[:, cols], tl("dgT2")[:], start=False, stop=True, reads=rs("g2b", "dgT2"), writes=[r_psA[ai]])
                        else:
                            P.mm(psA[ai][:, hp, :], tl(wn)[:, cols], tl(src)[:], reads=rs(wn, src), writes=[r_psA[ai]])
                    if bn is None:
                        P.add("dve", lambda e: e.tensor_copy(tl(dst)[:], psA[ai][:]), reads=[r_psA[ai]], writes=rs(dst))
                    else:
                        for hp in range(HP):
                            pr = hg * HP + hp
                            P.act(tl(dst)[:, hp, :], psA[ai][:, hp, :], AF.Sigmoid, reads=[r_psA[ai]] + rs(bn), writes=rs(dst), bias=tl(bn)[:, pr:pr + 1])
                op("dve", lambda e: e.tensor_scalar(tl("lw")[:], tl("lw")[:], -EXPM05, None, ALU.mult), ["lw"], ["lw"])
                op("dve", lambda e: e.tensor_tensor_scan(flat("cl"), tl("rst")[:].rearrange("p h c l -> p (h c l)"), flat("lw"), 0.0, ALU.mult, ALU.add),
                   ["lw", "rst"], ["cl"])
                P.act(tl("ep")[:], tl("cl")[:], AF.Exp, reads=rs("cl"), writes=rs("ep"))
                P.act(tl("en")[:], tl("cl")[:], AF.Exp, reads=rs("cl"), writes=rs("en"), scale=-1.0)
                op("dve", lambda e: e.tensor_tensor(tl("ex")[:], tl("cl")[:], tl("lw")[:], ALU.subtract), ["cl", "lw"], ["ex"])
                P.act(tl("ex")[:], tl("ex")[:], AF.Exp, reads=rs("ex"), writes=rs("ex"))
                op("pool", lambda e: e.tensor_copy(tl("gL")[:], tl("ep")[:].rearrange("p h (c l) -> p h c l", c=2)[:, :, :, 63]), ["ep"], ["gL"])
                op("dve", lambda e: e.tensor_tensor(tl("kk")[:], tl("k_")[:], vb("rw_k_k"), ALU.mult), ["k_", "rw_k_k"], ["kk"])
                op("pool", lambda e: e.tensor_tensor(tl("t1")[:], tl("kk")[:], tl("kk")[:], ALU.mult), ["kk"], ["t1"])
                ones_reduce("ones", "t1")
                P.add("dve", lambda e: e.tensor_scalar_max(tl("t1")[:], psC3, 1e-24), reads=[r_psC], writes=rs("t1"))
                P.act(tl("t1")[:], tl("t1")[:], AF.Sqrt, reads=rs("t1"), writes=rs("t1"))
                op("dve", lambda e: e.reciprocal(tl("t1")[:], tl("t1")[:]), ["t1"], ["t1"])
                op("dve", lambda e: e.tensor_tensor(tl("kk")[:], tl("kk")[:], tl("t1")[:], ALU.mult), ["kk", "t1"], ["kk"])
                op("dve", lambda e: e.scalar_tensor_tensor(tl("t2")[:], tl("a_")[:], -1.0, vb("rw_k_a"), ALU.add, ALU.mult), ["a_", "rw_k_a"], ["t2"])
                op("dve", lambda e: e.scalar_tensor_tensor(tl("kt")[:], tl("t2")[:], 1.0, tl("k_")[:], ALU.add, ALU.mult), ["t2", "k_"], ["kt"])
                op("pool", lambda e: e.tensor_tensor(tl("bh")[:], tl("kk")[:], tl("a_")[:], ALU.mult), ["kk", "a_"], ["bh"])
                op("pool", lambda e: e.tensor_tensor(tl("t2")[:], tl("r_")[:], tl("kt")[:], ALU.mult), ["r_", "kt"], ["t2"])
                op("pool", lambda e: e.tensor_tensor(tl("t2")[:], tl("t2")[:], vb("rw_r_k"), ALU.mult), ["t2", "rw_r_k"], ["t2"])
                ones_reduce("ones", "t2")
                P.add("dve", lambda e: e.tensor_tensor(tl("bonus")[:], psC3, tl("v_")[:], ALU.mult), reads=[r_psC] + rs("v_"), writes=rs("bonus"))
                op("pool", lambda e: e.tensor_tensor(tl("RG")[:], tl("r_")[:], tl("ep")[:], ALU.mult), ["r_", "ep"], ["RG"])
                op("dve", lambda e: e.scalar_tensor_tensor(tl("AG")[:], tl("kk")[:], -1.0, tl("ex")[:], ALU.mult, ALU.mult), ["kk", "ex"], ["AG"])
                gLb = tl("gL")[:].unsqueeze(3).to_broadcast([128, HP, 2, 64])
                op("pool", lambda e: e.tensor_tensor(tl("BI")[:], tl("bh")[:], tl("en")[:], ALU.mult), ["bh", "en"], ["BI"])
                op("pool", lambda e: e.tensor_tensor(tl("KI")[:], tl("kt")[:], tl("en")[:], ALU.mult), ["kt", "en"], ["KI"])
                op("dve", lambda e: e.tensor_tensor(tl("Bend")[:].rearrange("p h (c l) -> p h c l", c=2), tl("BI")[:].rearrange("p h (c l) -> p h c l", c=2), gLb, ALU.mult),
                   ["BI", "gL"], ["Bend"])
                op("dve", lambda e: e.tensor_tensor(tl("Kend")[:].rearrange("p h (c l) -> p h c l", c=2), tl("KI")[:].rearrange("p h (c l) -> p h c l", c=2), gLb, ALU.mult),
                   ["KI", "gL"], ["Kend"])
                op("act", lambda e: e.copy(tl("vb16")[:], tl("v_")[:]), ["v_"], ["vb16"])
                for cc in range(2):
                    c_ = slice(cc * 64, (cc + 1) * 64)
                    sfx = str(cc)
                    for k3, (src, dst) in enumerate((("vb16", "Vt"), ("Bend", "BeT"), ("Kend", "KeT"))):
                        pb, r_pb = nextB()
                        for h in range(H):
                            q, hp = hsl(h)
                            P.mm(pb[q, hp, :], tl(src)[q, hp, c_], g.ident_b[q, q], reads=rs(src) + [g.r_ident], writes=[r_pb])
                        if k3 % 2 == 0:
                            P.add("act", lambda e: e.copy(tl(dst + sfx)[:], pb[:]), reads=[r_pb], writes=rs(dst + sfx))
                        else:
                            P.add("dve", lambda e: e.tensor_copy(tl(dst + sfx)[:], pb[:]), reads=[r_pb], writes=rs(dst + sfx))
                    for (lh, rh, dst, msk) in (("BI", "AG", "M", "mS"), ("KI", "AG", "Aak", "mS"), ("BI", "RG", "Arb", "mI"), ("KI", "RG", "Ark", "mI"),
                                              ("AG", "BI", "N", "mL")):
                        pb, r_pb = nextB()
                        for h in range(H):
                            q, hp = hsl(h)
                            P.mm(pb[q, hp, :], tl(lh)[q, hp, c_], tl(rh)[q, hp, c_], reads=rs(lh, rh), writes=[r_pb])
                        P.add("dve", lambda e: e.tensor_tensor(tl(dst + sfx)[:], pb[:], bcm(msk), ALU.mult), reads=[r_pb] + rs(msk), writes=rs(dst + sfx))
                    op("pool", lambda e: e.tensor_tensor(tl("TT" + sfx)[:], tl("M" + sfx)[:], bcm("identS"), ALU.add), ["M" + sfx, "identS"], ["TT" + sfx])
                    cur_m, cur_n = "M" + sfx, "N" + sfx
                    oth_m, oth_n = "M2" + sfx, "N2" + sfx
                    for lvl in range(5):
                        last = (lvl == 4)
                        pn, r_pn = nextB()
                        for h in range(H):
                            q, hp = hsl(h)
                            P.mm(pn[q, hp, :], tl(cur_m)[q, hp, :], tl(cur_n)[q, hp, :], reads=rs(cur_m, cur_n), writes=[r_pn])
                        if not last:
                            pm, r_pm = nextB()
                            for h in range(H):
                                q, hp = hsl(h)
                                P.mm(pm[q, hp, :], tl(cur_n)[q, hp, :], tl(cur_m)[q, hp, :], reads=rs(cur_m, cur_n), writes=[r_pm])
                        P.add("act", lambda e: e.copy(tl(oth_n)[:], pn[:]), reads=[r_pn], writes=rs(oth_n))
                        if not last:
                            P.add("dve", lambda e: e.tensor_copy(tl(oth_m)[:], pm[:]), reads=[r_pm], writes=rs(oth_m))
                        pt_, r_pt = nextB()
                        for h in range(H):
                            q, hp = hsl(h)
                            P.mm(pt_[q, hp, :], tl(oth_n)[q, hp, :], tl("TT" + sfx)[q, hp, :], reads=rs(oth_n, "TT" + sfx), writes=[r_pt])
                        P.add("dve", lambda e: e.tensor_tensor(tl("TT" + sfx)[:], tl("TT" + sfx)[:], pt_[:], ALU.add), reads=[r_pt] + rs("TT" + sfx), writes=rs("TT" + sfx))
                        cur_m, oth_m = oth_m, cur_m
                        cur_n, oth_n = oth_n, cur_n
                for cc in range(2):
                    c_ = slice(cc * 64, (cc + 1) * 64)
                    sfx = str(cc)
                    px, r_px = nextB()
                    for h in range(H):
                        q, hp = hsl(h)
                        P.mm(px[q, hp, :], tl("AG")[q, hp, c_], tl("Pb")[q, hp, :], start=True, stop=False, reads=rs("AG", "Pb"), writes=[r_px])
                        P.mm(px[q, hp, :], tl("Aak" + sfx)[q, hp, :], tl("Vt" + sfx)[q, hp, :], start=False, stop=True, reads=rs("Aak" + sfx, "Vt" + sfx), writes=[r_px])
                    P.add("act", lambda e: e.copy(tl("Xs")[:], px[:]), reads=[r_px], writes=rs("Xs"))
                    pu, r_pu = nextB()
                    for h in range(H):
                        q, hp = hsl(h)
                        P.mm(pu[q, hp, :], tl("TT" + sfx)[q, hp, :], tl("Xs")[q, hp, :], reads=rs("TT" + sfx, "Xs"), writes=[r_pu])
                    P.add("dve", lambda e: e.tensor_copy(tl("Us")[:], pu[:]), reads=[r_pu], writes=rs("Us"))
                    py, r_py = nextB()
                    for h in range(H):
                        q, hp = hsl(h)
                        P.mm(py[q, hp, :], tl("Pb")[q, hp, :], tl("RG")[q, hp, c_], start=True, stop=False, reads=rs("Pb", "RG"), writes=[r_py])
                        P.mm(py[q, hp, :], tl("Us")[q, hp, :], tl("Arb" + sfx)[q, hp, :], start=False, stop=False, reads=rs("Us", "Arb" + sfx), writes=[r_py])
                        P.mm(py[q, hp, :], tl("Vt" + sfx)[q, hp, :], tl("Ark" + sfx)[q, hp, :], start=False, stop=True, reads=rs("Vt" + sfx, "Ark" + sfx), writes=[r_py])
                    P.add("act", lambda e: e.copy(tl("yT")[:, :, c_], py[:]), reads=[r_py], writes=rs("yT"))
                    pp, r_pp = nextB()
                    for h in range(H):
                        q, hp = hsl(h)
                        P.mm(pp[q, hp, :], tl("BeT" + sfx)[q, hp, :], tl("Us")[q, hp, :], start=True, stop=False, reads=rs("BeT" + sfx, "Us"), writes=[r_pp])
                        P.mm(pp[q, hp, :], tl("KeT" + sfx)[q, hp, :], tl("Vt" + sfx)[q, hp, :], start=False, stop=True, reads=rs("KeT" + sfx, "Vt" + sfx), writes=[r_pp])
                    op("pool", lambda e: e.tensor_tensor(tl("Pt")[:], tl("P")[:], tl("gL")[:, :, cc:cc + 1].to_broadcast([128, HP, 64]), ALU.mult), ["P", "gL"], ["Pt"])
                    P.add("dve", lambda e: e.tensor_tensor(tl("P")[:], tl("Pt")[:], pp[:], ALU.add), reads=[r_pp] + rs("Pt"), writes=rs("P"))
                    op("act", lambda e: e.copy(tl("Pb")[:], tl("P")[:]), ["P"], ["Pb"])
                ones_reduce("onesm", "yT")
                P.add("dve", lambda e: e.tensor_tensor(tl("yT")[:], tl("yT")[:], psC3, ALU.subtract), reads=[r_psC] + rs("yT"), writes=rs("yT"))
                op("pool", lambda e: e.tensor_tensor(tl("t1")[:], tl("yT")[:], tl("yT")[:], ALU.mult), ["yT"], ["t1"])
                ones_reduce("onesm", "t1")
                P.act(tl("t1")[:], psC3, AF.Sqrt, reads=[r_psC], writes=rs("t1"), bias=64e-5)
                op("dve", lambda e: e.reciprocal(tl("t1")[:], tl("t1")[:]), ["t1"], ["t1"])
                op("dve", lambda e: e.tensor_tensor(tl("yT")[:], tl("yT")[:], tl("t1")[:], ALU.mult), ["yT", "t1"], ["yT"])
                op("pool", lambda e: e.tensor_tensor(tl("yT")[:], tl("yT")[:], vb("rw_gn_g"), ALU.mult), ["yT", "rw_gn_g"], ["yT"])
                op("pool", lambda e: e.tensor_tensor(tl("yT")[:], tl("yT")[:], vb("rw_gn_b"), ALU.add), ["yT", "rw_gn_b"], ["yT"])
                op("dve", lambda e: e.tensor_tensor(tl("yT")[:], tl("yT")[:], tl("bonus")[:], ALU.add), ["yT", "bonus"], ["yT"])
                op("dve", lambda e: e.tensor_tensor(tl("ob")[:], tl("yT")[:], tl("g_")[:], ALU.mult), ["yT", "g_"], ["ob"])
                P.dma(A["oT_s"][1024 + hg * 512:1024 + (hg + 1) * 512, ts_].rearrange("(hp p) t -> p hp t", p=128), tl("ob")[:],
                      reads=rs("ob"), writes=[R["oT_s"]])


N_CORES = 4
_NC_CACHE = {}


def _core_inputs(inp, b):
    m = {"x": np.ascontiguousarray(inp["x"][b], dtype=np.float32)}
    for k, v in inp.items():
        if k == "x":
            continue
        v = np.asarray(v, dtype=np.float32)
        if k.startswith("ffn_") or k.startswith("ln_"):
            m[k] = np.ascontiguousarray(v)
        else:
            v0 = v[0]
            if k == "rw_r_k":
                v0 = v0.reshape(-1)
            m[k] = np.ascontiguousarray(v0)
    return m


def kernel(**inputs):
    if "nc" not in _NC_CACHE:
        _NC_CACHE["nc"] = build_program()
    nc = _NC_CACHE["nc"]
    in_maps = [_core_inputs(inputs, b) for b in range(N_CORES)]
    res = run_bass_kernel_spmd(nc, in_maps, core_ids=list(range(N_CORES)))
    out = np.stack([np.asarray(res.results[b]["y"], dtype=np.float32) for b in range(N_CORES)], axis=0)
    return out
```

```python
import contextlib
import numpy as np
import concourse.bass as bass
import concourse.mybir as mybir
from concourse.bass_utils import run_bass_kernel_spmd

F32 = mybir.dt.float32
BF16 = mybir.dt.bfloat16
AF = mybir.ActivationFunctionType
ALU = mybir.AluOpType
AX = mybir.AxisListType

ENGS = ("pe", "act", "dve", "pool", "sp")
SAME_ENG_SYNC = True
NSLOT = 8
SKEW = True


class Res:
    __slots__ = ("name", "lw", "rd")

    def __init__(self, name=""):
        self.name = name
        self.lw = None
        self.rd = []


class Op:
    __slots__ = ("eng", "fn", "reads", "writes", "dma", "deps", "need", "sig", "waits", "idx")

    def __init__(self, eng, fn, reads, writes, dma):
        self.eng = eng
        self.fn = fn
        self.reads = reads
        self.writes = writes
        self.dma = dma
        self.deps = []
        self.need = False
        self.sig = None
        self.waits = []


class _Rec:
    def __init__(self):
        self.call = None

    def __getattr__(self, name):
        def f(*a, **k):
            assert self.call is None
            self.call = (name, a, k)
            return self
        return f

    def replay(self, e):
        name, a, k = self.call
        return getattr(e, name)(*a, **k)


class Prog:
    def __init__(self, nc, stack):
        self.nc = nc
        self.ops = []
        self.sems = {e: stack.enter_context(nc.semaphore("s_" + e)) for e in ENGS}
        self.dsems = {e: [stack.enter_context(nc.semaphore("d_%s%d" % (e, i))) for i in range(NSLOT)]
                      for e in ("sp", "act", "pool")}
        self.sigcount = {e: 0 for e in ENGS}
        self.ndma = {e: 0 for e in ("sp", "act", "pool")}
        self.waited = {e: {} for e in ENGS}
        self.live = set()
        self.nops = 0

    def add(self, eng, fn, reads=(), writes=(), dma=False):
        if fn is not None:
            rec = _Rec()
            fn(rec)
            fn = rec.replay
        op = Op(eng, fn, tuple(reads), tuple(writes), dma)
        deps = []
        for r in op.reads:
            if r.lw is not None:
                deps.append(r.lw)
        for w in op.writes:
            if w.lw is not None:
                deps.append(w.lw)
            deps.extend(w.rd)
        seen = set()
        for d in deps:
            if id(d) in seen or d is op:
                continue
            seen.add(id(d))
            if (not d.dma) and d.eng == eng and not dma:
                if eng == "pe" or not SAME_ENG_SYNC:
                    continue
            op.deps.append(d)
            d.need = True
        for r in op.reads:
            r.rd.append(op)
            self.live.add(r)
        for w in op.writes:
            w.lw = op
            w.rd = []
            self.live.add(w)
        self.ops.append(op)
        return op

    def flush(self):
        nc = self.nc
        ops = self.ops
        self.ops = []
        if not ops:
            return
        for r in self.live:
            if r.lw is not None:
                r.lw.need = True
            for o in r.rd:
                o.need = True
        for op in ops:
            eng = op.eng
            waits = []
            if op.dma:
                i = self.ndma[eng]
                self.ndma[eng] = i + 1
                sem = self.dsems[eng][i % NSLOT]
                val = 16 * (i // NSLOT + 1)
                if val > 16:
                    if self.waited[eng].get(sem, 0) < val - 16:
                        waits.append((sem, val - 16))
                        self.waited[eng][sem] = val - 16
                op.sig = (sem, val)
            elif op.need and op.fn is not None:
                self.sigcount[eng] += 1
                op.sig = (self.sems[eng], self.sigcount[eng])
            for d in op.deps:
                if d.sig is None:
                    continue
                sem, val = d.sig
                if self.waited[eng].get(sem, 0) >= val:
                    continue
                self.waited[eng][sem] = val
                waits.append((sem, val))
            op.waits = waits
        per = {e: [o for o in ops if o.eng == e] for e in ENGS}
        self.nops += len(ops)

        def emit(e, lst):
            for op in lst:
                for sem, val in op.waits:
                    e.wait_ge(sem, val)
                if op.fn is None:
                    continue
                ins = op.fn(e)
                if op.sig is not None:
                    ins.then_inc(op.sig[0], 16 if op.dma else 1)

        with nc.Block() as block:
            @block.tensor
            def _(e):
                emit(e, per["pe"])

            @block.scalar
            def _(e):
                emit(e, per["act"])

            @block.vector
            def _(e):
                emit(e, per["dve"])

            @block.gpsimd
            def _(e):
                emit(e, per["pool"])

            @block.sync
            def _(e):
                emit(e, per["sp"])
        for op in ops:
            op.fn = None

    def dma(self, out, in_, reads=(), writes=(), q="sp"):
        return self.add(q, lambda e: e.dma_start(out=out, in_=in_), reads, writes, dma=True)

    def mm(self, out, lhsT, rhs, start=True, stop=True, reads=(), writes=()):
        return self.add("pe", lambda e: e.matmul(out, lhsT, rhs, start=start, stop=stop), reads, writes)

    def tr(self, out, in_, ident, reads=(), writes=()):
        return self.add("pe", lambda e: e.transpose(out, in_, ident), reads, writes)

    def act(self, out, in_, func, reads=(), writes=(), **kw):
        return self.add("act", lambda e: e.activation(out, in_, func, **kw), reads, writes)

    def wait_all(self, eng, ress):
        return self.add(eng, None, reads=ress)


S = 2048
D = 2048
DFF = 5632
ALPHA = 4.0 ** 0.25
LN_EPS = 1e-5
EXPM05 = float(np.exp(-0.5))


class Ctx:
    pass


DBG_RW = ["r_", "k_", "v_", "lw", "cl", "a_", "g_", "kk", "kt", "bh", "bonus", "RG", "AG", "BI", "KI", "Bend", "Kend", "yT"]
DBG_RW2 = ["Vt0", "BeT0", "KeT0", "M0", "N0", "Aak0", "Arb0", "Ark0", "TT0", "Vt1", "TT1", "P", "Xs", "Us"]


def build_program(phases=None, kinds=None):
    kinds = kinds or {}
    nc = bass.Bass("TRN2", target_bir_lowering=False)
    g = Ctx()
    g.nc = nc

    def dram(name, shape, dt, kind="Internal"):
        return nc.dram_tensor(name, list(shape), dt, kind=kinds.get(name, kind)).ap()

    I = "ExternalInput"
    A = {}
    A["x"] = dram("x", [S, D], F32, I)
    A["even_w_in"] = dram("even_w_in", [D, 6432], F32, I)
    A["even_shift_mu"] = dram("even_shift_mu", [3360], F32, I)
    for n in ("rw_w0", "rw_a0", "rw_k_k", "rw_k_a", "rw_r_k", "rw_gn_g", "rw_gn_b", "gla_gate_b"):
        A[n] = dram(n, [1024], F32, I)
    A["rw_w2"] = dram("rw_w2", [64, 1024], F32, I)
    A["rw_a2"] = dram("rw_a2", [64, 1024], F32, I)
    A["rw_g2"] = dram("rw_g2", [160, 1024], F32, I)
    A["even_w_out"] = dram("even_w_out", [D, D], F32, I)
    A["odd_w_in"] = dram("odd_w_in", [D, 6160], F32, I)
    A["gla_gate_w2"] = dram("gla_gate_w2", [16, 1024], F32, I)
    for n in ("gla_r_b", "gla_gn_g", "gla_gn_b"):
        A[n] = dram(n, [2048], F32, I)
    A["odd_w_out"] = dram("odd_w_out", [D, D], F32, I)
    for n in ("ln_mix_g", "ln_mix_b", "ln_ffn_g", "ln_ffn_b"):
        A[n] = dram(n, [2, D], F32, I)
    A["ffn_w_up"] = dram("ffn_w_up", [2, D, 2 * DFF], F32, I)
    A["ffn_conv_w"] = dram("ffn_conv_w", [2, 3, DFF], F32, I)
    A["ffn_conv_b"] = dram("ffn_conv_b", [2, DFF], F32, I)
    A["ffn_w_down"] = dram("ffn_w_down", [2, DFF, D], F32, I)
    A["y"] = dram("y", [S, D], F32, "ExternalOutput")
    A["qT_s"] = dram("qT_s", [1024, S], BF16)
    A["kT_s"] = dram("kT_s", [1024, S], BF16)
    A["v_s"] = dram("v_s", [S, 1024], BF16)
    A["rwT_s"] = dram("rwT_s", [3360, S], F32)
    A["oT_s"] = dram("oT_s", [D, S], BF16)
    A["h_a"] = dram("h_a", [S, D], F32)
    A["h_b"] = dram("h_b", [S, D], F32)
    A["hffT_s"] = dram("hffT_s", [4, 128, 44, 512], BF16)
    A["qk1_s"] = dram("qk1_s", [2048, S], F32)
    A["v1_s"] = dram("v1_s", [S, 2048], BF16)
    A["dg1_s"] = dram("dg1_s", [16, S], F32)
    A["r1_s"] = dram("r1_s", [S, 2048], F32)
    A["wdn_b"] = dram("wdn_b", [2, 4, 128, 44, 512], BF16)
    if "dbg_rw" in kinds:
        A["dbg_rw"] = dram("dbg_rw", [len(DBG_RW), 64, 8 * 128], F32)
        A["dbg_rw2"] = dram("dbg_rw2", [len(DBG_RW2), 64, 8 * 64], F32)
    g.A = A
    g.R = {n: Res(n) for n in A}

    with contextlib.ExitStack() as st:
        P = Prog(nc, st)
        g.P = P
        g.ident_f = st.enter_context(nc.sbuf_tensor("ident_f", [128, 128], F32))
        g.ident_b = st.enter_context(nc.sbuf_tensor("ident_b", [128, 128], BF16))
        g.r_ident = Res("ident")
        P.add("pool", lambda e: e.memset(g.ident_f[:], 1.0), writes=[g.r_ident])
        P.add("pool", lambda e: e.affine_select(g.ident_f[:], g.ident_f[:], [[-1, 128]], ALU.is_equal, 0.0,
                                                base=0, channel_multiplier=1), reads=[g.r_ident], writes=[g.r_ident])
        P.add("pool", lambda e: e.tensor_copy(g.ident_b[:], g.ident_f[:]), reads=[g.r_ident], writes=[g.r_ident])
        P.flush()
        allp = ["l0_inproj", "sb_attn", "rwkv", "l0_mixout", "l0_ffn", "l1_inproj", "gla", "l1_mixout", "l1_ffn"]
        ph = allp if phases is None else phases
        if "l0_inproj" in ph:
            phase_l0_inproj(g)
        if "sb_attn" in ph:
            phase_sb_attn(g)
        if "rwkv" in ph:
            phase_rwkv(g)
        if "l0_mixout" in ph:
            phase_proj_ln(g, "oT_s", 16, A["even_w_out"], "x", A["ln_mix_g"][0], A["ln_mix_b"][0], "h_a")
        if "l0_ffn" in ph:
            phase_ffn_up(g, 0, "h_a")
            phase_proj_ln(g, "hffT_s", 44, A["ffn_w_down"][0], "h_a", A["ln_ffn_g"][0], A["ln_ffn_b"][0], "h_b", wb=(A["wdn_b"][0], g.R["wdn_b"]))
        if "l1_inproj" in ph:
            phase_l1_inproj(g)
        if "gla" in ph:
            phase_gla(g)
        if "l1_mixout" in ph:
            phase_proj_ln(g, "oT_s", 16, A["odd_w_out"], "h_b", A["ln_mix_g"][1], A["ln_mix_b"][1], "h_a")
        if "l1_ffn" in ph:
            phase_ffn_up(g, 1, "h_a")
            phase_proj_ln(g, "hffT_s", 44, A["ffn_w_down"][1], "h_a", A["ln_ffn_g"][1], A["ln_ffn_b"][1], "y", wb=(A["wdn_b"][1], g.R["wdn_b"]))
        P.wait_all("sp", list(g.R.values()))
        P.flush()
    return nc


_UID = [0]


class Phase:
    def __init__(self, g):
        self.g = g
        self.st = contextlib.ExitStack()
        self.n = 0

    def __enter__(self):
        self.st.__enter__()
        return self

    def __exit__(self, *a):
        self.g.P.flush()
        return self.st.__exit__(*a)

    def sb(self, shape, dt, name=None):
        _UID[0] += 1
        t = self.st.enter_context(self.g.nc.sbuf_tensor("%s_%d" % (name or "t", _UID[0]), list(shape), dt))
        return t

    def ps(self, shape, dt, name=None):
        _UID[0] += 1
        t = self.st.enter_context(self.g.nc.psum_tensor("%s_%d" % (name or "p", _UID[0]), list(shape), dt))
        return t


def make_xT(g, ph, src_name, xT, xT_res):
    P, A, R = g.P, g.A, g.R
    src = A[src_name]
    xin = [ph.sb([128, D], BF16, "xin") for _ in range(2)]
    r_xin = [Res("xin0"), Res("xin1")]
    ptr = [ph.ps([128, 8, 128], BF16, "ptr") for _ in range(2)]
    r_ptr = [Res("ptr0"), Res("ptr1")]
    k = 0
    for tt in range(16):
        b = tt % 2
        P.dma(xin[b][:], src[tt * 128:(tt + 1) * 128, :], reads=[R[src_name]], writes=[r_xin[b]], q="pool")
        for gi in range(2):
            pb = k % 2
            k += 1
            for j in range(8):
                kc = gi * 8 + j
                P.tr(ptr[pb][:, j, :], xin[b][:, kc * 128:(kc + 1) * 128], g.ident_b[:],
                     reads=[r_xin[b], g.r_ident], writes=[r_ptr[pb]])
            dst = xT[:, gi * 8:(gi + 1) * 8, tt * 128:(tt + 1) * 128]
            if pb == 0:
                P.add("dve", lambda e, dst=dst, s=ptr[pb]: e.tensor_copy(dst, s[:]), reads=[r_ptr[pb]], writes=[xT_res[tt // 4]])
            else:
                P.add("act", lambda e, dst=dst, s=ptr[pb]: e.copy(dst, s[:]), reads=[r_ptr[pb]], writes=[xT_res[tt // 4]])


def load_vec_fm(g, ph, vec, n, dst, dst_res, pt, pt_res, parts=128):
    P = g.P
    nfull = n // parts
    rem = n - nfull * parts
    nch = nfull + (1 if rem else 0)
    tmp = ph.sb([128, parts], F32, "vtmp")
    r_tmp = Res("vtmp")
    P.add("pool", lambda e: e.memset(tmp[:], 0.0), writes=[r_tmp])
    if nfull:
        P.dma(tmp[0:nfull, :], vec[0:nfull * parts].rearrange("(c p) -> c p", p=parts), writes=[r_tmp])
    if rem:
        P.dma(tmp[nfull:nfull + 1, 0:rem], vec[nfull * parts:n].rearrange("(c p) -> c p", c=1), writes=[r_tmp])
    P.tr(pt[0:parts, 0:nch], tmp[0:nch, 0:parts], g.ident_f[0:nch, 0:nch], reads=[r_tmp, g.r_ident], writes=[pt_res])
    P.add("dve", lambda e: e.tensor_copy(dst, pt[0:parts, 0:nch]), reads=[pt_res], writes=[dst_res])
    return nch


def phase_l0_inproj(g):
    P, A, R, nc = g.P, g.A, g.R, g.nc
    with Phase(g) as ph:
        xT = ph.sb([128, 16, S], BF16, "xT")
        xT_res = [Res("xT%d" % i) for i in range(4)]
        with Phase(g) as ph2:
            make_xT(g, ph2, "x", xT, xT_res)
        w = [ph.sb([128, 16, 512], BF16, "w") for _ in range(2)]
        r_w = [Res("w0"), Res("w1")]
        pacc = [ph.ps([128, 512], F32, "pacc") for _ in range(4)]
        r_pacc = [Res("pacc%d" % i) for i in range(4)]
        mu = ph.sb([128, 27], F32, "mu")
        omu = ph.sb([128, 27], F32, "omu")
        r_mu = Res("mu")
        ptv = ph.ps([128, 128], F32, "ptv")
        r_ptv = Res("ptv")
        load_vec_fm(g, ph, A["even_shift_mu"], 3360, mu[:, 0:27], r_mu, ptv, r_ptv)
        P.add("dve", lambda e: e.tensor_scalar(omu[:], mu[:], -1.0, 1.0, ALU.mult, ALU.add), reads=[r_mu], writes=[r_mu])
        stage_b = [ph.sb([128, S], BF16, "stb") for _ in range(2)]
        r_stb = [Res("stb0"), Res("stb1")]
        stage_v = [ph.sb([128, 512], BF16, "stv") for _ in range(2)]
        r_stv = [Res("stv0"), Res("stv1")]
        ubuf = [ph.sb([128, S + 1], F32, "ubuf") for _ in range(2)]
        r_ub = [Res("ub0"), Res("ub1")]
        shf = [ph.sb([128, S], F32, "shf") for _ in range(2)]
        r_shf = [Res("shf0"), Res("shf1")]
        for b in range(2):
            P.add("pool", lambda e, b=b: e.memset(ubuf[b][:, 0:1], 0.0), writes=[r_ub[b]])
        W = A["even_w_in"]
        ngroups = (6432 + 511) // 512
        cnt = {"pacc": 0, "chunk": 0, "v": 0}

        def load_w(gi):
            c0 = gi * 512
            gw = min(512, 6432 - c0)
            b = gi % 2
            P.dma(w[b][:, :, 0:gw], W[:, c0:c0 + gw].rearrange("(kc p) c -> p kc c", p=128), writes=[r_w[b]], q="pool")

        load_w(0)
        for gi in range(ngroups):
            c0 = gi * 512
            gw = min(512, 6432 - c0)
            b = gi % 2
            if gi + 1 < ngroups:
                load_w(gi + 1)
            if 2048 <= c0 < 3072:
                for tt in range(16):
                    pi = cnt["pacc"] % 4
                    cnt["pacc"] += 1
                    for kc in range(16):
                        P.mm(pacc[pi][:, :], xT[:, kc, tt * 128:(tt + 1) * 128], w[b][:, kc, 0:512], start=(kc == 0), stop=(kc == 15),
                             reads=[xT_res[tt // 4], r_w[b]], writes=[r_pacc[pi]])
                    sv = cnt["v"] % 2
                    cnt["v"] += 1
                    P.add("act", lambda e, sv=sv, pi=pi: e.copy(stage_v[sv][:], pacc[pi][:]), reads=[r_pacc[pi]], writes=[r_stv[sv]])
                    P.dma(A["v_s"][tt * 128:(tt + 1) * 128, c0 - 2048:c0 - 2048 + 512], stage_v[sv][:], reads=[r_stv[sv]], writes=[R["v_s"]])
                continue
            nchunk = (gw + 127) // 128
            for j in range(nchunk):
                cw = min(128, gw - j * 128)
                col = c0 + j * 128
                ci = cnt["chunk"] % 2
                cnt["chunk"] += 1
                for qt in range(4):
                    pi = cnt["pacc"] % 4
                    cnt["pacc"] += 1
                    for kc in range(16):
                        P.mm(pacc[pi][0:cw, :], w[b][:, kc, j * 128:j * 128 + cw], xT[:, kc, qt * 512:(qt + 1) * 512], start=(kc == 0), stop=(kc == 15),
                             reads=[xT_res[qt], r_w[b]], writes=[r_pacc[pi]])
                    if col < 2048:
                        sc = (128.0 ** -0.5) if col < 1024 else 1.0
                        P.add("act", lambda e, ci=ci, pi=pi, qt=qt, sc=sc: e.activation(stage_b[ci][:, qt * 512:(qt + 1) * 512], pacc[pi][:], AF.Copy, scale=sc),
                              reads=[r_pacc[pi]], writes=[r_stb[ci]])
                    else:
                        P.add("act", lambda e, ci=ci, pi=pi, qt=qt, cw=cw: e.copy(ubuf[ci][0:cw, 1 + qt * 512:1 + (qt + 1) * 512], pacc[pi][0:cw, :]),
                              reads=[r_pacc[pi]], writes=[r_ub[ci]])
                if col < 2048:
                    dst = A["qT_s"] if col < 1024 else A["kT_s"]
                    dn = "qT_s" if col < 1024 else "kT_s"
                    r0 = col % 1024
                    P.dma(dst[r0:r0 + 128, :], stage_b[ci][:], reads=[r_stb[ci]], writes=[R[dn]])
                else:
                    rc = (col - 3072) // 128
                    P.add("dve", lambda e, ci=ci, rc=rc, cw=cw: e.tensor_scalar(shf[ci][0:cw, :], ubuf[ci][0:cw, 0:S], mu[0:cw, rc:rc + 1], None, ALU.mult),
                          reads=[r_ub[ci], r_mu], writes=[r_shf[ci]])
                    P.add("dve", lambda e, ci=ci, rc=rc, cw=cw: e.scalar_tensor_tensor(shf[ci][0:cw, :], ubuf[ci][0:cw, 1:S + 1], omu[0:cw, rc:rc + 1], shf[ci][0:cw, :], ALU.mult, ALU.add),
                          reads=[r_ub[ci], r_mu, r_shf[ci]], writes=[r_shf[ci]])
                    P.dma(A["rwT_s"][col - 3072:col - 3072 + cw, :], shf[ci][0:cw, :], reads=[r_shf[ci]], writes=[R["rwT_s"]])


def phase_sb_attn(g, NH=8):
    P, A, R, nc = g.P, g.A, g.R, g.nc
    NA = 3
    with Phase(g) as ph:
        qT = [ph.sb([128, S], BF16, "qT") for _ in range(2)]
        kT = [ph.sb([128, S], BF16, "kT") for _ in range(2)]
        vv = [ph.sb([128, 16, 128], BF16, "vv") for _ in range(2)]
        r_in = [Res("sbin0"), Res("sbin1")]
        tri = ph.sb([128, 128], F32, "tri")
        ones = ph.sb([128, 128], F32, "ones")
        r_c = Res("sbconst")
        P.add("pool", lambda e: e.memset(tri[:], 1.0), writes=[r_c])
        P.add("pool", lambda e: e.memset(ones[:], 1.0), writes=[r_c])
        P.add("pool", lambda e: e.affine_select(tri[:], tri[:], [[-1, 128]], ALU.is_ge, 0.0, base=0, channel_multiplier=1),
              reads=[r_c], writes=[r_c])
        pz = [ph.ps([128, 512], F32, "pz") for _ in range(NA)]
        r_pz = [Res("pz%d" % i) for i in range(NA)]
        zs = [ph.sb([128, 512], F32, "zs") for _ in range(NA)]
        r_zs = [Res("zs%d" % i) for i in range(NA)]
        sp = [ph.sb([128, 512], F32, "sp") for _ in range(NA)]
        r_sp = [Res("sp%d" % i) for i in range(NA)]
        pt = [ph.ps([128, 512], F32, "pt") for _ in range(2)]
        r_pt = [Res("pt0"), Res("pt1")]
        po = [ph.ps([128, 512], F32, "po") for _ in range(2)]
        r_po = [Res("po0"), Res("po1")]
        sacc = [ph.sb([128, 512], F32, "sacc") for _ in range(2)]
        r_sacc = [Res("sacc0"), Res("sacc1")]
        ee = [ph.sb([128, 512], F32, "ee") for _ in range(2)]
        r_ee = [Res("ee0"), Res("ee1")]
        ww = [ph.sb([128, 512], BF16, "ww") for _ in range(3)]
        r_ww = [Res("ww0"), Res("ww1"), Res("ww2")]
        osb = [ph.sb([128, S], BF16, "osb") for _ in range(2)]
        r_osb = [Res("osb0"), Res("osb1")]

        def load(h):
            b = h % 2
            P.dma(qT[b][:], A["qT_s"][h * 128:(h + 1) * 128, :], reads=[R["qT_s"]], writes=[r_in[b]])
            P.dma(kT[b][:], A["kT_s"][h * 128:(h + 1) * 128, :], reads=[R["kT_s"]], writes=[r_in[b]])
            P.dma(vv[b][:], A["v_s"][:, h * 128:(h + 1) * 128].rearrange("(t p) d -> p t d", p=128), reads=[R["v_s"]], writes=[r_in[b]])

        steps = []
        gq = 0
        for h in range(NH):
            for qt in range(4):
                kbs = list(range(4 * qt + 3, -1, -1))
                for ki, kb in enumerate(kbs):
                    steps.append((h, qt, ki, kb, len(kbs), gq))
                gq += 1

        def stageA(i):
            h, qt, ki, kb, nk, gq = steps[i]
            a = i % NA
            b = h % 2
            t0, s0 = qt * 512, kb * 128
            if i == 0:
                load(0)
            P.mm(pz[a][:], kT[b][:, s0:s0 + 128], qT[b][:, t0:t0 + 512], reads=[r_in[b]], writes=[r_pz[a]])
            P.act(zs[a][:], pz[a][:], AF.Exp, reads=[r_pz[a]], writes=[r_zs[a]])
            P.act(sp[a][:], zs[a][:], AF.Ln, reads=[r_zs[a]], writes=[r_sp[a]], bias=1.0)
            if kb >= 4 * qt:
                P.add("pool", lambda e: e.affine_select(sp[a][:], sp[a][:], [[1, 512]], ALU.is_gt, 0.0, base=t0 - s0, channel_multiplier=-1),
                      reads=[r_sp[a]], writes=[r_sp[a]])

        def stageB(i):
            h, qt, ki, kb, nk, gq = steps[i]
            a = i % NA
            i2 = i % 2
            b = h % 2
            t0, s0 = qt * 512, kb * 128
            ob = gq % 2
            sa, r_sa = sacc[gq % 2], r_sacc[gq % 2]
            P.mm(pt[i2][:], tri[:], sp[a][:], start=True, stop=(ki == 0), reads=[r_sp[a], r_c], writes=[r_pt[i2]])
            if ki > 0:
                P.mm(pt[i2][:], ones[:], sa[:], start=False, stop=True, reads=[r_sa, r_c], writes=[r_pt[i2]])
            P.act(ee[i2][:], pt[i2][:], AF.Exp, reads=[r_pt[i2]], writes=[r_ee[i2]], scale=-1.0)
            i3 = i % 3
            P.add("dve", lambda e: e.tensor_tensor(ww[i3][:], ee[i2][:], zs[a][:], ALU.mult), reads=[r_ee[i2], r_zs[a]], writes=[r_ww[i3]])
            if kb >= 4 * qt:
                P.add("pool", lambda e: e.affine_select(ww[i3][:], ww[i3][:], [[1, 512]], ALU.is_gt, 0.0, base=t0 - s0, channel_multiplier=-1),
                      reads=[r_ww[i3]], writes=[r_ww[i3]])
            if ki == 0:
                P.add("pool", lambda e: e.tensor_copy(sa[:], sp[a][:]), reads=[r_sp[a]], writes=[r_sa])
            elif ki + 1 < nk:
                P.add("pool", lambda e: e.tensor_tensor(sa[:], sa[:], sp[a][:], ALU.add), reads=[r_sp[a], r_sa], writes=[r_sa])

        def stageC(i):
            h, qt, ki, kb, nk, gq = steps[i]
            i3 = i % 3
            b = h % 2
            t0 = qt * 512
            ob = gq % 2
            P.mm(po[ob][:], vv[b][:, kb, :], ww[i3][:], start=(ki == 0), stop=(ki == nk - 1), reads=[r_ww[i3], r_in[b]], writes=[r_po[ob]])
            if ki == nk - 1:
                P.add("act", lambda e: e.copy(osb[b][:, t0:t0 + 512], po[ob][:]), reads=[r_po[ob]], writes=[r_osb[b]])
                if qt == 3:
                    P.dma(A["oT_s"][h * 128:(h + 1) * 128, :], osb[b][:], reads=[r_osb[b]], writes=[R["oT_s"]])

        for i in range(len(steps) + 2):
            if i < len(steps):
                stageA(i)
            if 1 <= i <= len(steps):
                stageB(i - 1)
            if i >= 2:
                stageC(i - 2)
            j = i - 1
            if 0 <= j < len(steps) and steps[j][1] == 0 and steps[j][2] == 0 and steps[j][0] + 1 < NH:
                load(steps[j][0] + 1)


def bcast_vec(g, ph, vec, n, name):
    t = ph.sb([128, n], F32, name)
    r = Res(name)
    g.P.dma(t[:], vec.partition_broadcast(128), writes=[r])
    return t, r


def layer_norm_tile(g, pre, r_pre, gam, bet, r_gb, out, r_out, stats, mv, rstd, r_tmp, negh, n=D, eps=LN_EPS):
    P = g.P
    nch = n // 512
    for c in range(nch):
        P.add("dve", lambda e, c=c: e.bn_stats(stats[:, c, :], pre[:, c * 512:(c + 1) * 512]), reads=[r_pre], writes=[r_tmp])
    P.add("dve", lambda e: e.bn_aggr(mv[:], stats[:, 0:nch, :]), reads=[r_tmp], writes=[r_tmp])
    P.add("pool", lambda e: e.tensor_scalar(rstd[:], mv[:, 1:2], eps, None, ALU.add), reads=[r_tmp], writes=[r_tmp])
    P.add("pool", lambda e: e.tensor_tensor(rstd[:], rstd[:], negh, ALU.pow), reads=[r_tmp, r_gb], writes=[r_tmp])
    P.add("dve", lambda e: e.tensor_scalar(out, pre, mv[:, 0:1], rstd[:, 0:1], ALU.subtract, ALU.mult), reads=[r_pre, r_tmp], writes=[r_out])
    P.add("dve", lambda e: e.tensor_tensor(out, out, gam, ALU.mult), reads=[r_gb, r_out], writes=[r_out])
    P.add("pool", lambda e: e.tensor_tensor(out, out, bet, ALU.add), reads=[r_gb, r_out], writes=[r_out])


def phase_proj_ln(g, a_name, KC, Wd, resid_name, ln_g, ln_b, out_name, wb=None):
    P, A, R, nc = g.P, g.A, g.R, g.nc
    CG = 512
    ncg = D // CG
    resident = KC <= 16
    with Phase(g) as ph:
        naT = 2 if resident else 1
        aT = [ph.sb([128, KC, 512], BF16, "aT") for _ in range(naT)]
        r_aT = [Res("aT%d" % i) for i in range(naT)]
        nw = ncg if resident else 2
        w = [ph.sb([128, KC, CG], BF16, "wp") for _ in range(nw)]
        r_w = [Res("wp%d" % i) for i in range(nw)]
        npre = 2 if resident else 1
        pre2 = [ph.sb([128, 4, D], F32, "pre") for _ in range(npre)]
        r_pre2 = [[Res("pre%d_%d" % (j, i)) for i in range(4)] for j in range(npre)]
        gam, r_g = bcast_vec(g, ph, ln_g, D, "gam")
        bet, r_b = bcast_vec(g, ph, ln_b, D, "bet")
        r_gb = Res("gb")
        negh = ph.sb([128, 1], F32, "negh")
        P.add("pool", lambda e: e.memset(negh[:], -0.5), reads=[r_g, r_b], writes=[r_gb])
        P.add("dve", None, reads=[r_g, r_b, r_gb])
        stats = [ph.sb([128, 4, 6], F32, "stats") for _ in range(4)]
        mv = [ph.sb([128, 2], F32, "mv") for _ in range(4)]
        rstd = [ph.sb([128, 1], F32, "rstd") for _ in range(4)]
        r_tmp = [Res("lntmp%d" % i) for i in range(4)]
        pacc = [ph.ps([128, 512], F32, "pp") for _ in range(4)]
        r_pacc = [Res("pp%d" % i) for i in range(4)]
        cnt = 0

        def load_w(i):
            cg = i % ncg
            b = cg if resident else i % 2
            if wb is None:
                P.dma(w[b][:], Wd[:, cg * CG:(cg + 1) * CG].rearrange("(kc p) c -> p kc c", p=128), writes=[r_w[b]], q="pool")
            else:
                P.dma(w[b][:], wb[0][cg], reads=[wb[1]], writes=[r_w[b]], q="act")

        def load_a(tg):
            if a_name == "hffT_s":
                P.dma(aT[tg % naT][:], A[a_name][tg], reads=[R[a_name]], writes=[r_aT[tg % naT]])
            else:
                P.dma(aT[tg % naT][:], A[a_name][:, tg * 512:(tg + 1) * 512].rearrange("(kc p) t -> p kc t", p=128), reads=[R[a_name]], writes=[r_aT[tg % naT]])

        total = 4 * ncg
        if resident:
            for i in range(ncg):
                load_w(i)
        else:
            load_w(0)
        load_a(0)
        for tg in range(4):
            ab = tg % naT
            pre = pre2[tg % npre]
            r_pre = r_pre2[tg % npre]
            if naT > 1 and tg + 1 < 4:
                load_a(tg + 1)
            for cg in range(ncg):
                i = tg * ncg + cg
                b = cg if resident else i % 2
                if not resident and i + 1 < total:
                    load_w(i + 1)
                for tt in range(4):
                    pi = cnt % 4
                    cnt += 1
                    for kc in range(KC):
                        P.mm(pacc[pi][:, 0:CG], aT[ab][:, kc, tt * 128:(tt + 1) * 128], w[b][:, kc, :], start=(kc == 0), stop=(kc == KC - 1),
                             reads=[r_aT[ab], r_w[b]], writes=[r_pacc[pi]])
                    P.add("act", lambda e: e.activation(pre[:, tt, cg * CG:(cg + 1) * CG], pacc[pi][:, 0:CG], AF.Copy, scale=1.0 / ALPHA),
                          reads=[r_pacc[pi]], writes=[r_pre[tt]])
            if naT == 1 and tg + 1 < 4:
                load_a(tg + 1)
            for tt in range(4):
                tok = tg * 4 + tt
                P.add("pool", lambda e: e.dma_start(out=pre[:, tt, :], in_=A[resid_name][tok * 128:(tok + 1) * 128, :], accum_op=ALU.add),
                      reads=[R[resid_name]], writes=[r_pre[tt]], dma=True)
            for tt in range(4):
                tok = tg * 4 + tt
                layer_norm_tile(g, pre[:, tt, :], r_pre[tt], gam[:], bet[:], r_gb, pre[:, tt, :], r_pre[tt], stats[tt], mv[tt], rstd[tt], r_tmp[tt], negh[:],
                                eps=LN_EPS / (ALPHA * ALPHA))
                P.dma(A[out_name][tok * 128:(tok + 1) * 128, :], pre[:, tt, :], reads=[r_pre[tt]], writes=[R[out_name]], q="pool")


def phase_ffn_up(g, layer, h_name):
    P, A, R, nc = g.P, g.A, g.R, g.nc
    Wu = A["ffn_w_up"][layer]
    with Phase(g) as ph:
        xT = ph.sb([128, 16, S], BF16, "hT")
        xT_res = [Res("hT%d" % i) for i in range(4)]
        with Phase(g) as ph2:
            make_xT(g, ph2, h_name, xT, xT_res)
        for cg in range(4):
            P.dma(A["wdn_b"][layer, cg], A["ffn_w_down"][layer][:, cg * 512:(cg + 1) * 512].rearrange("(kc p) c -> p kc c", p=128),
                  writes=[R["wdn_b"]], q="pool")
        w = [ph.sb([128, 16, 256], BF16, "wu") for _ in range(2)]
        r_w = [Res("wu0"), Res("wu1")]
        cw = ph.sb([128, 3, 44], F32, "cw")
        cb = ph.sb([128, 44], F32, "cb")
        r_cw = Res("cw")
        ptv = ph.ps([128, 128], F32, "ptv")
        r_ptv = Res("ptv")
        for j in range(3):
            load_vec_fm(g, ph, A["ffn_conv_w"][layer, j], DFF, cw[:, j, :], r_cw, ptv, r_ptv)
        load_vec_fm(g, ph, A["ffn_conv_b"][layer], DFF, cb[:, :], r_cw, ptv, r_ptv)
        pg = [ph.ps([128, 512], F32, "pg") for _ in range(2)]
        r_pg = [Res("pg0"), Res("pg1")]
        pu = [ph.ps([128, 512], F32, "pu") for _ in range(2)]
        r_pu = [Res("pu0"), Res("pu1")]
        gbuf = [ph.sb([128, S + 2], F32, "gbuf") for _ in range(2)]
        r_gb = [Res("gbuf0"), Res("gbuf1")]
        ubuf = [ph.sb([128, S], F32, "ubuf") for _ in range(2)]
        r_ub = [Res("fub0"), Res("fub1")]
        t1 = [ph.sb([128, S], F32, "t1") for _ in range(2)]
        r_t1 = [Res("t10"), Res("t11")]
        hf = [ph.sb([128, S], BF16, "hf") for _ in range(2)]
        r_hf = [Res("hf0"), Res("hf1")]
        for b in range(2):
            P.add("pool", lambda e, b=b: e.memset(gbuf[b][:, 0:2], 0.0), writes=[r_gb[b]])

        def load_w(c):
            b = c % 2
            P.dma(w[b][:, :, 0:128], Wu[:, c * 128:(c + 1) * 128].rearrange("(kc p) c -> p kc c", p=128), writes=[r_w[b]], q="pool")
            P.dma(w[b][:, :, 128:256], Wu[:, DFF + c * 128:DFF + (c + 1) * 128].rearrange("(kc p) c -> p kc c", p=128), writes=[r_w[b]], q="pool")

        load_w(0)
        k = 0
        for c in range(44):
            b = c % 2
            if c + 1 < 44:
                load_w(c + 1)
            for qt in range(4):
                pi = k % 2
                k += 1
                for kc in range(16):
                    P.mm(pg[pi][:], w[b][:, kc, 0:128], xT[:, kc, qt * 512:(qt + 1) * 512], start=(kc == 0), stop=(kc == 15),
                         reads=[xT_res[qt], r_w[b]], writes=[r_pg[pi]])
                for kc in range(16):
                    P.mm(pu[pi][:], w[b][:, kc, 128:256], xT[:, kc, qt * 512:(qt + 1) * 512], start=(kc == 0), stop=(kc == 15),
                         reads=[xT_res[qt], r_w[b]], writes=[r_pu[pi]])
                P.add("act", lambda e, b=b, pi=pi, qt=qt: e.copy(gbuf[b][:, 2 + qt * 512:2 + (qt + 1) * 512], pg[pi][:]), reads=[r_pg[pi]], writes=[r_gb[b]])
                P.add("dve", lambda e, b=b, pi=pi, qt=qt: e.tensor_copy(ubuf[b][:, qt * 512:(qt + 1) * 512], pu[pi][:]), reads=[r_pu[pi]], writes=[r_ub[b]])
            P.add("pool", lambda e, b=b, c=c: e.tensor_scalar(t1[b][:], gbuf[b][:, 0:S], cw[:, 0, c:c + 1], cb[:, c:c + 1], ALU.mult, ALU.add),
                  reads=[r_gb[b], r_cw], writes=[r_t1[b]])
            P.add("dve", lambda e, b=b, c=c: e.scalar_tensor_tensor(t1[b][:], gbuf[b][:, 1:S + 1], cw[:, 1, c:c + 1], t1[b][:], ALU.mult, ALU.add),
                  reads=[r_gb[b], r_cw, r_t1[b]], writes=[r_t1[b]])
            P.add("dve", lambda e, b=b, c=c: e.scalar_tensor_tensor(t1[b][:], gbuf[b][:, 2:S + 2], cw[:, 2, c:c + 1], t1[b][:], ALU.mult, ALU.add),
                  reads=[r_gb[b], r_cw, r_t1[b]], writes=[r_t1[b]])
            P.act(t1[b][:], t1[b][:], AF.Gelu, reads=[r_t1[b]], writes=[r_t1[b]])
            P.add("pool", lambda e, b=b: e.tensor_tensor(hf[b][:], t1[b][:], ubuf[b][:], ALU.mult), reads=[r_t1[b], r_ub[b]], writes=[r_hf[b]])
            for tg in range(4):
                P.dma(A["hffT_s"][tg, :, c, :], hf[b][:, tg * 512:(tg + 1) * 512], reads=[r_hf[b]], writes=[R["hffT_s"]])


def phase_l1_inproj(g):
    P, A, R, nc = g.P, g.A, g.R, g.nc
    W = A["odd_w_in"]
    groups = [(c, 512, "fm") for c in range(0, 2048, 512)] + [(c, 512, "v") for c in range(2048, 4096, 512)] \
        + [(4096, 16, "dg")] + [(c, 512, "r") for c in range(4112, 6160, 512)]
    with Phase(g) as ph:
        xT = ph.sb([128, 16, S], BF16, "xT1")
        xT_res = [Res("xT1%d" % i) for i in range(4)]
        with Phase(g) as ph2:
            make_xT(g, ph2, "h_b", xT, xT_res)
        w = [ph.sb([128, 16, 512], BF16, "w1") for _ in range(2)]
        r_w = [Res("w10"), Res("w11")]
        pacc = [ph.ps([128, 512], F32, "pacc") for _ in range(4)]
        r_pacc = [Res("pacc%d" % i) for i in range(4)]
        stf = [ph.sb([128, S], F32, "stf") for _ in range(2)]
        r_stf = [Res("stf0"), Res("stf1")]
        stv = [ph.sb([128, 512], BF16, "stv") for _ in range(2)]
        r_stv = [Res("stv0"), Res("stv1")]
        strr = [ph.sb([128, 512], F32, "str") for _ in range(2)]
        r_str = [Res("str0"), Res("str1")]
        cnt = {"pacc": 0, "chunk": 0, "v": 0}

        def load_w(gi):
            c0, gw, _ = groups[gi]
            b = gi % 2
            P.dma(w[b][:, :, 0:gw], W[:, c0:c0 + gw].rearrange("(kc p) c -> p kc c", p=128), writes=[r_w[b]], q="pool")

        load_w(0)
        for gi, (c0, gw, kind) in enumerate(groups):
            b = gi % 2
            if gi + 1 < len(groups):
                load_w(gi + 1)
            if kind in ("v", "r"):
                for tt in range(16):
                    pi = cnt["pacc"] % 4
                    cnt["pacc"] += 1
                    for kc in range(16):
                        P.mm(pacc[pi][:, :], xT[:, kc, tt * 128:(tt + 1) * 128], w[b][:, kc, 0:512], start=(kc == 0), stop=(kc == 15),
                             reads=[xT_res[tt // 4], r_w[b]], writes=[r_pacc[pi]])
                    sv = cnt["v"] % 2
                    cnt["v"] += 1
                    if kind == "v":
                        P.add("act", lambda e, sv=sv, pi=pi: e.copy(stv[sv][:], pacc[pi][:]), reads=[r_pacc[pi]], writes=[r_stv[sv]])
                        P.dma(A["v1_s"][tt * 128:(tt + 1) * 128, c0 - 2048:c0 - 2048 + 512], stv[sv][:], reads=[r_stv[sv]], writes=[R["v1_s"]])
                    else:
                        P.add("dve", lambda e, sv=sv, pi=pi: e.tensor_copy(strr[sv][:], pacc[pi][:]), reads=[r_pacc[pi]], writes=[r_str[sv]])
                        P.dma(A["r1_s"][tt * 128:(tt + 1) * 128, c0 - 4112:c0 - 4112 + 512], strr[sv][:], reads=[r_str[sv]], writes=[R["r1_s"]])
                continue
            nchunk = (gw + 127) // 128
            for j in range(nchunk):
                cw = min(128, gw - j * 128)
                col = c0 + j * 128
                ci = cnt["chunk"] % 2
                cnt["chunk"] += 1
                for qt in range(4):
                    pi = cnt["pacc"] % 4
                    cnt["pacc"] += 1
                    for kc in range(16):
                        P.mm(pacc[pi][0:cw, :], w[b][:, kc, j * 128:j * 128 + cw], xT[:, kc, qt * 512:(qt + 1) * 512], start=(kc == 0), stop=(kc == 15),
                             reads=[xT_res[qt], r_w[b]], writes=[r_pacc[pi]])
                    P.add("act", lambda e, ci=ci, pi=pi, qt=qt, cw=cw: e.copy(stf[ci][0:cw, qt * 512:(qt + 1) * 512], pacc[pi][0:cw, :]),
                          reads=[r_pacc[pi]], writes=[r_stf[ci]])
                if kind == "fm":
                    P.dma(A["qk1_s"][col:col + 128, :], stf[ci][:], reads=[r_stf[ci]], writes=[R["qk1_s"]])
                else:
                    P.dma(A["dg1_s"][0:16, :], stf[ci][0:16, :], reads=[r_stf[ci]], writes=[R["dg1_s"]])


def phase_gla(g):
    P, A, R, nc = g.P, g.A, g.R, g.nc
    SC = 1.0 / 16.0
    with Phase(g) as ph:
        gw2 = ph.sb([16, 1024], F32, "gw2")
        dgT = ph.sb([16, S], F32, "dgT")
        r_c = Res("glac")
        P.dma(gw2[:], A["gla_gate_w2"], writes=[r_c])
        P.dma(dgT[:], A["dg1_s"], reads=[R["dg1_s"]], writes=[r_c])
        ngb = ph.sb([128, 8], F32, "ngb")
        ptv = ph.ps([128, 128], F32, "ptv")
        r_ptv = Res("ptv")
        load_vec_fm(g, ph, A["gla_gate_b"], 1024, ngb[:, :], r_c, ptv, r_ptv)
        P.add("dve", lambda e: e.tensor_scalar(ngb[:], ngb[:], -1.0, None, ALU.mult), reads=[r_c], writes=[r_c])
        msk = ph.sb([128, 128], F32, "msk")
        P.add("pool", lambda e: e.memset(msk[:], 1.0), writes=[r_c])
        P.add("pool", lambda e: e.affine_select(msk[:], msk[:], [[1, 128]], ALU.is_ge, 0.0, base=0, channel_multiplier=-1),
              reads=[r_c], writes=[r_c])
        gnegh = ph.sb([128, 1], F32, "gnegh")
        P.add("pool", lambda e: e.memset(gnegh[:], -0.5), writes=[r_c])
        rst = ph.sb([128, 16, 128], F32, "rst")
        P.add("pool", lambda e: e.memset(rst[:], 1.0), writes=[r_c])
        P.add("pool", lambda e: e.memset(rst[:, :, 0:1], 0.0), reads=[r_c], writes=[r_c])
        rstf = rst[:].rearrange("p c l -> p (c l)")
        qf = ph.sb([128, S], F32, "qf")
        kf = ph.sb([128, S], F32, "kf")
        r_qk = Res("qkf")
        csp = ph.sb([128, 16, 128], F32, "csp")
        cspf = csp[:].rearrange("p c l -> p (c l)")
        r_csp = Res("csp")
        tmp = ph.sb([128, 16, 128], F32, "gtmp")
        tmpf = tmp[:].rearrange("p c l -> p (c l)")
        r_tmp = Res("gtmp")
        qd = ph.sb([128, 2, S], BF16, "qd")
        kinv = ph.sb([128, 2, S], BF16, "kinv")
        kendT = ph.sb([128, 2, S], BF16, "kendT")
        r_qd, r_kinv, r_kendT = Res("qd"), Res("kinv"), Res("kendT")
        kend = ph.sb([128, 16, 256], BF16, "kend")
        r_kend = Res("kend")
        dec = ph.sb([128, 2, 16], F32, "dec")
        r_dec = Res("dec")
        vv = ph.sb([128, 16, 512], BF16, "vv1")
        r_vv = Res("vv1")
        S_f = ph.sb([128, 2, 512], F32, "S_f")
        S_b = ph.sb([128, 2, 512], BF16, "S_b")
        r_Sf, r_Sb = Res("S_f"), Res("S_b")
        oT_h = ph.sb([128, 4, S], BF16, "oT_h")
        r_oTh = Res("oT_h")
        gng = ph.sb([128, 512], F32, "gng")
        gnb = ph.sb([128, 512], F32, "gnb")
        rbb = ph.sb([128, 512], F32, "rbb")
        r_hc = Res("headc")
        pA = [ph.ps([128, 512], F32, "pA") for _ in range(2)]
        r_pA = [Res("pA0"), Res("pA1")]
        patt = ph.ps([128, 128], F32, "patt")
        r_patt = Res("patt")
        po = [ph.ps([128, 512], F32, "po1") for _ in range(2)]
        r_po = [Res("po10"), Res("po11")]
        ptr = ph.ps([128, 8, 128], BF16, "ptr1")
        r_ptr = Res("ptr1")
        att_b = ph.sb([128, 128], BF16, "att_b")
        r_att = Res("att_b")
        NSL = 4
        o_sb = [ph.sb([128, 512], F32, "o_sb") for _ in range(NSL)]
        r_osb = [Res("o_sb%d" % i) for i in range(NSL)]
        rt = [ph.sb([128, 512], F32, "rt") for _ in range(NSL)]
        r_rt = [Res("rt%d" % i) for i in range(NSL)]
        og = [ph.sb([128, 512], BF16, "og") for _ in range(NSL)]
        r_og = [Res("og%d" % i) for i in range(NSL)]
        stats = [ph.sb([128, 1, 6], F32, "gstats") for _ in range(NSL)]
        mv = [ph.sb([128, 2], F32, "gmv") for _ in range(NSL)]
        rstd = [ph.sb([128, 1], F32, "grstd") for _ in range(NSL)]
        r_st = [Res("gst%d" % i) for i in range(NSL)]
        k = 0
        for h in range(4):
            P.dma(gng[:], A["gla_gn_g"][h * 512:(h + 1) * 512].partition_broadcast(128), writes=[r_hc])
            P.dma(gnb[:], A["gla_gn_b"][h * 512:(h + 1) * 512].partition_broadcast(128), writes=[r_hc])
            P.dma(rbb[:], A["gla_r_b"][h * 512:(h + 1) * 512].partition_broadcast(128), writes=[r_hc])
            P.dma(vv[:], A["v1_s"][:, h * 512:(h + 1) * 512].rearrange("(c p) v -> p c v", p=128), reads=[R["v1_s"]], writes=[r_vv])
            for dc in range(2):
                ch = h * 2 + dc
                P.dma(qf[:], A["qk1_s"][ch * 128:(ch + 1) * 128, :], reads=[R["qk1_s"]], writes=[r_qk])
                P.dma(kf[:], A["qk1_s"][1024 + ch * 128:1024 + (ch + 1) * 128, :], reads=[R["qk1_s"]], writes=[r_qk])
                for qt in range(4):
                    pi = k % 2
                    k += 1
                    P.mm(pA[pi][:], gw2[0:16, ch * 128:(ch + 1) * 128], dgT[0:16, qt * 512:(qt + 1) * 512], reads=[r_c], writes=[r_pA[pi]])
                    P.act(tmpf[:, qt * 512:(qt + 1) * 512], pA[pi][:], AF.Exp, reads=[r_pA[pi], r_c], writes=[r_tmp], scale=-1.0, bias=ngb[:, ch:ch + 1])
                P.act(tmpf, tmpf, AF.Ln, reads=[r_tmp], writes=[r_tmp], bias=1.0)
                P.add("dve", lambda e: e.tensor_tensor_scan(cspf, rstf, tmpf, 0.0, ALU.mult, ALU.add), reads=[r_tmp, r_c], writes=[r_csp])
                P.act(tmpf, cspf, AF.Exp, reads=[r_csp], writes=[r_tmp], scale=-SC)
                P.add("dve", lambda e, dc=dc: e.scalar_tensor_tensor(qd[:, dc, :], tmpf, SC, qf[:], ALU.mult, ALU.mult), reads=[r_tmp, r_qk], writes=[r_qd])
                P.act(dec[:, dc, :], csp[:, :, 127], AF.Exp, reads=[r_csp], writes=[r_dec], scale=-SC)
                P.act(tmpf, cspf, AF.Exp, reads=[r_csp], writes=[r_tmp], scale=SC)
                P.add("dve", lambda e, dc=dc: e.tensor_tensor(kinv[:, dc, :], tmpf, kf[:], ALU.mult), reads=[r_tmp, r_qk], writes=[r_kinv])
                P.add("dve", lambda e: e.tensor_tensor(tmp[:], csp[:], csp[:, :, 127:128].to_broadcast([128, 16, 128]), ALU.subtract),
                      reads=[r_csp], writes=[r_tmp])
                P.act(tmpf, tmpf, AF.Exp, reads=[r_tmp], writes=[r_tmp], scale=SC)
                P.add("dve", lambda e, dc=dc: e.tensor_tensor(kendT[:, dc, :], tmpf, kf[:], ALU.mult), reads=[r_tmp, r_qk], writes=[r_kendT])
                for half in range(2):
                    for j in range(8):
                        c = half * 8 + j
                        P.tr(ptr[:, j, :], kendT[:, dc, c * 128:(c + 1) * 128], g.ident_b[:], reads=[r_kendT, g.r_ident], writes=[r_ptr])
                    P.add("dve", lambda e, dc=dc, half=half: e.tensor_copy(kend[:, half * 8:(half + 1) * 8, dc * 128:(dc + 1) * 128], ptr[:]),
                          reads=[r_ptr], writes=[r_kend])
            kctr = [0]

            def gla_stage_a(c):
                cs = slice(c * 128, (c + 1) * 128)
                ob = c % 2
                for dc in range(2):
                    P.mm(patt[:], kinv[:, dc, cs], qd[:, dc, cs], start=(dc == 0), stop=(dc == 1), reads=[r_kinv, r_qd], writes=[r_patt])
                P.add("dve", lambda e: e.tensor_tensor(att_b[:], patt[:], msk[:], ALU.mult), reads=[r_patt, r_c], writes=[r_att])
                P.mm(po[ob][:], att_b[:], vv[:, c, :], start=True, stop=(c == 0), reads=[r_att, r_vv], writes=[r_po[ob]])
                if c > 0:
                    for dc in range(2):
                        P.mm(po[ob][:], qd[:, dc, cs], S_b[:, dc, :], start=False, stop=(dc == 1), reads=[r_qd, r_Sb], writes=[r_po[ob]])
                if c < 15:
                    for dc in range(2):
                        pi = kctr[0] % 2
                        kctr[0] += 1
                        P.mm(pA[pi][:], kend[:, c, dc * 128:(dc + 1) * 128], vv[:, c, :], reads=[r_kend, r_vv], writes=[r_pA[pi]])
                        if c == 0:
                            P.add("dve", lambda e: e.tensor_copy(S_b[:, dc, :], pA[pi][:]), reads=[r_pA[pi]], writes=[r_Sb])
                            P.add("dve", lambda e: e.tensor_copy(S_f[:, dc, :], pA[pi][:]), reads=[r_pA[pi]], writes=[r_Sf])
                        else:
                            P.add("dve", lambda e: e.scalar_tensor_tensor(S_b[:, dc, :], S_f[:, dc, :], dec[:, dc, c:c + 1], pA[pi][:], ALU.mult, ALU.add),
                                  reads=[r_pA[pi], r_Sf, r_dec], writes=[r_Sb])
                            P.add("dve", lambda e: e.scalar_tensor_tensor(S_f[:, dc, :], S_f[:, dc, :], dec[:, dc, c:c + 1], pA[pi][:], ALU.mult, ALU.add),
                                  reads=[r_pA[pi], r_Sf, r_dec], writes=[r_Sf])

            def gla_s0(c):
                cs = slice(c * 128, (c + 1) * 128)
                ob = c % 2
                sl = c % NSL
                P.add("dve", lambda e: e.tensor_copy(o_sb[sl][:], po[ob][:]), reads=[r_po[ob]], writes=[r_osb[sl]])
                P.dma(rt[sl][:], A["r1_s"][cs, h * 512:(h + 1) * 512], reads=[R["r1_s"]], writes=[r_rt[sl]])
                P.add("pool", lambda e: e.tensor_tensor(rt[sl][:], rt[sl][:], rbb[:], ALU.add), reads=[r_rt[sl], r_hc], writes=[r_rt[sl]])
                P.act(rt[sl][:], rt[sl][:], AF.Silu, reads=[r_rt[sl]], writes=[r_rt[sl]])
                P.add("dve", lambda e: e.bn_stats(stats[sl][:, 0, :], o_sb[sl][:]), reads=[r_osb[sl]], writes=[r_st[sl]])
                P.add("dve", lambda e: e.bn_aggr(mv[sl][:], stats[sl][:, 0:1, :]), reads=[r_st[sl]], writes=[r_st[sl]])

            def gla_s1(c):
                sl = c % NSL
                P.add("pool", lambda e: e.tensor_scalar(rstd[sl][:], mv[sl][:, 1:2], LN_EPS, None, ALU.add), reads=[r_st[sl]], writes=[r_st[sl]])
                P.add("pool", lambda e: e.tensor_tensor(rstd[sl][:], rstd[sl][:], gnegh[:], ALU.pow), reads=[r_st[sl], r_c], writes=[r_st[sl]])
                P.add("dve", lambda e: e.tensor_scalar(o_sb[sl][:], o_sb[sl][:], mv[sl][:, 0:1], rstd[sl][:, 0:1], ALU.subtract, ALU.mult),
                      reads=[r_st[sl], r_osb[sl]], writes=[r_osb[sl]])

            def gla_s2(c):
                sl = c % NSL
                P.add("pool", lambda e: e.tensor_tensor(o_sb[sl][:], o_sb[sl][:], gng[:], ALU.mult), reads=[r_hc, r_osb[sl]], writes=[r_osb[sl]])
                P.add("pool", lambda e: e.tensor_tensor(o_sb[sl][:], o_sb[sl][:], gnb[:], ALU.add), reads=[r_hc, r_osb[sl]], writes=[r_osb[sl]])
                P.add("dve", lambda e: e.tensor_tensor(og[sl][:], o_sb[sl][:], rt[sl][:], ALU.mult), reads=[r_osb[sl], r_rt[sl]], writes=[r_og[sl]])

            def gla_s3(c):
                cs = slice(c * 128, (c + 1) * 128)
                sl = c % NSL
                for vc in range(4):
                    P.tr(ptr[:, vc, :], og[sl][:, vc * 128:(vc + 1) * 128], g.ident_b[:], reads=[r_og[sl], g.r_ident], writes=[r_ptr])
                P.add("dve", lambda e: e.tensor_copy(oT_h[:, :, cs], ptr[:, 0:4, :]), reads=[r_ptr], writes=[r_oTh])

            for c in range(16 + 4):
                if c < 16:
                    gla_stage_a(c)
                for k_, fn_ in enumerate((gla_s0, gla_s1, gla_s2, gla_s3)):
                    cc_ = c - 1 - k_
                    if 0 <= cc_ < 16:
                        fn_(cc_)
            P.dma(A["oT_s"][h * 512:(h + 1) * 512, :].rearrange("(vc p) t -> p vc t", p=128), oT_h[:], reads=[r_oTh], writes=[R["oT_s"]])


def phase_rwkv(g):
    P, A, R, nc = g.P, g.A, g.R, g.nc
    H = 8
    HP = 4
    RW = A["rwT_s"]
    with Phase(g) as ph:
        T = {}

        def mk(name, shape, dt=F32):
            t = ph.sb(shape, dt, name)
            T[name] = (t, Res(name))
            return t

        def tl(name):
            return T[name][0]

        def rs(*names):
            return [T[n][1] for n in names]

        def op(eng, fn, reads, writes):
            P.add(eng, fn, rs(*reads), rs(*writes))

        def hsl(h):
            return slice((h % 2) * 64, (h % 2) * 64 + 64), h // 2

        mk("w2", [64, 1024]); mk("a2", [64, 1024]); mk("g2a", [128, 1024]); mk("g2b", [32, 1024])
        P.dma(tl("w2")[:], A["rw_w2"], writes=rs("w2"))
        P.dma(tl("a2")[:], A["rw_a2"], writes=rs("a2"))
        P.dma(tl("g2a")[:], A["rw_g2"][0:128, :], writes=rs("g2a"))
        P.dma(tl("g2b")[:], A["rw_g2"][128:160, :], writes=rs("g2b"))
        psA = [ph.ps([128, HP, 128], F32, "psA") for _ in range(2)]
        r_psA = [Res("psA0"), Res("psA1")]
        for n in ("rw_w0", "rw_a0", "rw_k_k", "rw_k_a", "rw_r_k", "rw_gn_g", "rw_gn_b"):
            mk(n, [128, 8])
            load_vec_fm(g, ph, A[n], 1024, tl(n)[:, :], T[n][1], psA[0][:, 0, :], r_psA[0], parts=128)
        mk("ones", [128, 128]); mk("onesm", [128, 128]); mk("mS", [128, 64]); mk("mI", [128, 64]); mk("mL", [128, 64])
        mk("identS", [128, 64])
        mk("rst", [128, HP, 2, 64])
        for nm, val in (("ones", 1.0), ("onesm", 1.0 / 64.0)):
            op("pool", lambda e: e.memset(tl(nm)[:], 0.0), [], [nm])
            for h2 in range(2):
                q = slice(h2 * 64, h2 * 64 + 64)
                op("pool", lambda e: e.memset(tl(nm)[q, q], val), [nm], [nm])
        for nm, cmp_, cm, st_ in (("mS", ALU.is_gt, -1, 1), ("mI", ALU.is_ge, -1, 1), ("mL", ALU.is_gt, 1, -1)):
            op("pool", lambda e: e.memset(tl(nm)[:], 1.0), [], [nm])
            for h2 in range(2):
                q = slice(h2 * 64, h2 * 64 + 64)
                op("pool", lambda e: e.affine_select(tl(nm)[q, :], tl(nm)[q, :], [[st_, 64]], cmp_, 0.0, base=0, channel_multiplier=cm), [nm], [nm])
        for h2 in range(2):
            q = slice(h2 * 64, h2 * 64 + 64)
            P.add("pool", lambda e: e.tensor_copy(tl("identS")[q, :], g.ident_f[q, q]), reads=[g.r_ident], writes=rs("identS"))
        op("pool", lambda e: e.memset(tl("rst")[:], 1.0), [], ["rst"])
        op("pool", lambda e: e.memset(tl("rst")[:, :, :, 0:1], 0.0), ["rst"], ["rst"])

        def bcm(name):
            return tl(name)[:].unsqueeze(1).to_broadcast([128, HP, 64])

        for n in ("r_", "k_", "v_", "lw", "cl", "a_", "g_", "ep", "en", "ex", "kk", "kt", "bh", "t1", "bonus",
                  "yT", "t2"):
            mk(n, [128, HP, 128])
        for n in ("RG", "AG", "BI", "KI", "Bend", "Kend", "vb16"):
            mk(n, [128, HP, 128], BF16)
        mk("dwT", [64, 128]); mk("daT", [64, 128]); mk("dgT", [128, 128]); mk("dgT2", [32, 128])
        mk("gL", [128, HP, 2])
        mk("ob", [128, HP, 128], BF16)
        for cc in range(2):
            for n in ("Vt", "BeT", "KeT", "M", "N", "Aak", "Arb", "Ark", "TT", "M2", "N2"):
                mk("%s%d" % (n, cc), [128, HP, 64], BF16)
        mk("P", [128, HP, 64]); mk("Xs", [128, HP, 64], BF16); mk("Us", [128, HP, 64], BF16); mk("Pt", [128, HP, 64])
        mk("Pb", [128, HP, 64], BF16)
        psB = [ph.ps([128, HP, 64], F32, "psB") for _ in range(4)]
        r_psB = [Res("psB%d" % i) for i in range(4)]
        psC = ph.ps([128, HP * 128], F32, "psC")
        r_psC = Res("psC")
        cnt = {"A": 0, "B": 0}

        def nextB():
            i = cnt["B"] % 4
            cnt["B"] += 1
            return psB[i], r_psB[i]

        def flat(name):
            return tl(name)[:].rearrange("p h t -> p (h t)")

        def ones_reduce(lhs_name, src_name):
            P.mm(psC[:, :], tl(lhs_name)[:], flat(src_name), reads=rs(lhs_name, src_name), writes=[r_psC])

        psC3 = psC[:].rearrange("p (h t) -> p h t", h=HP)

        for hg in range(2):
            prs = slice(hg * HP, (hg + 1) * HP)

            def vb(name, n=128):
                return tl(name)[:, prs].unsqueeze(2).to_broadcast([128, HP, n])

            op("pool", lambda e: e.memset(tl("P")[:], 0.0), [], ["P"])
            op("pool", lambda e: e.memset(tl("Pb")[:], 0.0), [], ["Pb"])
            for blk in range(16):
                t0 = blk * 128
                ts_ = slice(t0, t0 + 128)
                for nm, base in (("r_", 0), ("k_", 1024), ("v_", 2048)):
                    P.dma(tl(nm)[:], RW[base + hg * 512:base + (hg + 1) * 512, ts_].rearrange("(hp p) t -> p hp t", p=128),
                          reads=[R["rwT_s"]], writes=rs(nm))
                P.dma(tl("dwT")[:], RW[3072:3136, ts_], reads=[R["rwT_s"]], writes=rs("dwT"))
                P.dma(tl("daT")[:], RW[3136:3200, ts_], reads=[R["rwT_s"]], writes=rs("daT"))
                P.dma(tl("dgT")[:], RW[3200:3328, ts_], reads=[R["rwT_s"]], writes=rs("dgT"))
                P.dma(tl("dgT2")[:], RW[3328:3360, ts_], reads=[R["rwT_s"]], writes=rs("dgT2"))
                P.act(tl("dwT")[:], tl("dwT")[:], AF.Tanh, reads=rs("dwT"), writes=rs("dwT"))
                P.act(tl("dgT")[:], tl("dgT")[:], AF.Sigmoid, reads=rs("dgT"), writes=rs("dgT"))
                P.act(tl("dgT2")[:], tl("dgT2")[:], AF.Sigmoid, reads=rs("dgT2"), writes=rs("dgT2"))
                for (wn, src, dst, bn) in (("w2", "dwT", "lw", "rw_w0"), ("a2", "daT", "a_", "rw_a0"), ("g2", "dgT", "g_", None)):
                    ai = cnt["A"] % 2
                    cnt["A"] += 1
                    for hp in range(HP):
                        pr = hg * HP + hp
                        cols = slice(pr * 128, (pr + 1) * 128)
                        if wn == "g2":
                            P.mm(psA[ai][:, hp, :], tl("g2a")[:, cols], tl("dgT")[:], start=True, stop=False, reads=rs("g2a", "dgT"), writes=[r_psA[ai]])
                            P.mm(psA[ai][:, hp, :], tl("g2b")[:, cols], tl("dgT2")[:], start=False, stop=True, reads=rs("g2b", "dgT2"), writes=[r_psA[ai]])
                        else:
                            P.mm(psA[ai][:, hp, :], tl(wn)[:, cols], tl(src)[:], reads=rs(wn, src), writes=[r_psA[ai]])
                    if bn is None:
                        P.add("dve", lambda e: e.tensor_copy(tl(dst)[:], psA[ai][:]), reads=[r_psA[ai]], writes=rs(dst))
                    else:
                        for hp in range(HP):
                            pr = hg * HP + hp
                            P.act(tl(dst)[:, hp, :], psA[ai][:, hp, :], AF.Sigmoid, reads=[r_psA[ai]] + rs(bn), writes=rs(dst), bias=tl(bn)[:, pr:pr + 1])
                op("dve", lambda e: e.tensor_scalar(tl("lw")[:], tl("lw")[:], -EXPM05, None, ALU.mult), ["lw"], ["lw"])
                op("dve", lambda e: e.tensor_tensor_scan(flat("cl"), tl("rst")[:].rearrange("p h c l -> p (h c l)"), flat("lw"), 0.0, ALU.mult, ALU.add),
                   ["lw", "rst"], ["cl"])
                P.act(tl("ep")[:], tl("cl")[:], AF.Exp, reads=rs("cl"), writes=rs("ep"))
                P.act(tl("en")[:], tl("cl")[:], AF.Exp, reads=rs("cl"), writes=rs("en"), scale=-1.0)
                op("dve", lambda e: e.tensor_tensor(tl("ex")[:], tl("cl")[:], tl("lw")[:], ALU.subtract), ["cl", "lw"], ["ex"])
                P.act(tl("ex")[:], tl("ex")[:], AF.Exp, reads=rs("ex"), writes=rs("ex"))
                op("pool", lambda e: e.tensor_copy(tl("gL")[:], tl("ep")[:].rearrange("p h (c l) -> p h c l", c=2)[:, :, :, 63]), ["ep"], ["gL"])
                op("dve", lambda e: e.tensor_tensor(tl("kk")[:], tl("k_")[:], vb("rw_k_k"), ALU.mult), ["k_", "rw_k_k"], ["kk"])
                op("pool", lambda e: e.tensor_tensor(tl("t1")[:], tl("kk")[:], tl("kk")[:], ALU.mult), ["kk"], ["t1"])
                ones_reduce("ones", "t1")
                P.add("dve", lambda e: e.tensor_scalar_max(tl("t1")[:], psC3, 1e-24), reads=[r_psC], writes=rs("t1"))
                P.act(tl("t1")[:], tl("t1")[:], AF.Sqrt, reads=rs("t1"), writes=rs("t1"))
                op("dve", lambda e: e.reciprocal(tl("t1")[:], tl("t1")[:]), ["t1"], ["t1"])
                op("dve", lambda e: e.tensor_tensor(tl("kk")[:], tl("kk")[:], tl("t1")[:], ALU.mult), ["kk", "t1"], ["kk"])
                op("dve", lambda e: e.scalar_tensor_tensor(tl("t2")[:], tl("a_")[:], -1.0, vb("rw_k_a"), ALU.add, ALU.mult), ["a_", "rw_k_a"], ["t2"])
                op("dve", lambda e: e.scalar_tensor_tensor(tl("kt")[:], tl("t2")[:], 1.0, tl("k_")[:], ALU.add, ALU.mult), ["t2", "k_"], ["kt"])
                op("pool", lambda e: e.tensor_tensor(tl("bh")[:], tl("kk")[:], tl("a_")[:], ALU.mult), ["kk", "a_"], ["bh"])
                op("pool", lambda e: e.tensor_tensor(tl("t2")[:], tl("r_")[:], tl("kt")[:], ALU.mult), ["r_", "kt"], ["t2"])
                op("pool", lambda e: e.tensor_tensor(tl("t2")[:], tl("t2")[:], vb("rw_r_k"), ALU.mult), ["t2", "rw_r_k"], ["t2"])
                ones_reduce("ones", "t2")
                P.add("dve", lambda e: e.tensor_tensor(tl("bonus")[:], psC3, tl("v_")[:], ALU.mult), reads=[r_psC] + rs("v_"), writes=rs("bonus"))
                op("pool", lambda e: e.tensor_tensor(tl("RG")[:], tl("r_")[:], tl("ep")[:], ALU.mult), ["r_", "ep"], ["RG"])
                op("dve", lambda e: e.scalar_tensor_tensor(tl("AG")[:], tl("kk")[:], -1.0, tl("ex")[:], ALU.mult, ALU.mult), ["kk", "ex"], ["AG"])
                gLb = tl("gL")[:].unsqueeze(3).to_broadcast([128, HP, 2, 64])
                op("pool", lambda e: e.tensor_tensor(tl("BI")[:], tl("bh")[:], tl("en")[:], ALU.mult), ["bh", "en"], ["BI"])
                op("pool", lambda e: e.tensor_tensor(tl("KI")[:], tl("kt")[:], tl("en")[:], ALU.mult), ["kt", "en"], ["KI"])
                op("dve", lambda e: e.tensor_tensor(tl("Bend")[:].rearrange("p h (c l) -> p h c l", c=2), tl("BI")[:].rearrange("p h (c l) -> p h c l", c=2), gLb, ALU.mult),
                   ["BI", "gL"], ["Bend"])
                op("dve", lambda e: e.tensor_tensor(tl("Kend")[:].rearrange("p h (c l) -> p h c l", c=2), tl("KI")[:].rearrange("p h (c l) -> p h c l", c=2), gLb, ALU.mult),
                   ["KI", "gL"], ["Kend"])
                op("act", lambda e: e.copy(tl("vb16")[:], tl("v_")[:]), ["v_"], ["vb16"])
                for cc in range(2):
                    c_ = slice(cc * 64, (cc + 1) * 64)
                    sfx = str(cc)
                    for k3, (src, dst) in enumerate((("vb16", "Vt"), ("Bend", "BeT"), ("Kend", "KeT"))):
                        pb, r_pb = nextB()
                        for h in range(H):
                            q, hp = hsl(h)
                            P.mm(pb[q, hp, :], tl(src)[q, hp, c_], g.ident_b[q, q], reads=rs(src) + [g.r_ident], writes=[r_pb])
                        if k3 % 2 == 0:
                            P.add("act", lambda e: e.copy(tl(dst + sfx)[:], pb[:]), reads=[r_pb], writes=rs(dst + sfx))
                        else:
                            P.add("dve", lambda e: e.tensor_copy(tl(dst + sfx)[:], pb[:]), reads=[r_pb], writes=rs(dst + sfx))
                    for (lh, rh, dst, msk) in (("BI", "AG", "M", "mS"), ("KI", "AG", "Aak", "mS"), ("BI", "RG", "Arb", "mI"), ("KI", "RG", "Ark", "mI"),
                                              ("AG", "BI", "N", "mL")):
                        pb, r_pb = nextB()
                        for h in range(H):
                            q, hp = hsl(h)
                            P.mm(pb[q, hp, :], tl(lh)[q, hp, c_], tl(rh)[q, hp, c_], reads=rs(lh, rh), writes=[r_pb])
                        P.add("dve", lambda e: e.tensor_tensor(tl(dst + sfx)[:], pb[:], bcm(msk), ALU.mult), reads=[r_pb] + rs(msk), writes=rs(dst + sfx))
                    op("pool", lambda e: e.tensor_tensor(tl("TT" + sfx)[:], tl("M" + sfx)[:], bcm("identS"), ALU.add), ["M" + sfx, "identS"], ["TT" + sfx])
                    cur_m, cur_n = "M" + sfx, "N" + sfx
                    oth_m, oth_n = "M2" + sfx, "N2" + sfx
                    for lvl in range(5):
                        last = (lvl == 4)
                        pn, r_pn = nextB()
                        for h in range(H):
                            q, hp = hsl(h)
                            P.mm(pn[q, hp, :], tl(cur_m)[q, hp, :], tl(cur_n)[q, hp, :], reads=rs(cur_m, cur_n), writes=[r_pn])
                        if not last:
                            pm, r_pm = nextB()
                            for h in range(H):
                                q, hp = hsl(h)
                                P.mm(pm[q, hp, :], tl(cur_n)[q, hp, :], tl(cur_m)[q, hp, :], reads=rs(cur_m, cur_n), writes=[r_pm])
                        P.add("act", lambda e: e.copy(tl(oth_n)[:], pn[:]), reads=[r_pn], writes=rs(oth_n))
                        if not last:
                            P.add("dve", lambda e: e.tensor_copy(tl(oth_m)[:], pm[:]), reads=[r_pm], writes=rs(oth_m))
                        pt_, r_pt = nextB()
                        for h in range(H):
                            q, hp = hsl(h)
                            P.mm(pt_[q, hp, :], tl(oth_n)[q, hp, :], tl("TT" + sfx)[q, hp, :], reads=rs(oth_n, "TT" + sfx), writes=[r_pt])
                        P.add("dve", lambda e: e.tensor_tensor(tl("TT" + sfx)[:], tl("TT" + sfx)[:], pt_[:], ALU.add), reads=[r_pt] + rs("TT" + sfx), writes=rs("TT" + sfx))
                        cur_m, oth_m = oth_m, cur_m
                        cur_n, oth_n = oth_n, cur_n
                for cc in range(2):
                    c_ = slice(cc * 64, (cc + 1) * 64)
                    sfx = str(cc)
                    px, r_px = nextB()
                    for h in range(H):
                        q, hp = hsl(h)
                        P.mm(px[q, hp, :], tl("AG")[q, hp, c_], tl("Pb")[q, hp, :], start=True, stop=False, reads=rs("AG", "Pb"), writes=[r_px])
                        P.mm(px[q, hp, :], tl("Aak" + sfx)[q, hp, :], tl("Vt" + sfx)[q, hp, :], start=False, stop=True, reads=rs("Aak" + sfx, "Vt" + sfx), writes=[r_px])
                    P.add("act", lambda e: e.copy(tl("Xs")[:], px[:]), reads=[r_px], writes=rs("Xs"))
                    pu, r_pu = nextB()
                    for h in range(H):
                        q, hp = hsl(h)
                        P.mm(pu[q, hp, :], tl("TT" + sfx)[q, hp, :], tl("Xs")[q, hp, :], reads=rs("TT" + sfx, "Xs"), writes=[r_pu])
                    P.add("dve", lambda e: e.tensor_copy(tl("Us")[:], pu[:]), reads=[r_pu], writes=rs("Us"))
                    py, r_py = nextB()
                    for h in range(H):
                        q, hp = hsl(h)
                        P.mm(py[q, hp, :], tl("Pb")[q, hp, :], tl("RG")[q, hp, c_], start=True, stop=False, reads=rs("Pb", "RG"), writes=[r_py])
                        P.mm(py[q, hp, :], tl("Us")[q, hp, :], tl("Arb" + sfx)[q, hp, :], start=False, stop=False, reads=rs("Us", "Arb" + sfx), writes=[r_py])
                        P.mm(py[q, hp, :], tl("Vt" + sfx)[q, hp, :], tl("Ark" + sfx)[q, hp, :], start=False, stop=True, reads=rs("Vt" + sfx, "Ark" + sfx), writes=[r_py])
                    P.add("act", lambda e: e.copy(tl("yT")[:, :, c_], py[:]), reads=[r_py], writes=rs("yT"))
                    pp, r_pp = nextB()
                    for h in range(H):
                        q, hp = hsl(h)
                        P.mm(pp[q, hp, :], tl("BeT" + sfx)[q, hp, :], tl("Us")[q, hp, :], start=True, stop=False, reads=rs("BeT" + sfx, "Us"), writes=[r_pp])
                        P.mm(pp[q, hp, :], tl("KeT" + sfx)[q, hp, :], tl("Vt" + sfx)[q, hp, :], start=False, stop=True, reads=rs("KeT" + sfx, "Vt" + sfx), writes=[r_pp])
                    op("pool", lambda e: e.tensor_tensor(tl("Pt")[:], tl("P")[:], tl("gL")[:, :, cc:cc + 1].to_broadcast([128, HP, 64]), ALU.mult), ["P", "gL"], ["Pt"])
                    P.add("dve", lambda e: e.tensor_tensor(tl("P")[:], tl("Pt")[:], pp[:], ALU.add), reads=[r_pp] + rs("Pt"), writes=rs("P"))
                    op("act", lambda e: e.copy(tl("Pb")[:], tl("P")[:]), ["P"], ["Pb"])
                ones_reduce("onesm", "yT")
                P.add("dve", lambda e: e.tensor_tensor(tl("yT")[:], tl("yT")[:], psC3, ALU.subtract), reads=[r_psC] + rs("yT"), writes=rs("yT"))
                op("pool", lambda e: e.tensor_tensor(tl("t1")[:], tl("yT")[:], tl("yT")[:], ALU.mult), ["yT"], ["t1"])
                ones_reduce("onesm", "t1")
                P.act(tl("t1")[:], psC3, AF.Sqrt, reads=[r_psC], writes=rs("t1"), bias=64e-5)
                op("dve", lambda e: e.reciprocal(tl("t1")[:], tl("t1")[:]), ["t1"], ["t1"])
                op("dve", lambda e: e.tensor_tensor(tl("yT")[:], tl("yT")[:], tl("t1")[:], ALU.mult), ["yT", "t1"], ["yT"])
                op("pool", lambda e: e.tensor_tensor(tl("yT")[:], tl("yT")[:], vb("rw_gn_g"), ALU.mult), ["yT", "rw_gn_g"], ["yT"])
                op("pool", lambda e: e.tensor_tensor(tl("yT")[:], tl("yT")[:], vb("rw_gn_b"), ALU.add), ["yT", "rw_gn_b"], ["yT"])
                op("dve", lambda e: e.tensor_tensor(tl("yT")[:], tl("yT")[:], tl("bonus")[:], ALU.add), ["yT", "bonus"], ["yT"])
                op("dve", lambda e: e.tensor_tensor(tl("ob")[:], tl("yT")[:], tl("g_")[:], ALU.mult), ["yT", "g_"], ["ob"])
                P.dma(A["oT_s"][1024 + hg * 512:1024 + (hg + 1) * 512, ts_].rearrange("(hp p) t -> p hp t", p=128), tl("ob")[:],
                      reads=rs("ob"), writes=[R["oT_s"]])


N_CORES = 4
_NC_CACHE = {}


def _core_inputs(inp, b):
    m = {"x": np.ascontiguousarray(inp["x"][b], dtype=np.float32)}
    for k, v in inp.items():
        if k == "x":
            continue
        v = np.asarray(v, dtype=np.float32)
        if k.startswith("ffn_") or k.startswith("ln_"):
            m[k] = np.ascontiguousarray(v)
        else:
            v0 = v[0]
            if k == "rw_r_k":
                v0 = v0.reshape(-1)
            m[k] = np.ascontiguousarray(v0)
    return m


def kernel(**inputs):
    if "nc" not in _NC_CACHE:
        _NC_CACHE["nc"] = build_program()
    nc = _NC_CACHE["nc"]
    in_maps = [_core_inputs(inputs, b) for b in range(N_CORES)]
    res = run_bass_kernel_spmd(nc, in_maps, core_ids=list(range(N_CORES)))
    out = np.stack([np.asarray(res.results[b]["y"], dtype=np.float32) for b in range(N_CORES)], axis=0)
    return out
```

```python
import contextlib
import numpy as np
import concourse.bass as bass
import concourse.mybir as mybir
from concourse.bass_utils import run_bass_kernel_spmd

F32 = mybir.dt.float32
BF16 = mybir.dt.bfloat16
AF = mybir.ActivationFunctionType
ALU = mybir.AluOpType
AX = mybir.AxisListType

ENGS = ("pe", "act", "dve", "pool", "sp")
SAME_ENG_SYNC = True
NSLOT = 8
SKEW = True


class Res:
    __slots__ = ("name", "lw", "rd")

    def __init__(self, name=""):
        self.name = name
        self.lw = None
        self.rd = []


class Op:
    __slots__ = ("eng", "fn", "reads", "writes", "dma", "deps", "need", "sig", "waits", "idx")

    def __init__(self, eng, fn, reads, writes, dma):
        self.eng = eng
        self.fn = fn
        self.reads = reads
        self.writes = writes
        self.dma = dma
        self.deps = []
        self.need = False
        self.sig = None
        self.waits = []


class _Rec:
    def __init__(self):
        self.call = None

    def __getattr__(self, name):
        def f(*a, **k):
            assert self.call is None
            self.call = (name, a, k)
            return self
        return f

    def replay(self, e):
        name, a, k = self.call
        return getattr(e, name)(*a, **k)


class Prog:
    def __init__(self, nc, stack):
        self.nc = nc
        self.ops = []
        self.sems = {e: stack.enter_context(nc.semaphore("s_" + e)) for e in ENGS}
        self.dsems = {e: [stack.enter_context(nc.semaphore("d_%s%d" % (e, i))) for i in range(NSLOT)]
                      for e in ("sp", "act", "pool")}
        self.sigcount = {e: 0 for e in ENGS}
        self.ndma = {e: 0 for e in ("sp", "act", "pool")}
        self.waited = {e: {} for e in ENGS}
        self.live = set()
        self.nops = 0

    def add(self, eng, fn, reads=(), writes=(), dma=False):
        if fn is not None:
            rec = _Rec()
            fn(rec)
            fn = rec.replay
        op = Op(eng, fn, tuple(reads), tuple(writes), dma)
        deps = []
        for r in op.reads:
            if r.lw is not None:
                deps.append(r.lw)
        for w in op.writes:
            if w.lw is not None:
                deps.append(w.lw)
            deps.extend(w.rd)
        seen = set()
        for d in deps:
            if id(d) in seen or d is op:
                continue
            seen.add(id(d))
            if (not d.dma) and d.eng == eng and not dma:
                if eng == "pe" or not SAME_ENG_SYNC:
                    continue
            op.deps.append(d)
            d.need = True
        for r in op.reads:
            r.rd.append(op)
            self.live.add(r)
        for w in op.writes:
            w.lw = op
            w.rd = []
            self.live.add(w)
        self.ops.append(op)
        return op

    def flush(self):
        nc = self.nc
        ops = self.ops
        self.ops = []
        if not ops:
            return
        for r in self.live:
            if r.lw is not None:
                r.lw.need = True
            for o in r.rd:
                o.need = True
        for op in ops:
            eng = op.eng
            waits = []
            if op.dma:
                i = self.ndma[eng]
                self.ndma[eng] = i + 1
                sem = self.dsems[eng][i % NSLOT]
                val = 16 * (i // NSLOT + 1)
                if val > 16:
                    if self.waited[eng].get(sem, 0) < val - 16:
                        waits.append((sem, val - 16))
                        self.waited[eng][sem] = val - 16
                op.sig = (sem, val)
            elif op.need and op.fn is not None:
                self.sigcount[eng] += 1
                op.sig = (self.sems[eng], self.sigcount[eng])
            for d in op.deps:
                if d.sig is None:
                    continue
                sem, val = d.sig
                if self.waited[eng].get(sem, 0) >= val:
                    continue
                self.waited[eng][sem] = val
                waits.append((sem, val))
            op.waits = waits
        per = {e: [o for o in ops if o.eng == e] for e in ENGS}
        self.nops += len(ops)

        def emit(e, lst):
            for op in lst:
                for sem, val in op.waits:
                    e.wait_ge(sem, val)
                if op.fn is None:
                    continue
                ins = op.fn(e)
                if op.sig is not None:
                    ins.then_inc(op.sig[0], 16 if op.dma else 1)

        with nc.Block() as block:
            @block.tensor
            def _(e):
                emit(e, per["pe"])

            @block.scalar
            def _(e):
                emit(e, per["act"])

            @block.vector
            def _(e):
                emit(e, per["dve"])

            @block.gpsimd
            def _(e):
                emit(e, per["pool"])

            @block.sync
            def _(e):
                emit(e, per["sp"])
        for op in ops:
            op.fn = None

    def dma(self, out, in_, reads=(), writes=(), q="sp"):
        return self.add(q, lambda e: e.dma_start(out=out, in_=in_), reads, writes, dma=True)

    def mm(self, out, lhsT, rhs, start=True, stop=True, reads=(), writes=()):
        return self.add("pe", lambda e: e.matmul(out, lhsT, rhs, start=start, stop=stop), reads, writes)

    def tr(self, out, in_, ident, reads=(), writes=()):
        return self.add("pe", lambda e: e.transpose(out, in_, ident), reads, writes)

    def act(self, out, in_, func, reads=(), writes=(), **kw):
        return self.add("act", lambda e: e.activation(out, in_, func, **kw), reads, writes)

    def wait_all(self, eng, ress):
        return self.add(eng, None, reads=ress)


S = 2048
D = 2048
DFF = 5632
ALPHA = 4.0 ** 0.25
LN_EPS = 1e-5
EXPM05 = float(np.exp(-0.5))


class Ctx:
    pass


DBG_RW = ["r_", "k_", "v_", "lw", "cl", "a_", "g_", "kk", "kt", "bh", "bonus", "RG", "AG", "BI", "KI", "Bend", "Kend", "yT"]
DBG_RW2 = ["Vt0", "BeT0", "KeT0", "M0", "N0", "Aak0", "Arb0", "Ark0", "TT0", "Vt1", "TT1", "P", "Xs", "Us"]


def build_program(phases=None, kinds=None):
    kinds = kinds or {}
    nc = bass.Bass("TRN2", target_bir_lowering=False)
    g = Ctx()
    g.nc = nc

    def dram(name, shape, dt, kind="Internal"):
        return nc.dram_tensor(name, list(shape), dt, kind=kinds.get(name, kind)).ap()

    I = "ExternalInput"
    A = {}
    A["x"] = dram("x", [S, D], F32, I)
    A["even_w_in"] = dram("even_w_in", [D, 6432], F32, I)
    A["even_shift_mu"] = dram("even_shift_mu", [3360], F32, I)
    for n in ("rw_w0", "rw_a0", "rw_k_k", "rw_k_a", "rw_r_k", "rw_gn_g", "rw_gn_b", "gla_gate_b"):
        A[n] = dram(n, [1024], F32, I)
    A["rw_w2"] = dram("rw_w2", [64, 1024], F32, I)
    A["rw_a2"] = dram("rw_a2", [64, 1024], F32, I)
    A["rw_g2"] = dram("rw_g2", [160, 1024], F32, I)
    A["even_w_out"] = dram("even_w_out", [D, D], F32, I)
    A["odd_w_in"] = dram("odd_w_in", [D, 6160], F32, I)
    A["gla_gate_w2"] = dram("gla_gate_w2", [16, 1024], F32, I)
    for n in ("gla_r_b", "gla_gn_g", "gla_gn_b"):
        A[n] = dram(n, [2048], F32, I)
    A["odd_w_out"] = dram("odd_w_out", [D, D], F32, I)
    for n in ("ln_mix_g", "ln_mix_b", "ln_ffn_g", "ln_ffn_b"):
        A[n] = dram(n, [2, D], F32, I)
    A["ffn_w_up"] = dram("ffn_w_up", [2, D, 2 * DFF], F32, I)
    A["ffn_conv_w"] = dram("ffn_conv_w", [2, 3, DFF], F32, I)
    A["ffn_conv_b"] = dram("ffn_conv_b", [2, DFF], F32, I)
    A["ffn_w_down"] = dram("ffn_w_down", [2, DFF, D], F32, I)
    A["y"] = dram("y", [S, D], F32, "ExternalOutput")
    A["qT_s"] = dram("qT_s", [1024, S], BF16)
    A["kT_s"] = dram("kT_s", [1024, S], BF16)
    A["v_s"] = dram("v_s", [S, 1024], BF16)
    A["rwT_s"] = dram("rwT_s", [3360, S], F32)
    A["oT_s"] = dram("oT_s", [D, S], BF16)
    A["h_a"] = dram("h_a", [S, D], F32)
    A["h_b"] = dram("h_b", [S, D], F32)
    A["hffT_s"] = dram("hffT_s", [4, 128, 44, 512], BF16)
    A["qk1_s"] = dram("qk1_s", [2048, S], F32)
    A["v1_s"] = dram("v1_s", [S, 2048], BF16)
    A["dg1_s"] = dram("dg1_s", [16, S], F32)
    A["r1_s"] = dram("r1_s", [S, 2048], F32)
    A["wdn_b"] = dram("wdn_b", [2, 4, 128, 44, 512], BF16)
    if "dbg_rw" in kinds:
        A["dbg_rw"] = dram("dbg_rw", [len(DBG_RW), 64, 8 * 128], F32)
        A["dbg_rw2"] = dram("dbg_rw2", [len(DBG_RW2), 64, 8 * 64], F32)
    g.A = A
    g.R = {n: Res(n) for n in A}

    with contextlib.ExitStack() as st:
        P = Prog(nc, st)
        g.P = P
        g.ident_f = st.enter_context(nc.sbuf_tensor("ident_f", [128, 128], F32))
        g.ident_b = st.enter_context(nc.sbuf_tensor("ident_b", [128, 128], BF16))
        g.r_ident = Res("ident")
        P.add("pool", lambda e: e.memset(g.ident_f[:], 1.0), writes=[g.r_ident])
        P.add("pool", lambda e: e.affine_select(g.ident_f[:], g.ident_f[:], [[-1, 128]], ALU.is_equal, 0.0,
                                                base=0, channel_multiplier=1), reads=[g.r_ident], writes=[g.r_ident])
        P.add("pool", lambda e: e.tensor_copy(g.ident_b[:], g.ident_f[:]), reads=[g.r_ident], writes=[g.r_ident])
        P.flush()
        allp = ["l0_inproj", "sb_attn", "rwkv", "l0_mixout", "l0_ffn", "l1_inproj", "gla", "l1_mixout", "l1_ffn"]
        ph = allp if phases is None else phases
        if "l0_inproj" in ph:
            phase_l0_inproj(g)
        if "sb_attn" in ph:
            phase_sb_attn(g)
        if "rwkv" in ph:
            phase_rwkv(g)
        if "l0_mixout" in ph:
            phase_proj_ln(g, "oT_s", 16, A["even_w_out"], "x", A["ln_mix_g"][0], A["ln_mix_b"][0], "h_a")
        if "l0_ffn" in ph:
            phase_ffn_up(g, 0, "h_a")
            phase_proj_ln(g, "hffT_s", 44, A["ffn_w_down"][0], "h_a", A["ln_ffn_g"][0], A["ln_ffn_b"][0], "h_b", wb=(A["wdn_b"][0], g.R["wdn_b"]))
        if "l1_inproj" in ph:
            phase_l1_inproj(g)
        if "gla" in ph:
            phase_gla(g)
        if "l1_mixout" in ph:
            phase_proj_ln(g, "oT_s", 16, A["odd_w_out"], "h_b", A["ln_mix_g"][1], A["ln_mix_b"][1], "h_a")
        if "l1_ffn" in ph:
            phase_ffn_up(g, 1, "h_a")
            phase_proj_ln(g, "hffT_s", 44, A["ffn_w_down"][1], "h_a", A["ln_ffn_g"][1], A["ln_ffn_b"][1], "y", wb=(A["wdn_b"][1], g.R["wdn_b"]))
        P.wait_all("sp", list(g.R.values()))
        P.flush()
    return nc


_UID = [0]


class Phase:
    def __init__(self, g):
        self.g = g
        self.st = contextlib.ExitStack()
        self.n = 0

    def __enter__(self):
        self.st.__enter__()
        return self

    def __exit__(self, *a):
        self.g.P.flush()
        return self.st.__exit__(*a)

    def sb(self, shape, dt, name=None):
        _UID[0] += 1
        t = self.st.enter_context(self.g.nc.sbuf_tensor("%s_%d" % (name or "t", _UID[0]), list(shape), dt))
        return t

    def ps(self, shape, dt, name=None):
        _UID[0] += 1
        t = self.st.enter_context(self.g.nc.psum_tensor("%s_%d" % (name or "p", _UID[0]), list(shape), dt))
        return t


def make_xT(g, ph, src_name, xT, xT_res):
    P, A, R = g.P, g.A, g.R
    src = A[src_name]
    xin = [ph.sb([128, D], BF16, "xin") for _ in range(2)]
    r_xin = [Res("xin0"), Res("xin1")]
    ptr = [ph.ps([128, 8, 128], BF16, "ptr") for _ in range(2)]
    r_ptr = [Res("ptr0"), Res("ptr1")]
    k = 0
    for tt in range(16):
        b = tt % 2
        P.dma(xin[b][:], src[tt * 128:(tt + 1) * 128, :], reads=[R[src_name]], writes=[r_xin[b]], q="pool")
        for gi in range(2):
            pb = k % 2
            k += 1
            for j in range(8):
                kc = gi * 8 + j
                P.tr(ptr[pb][:, j, :], xin[b][:, kc * 128:(kc + 1) * 128], g.ident_b[:],
                     reads=[r_xin[b], g.r_ident], writes=[r_ptr[pb]])
            dst = xT[:, gi * 8:(gi + 1) * 8, tt * 128:(tt + 1) * 128]
            if pb == 0:
                P.add("dve", lambda e, dst=dst, s=ptr[pb]: e.tensor_copy(dst, s[:]), reads=[r_ptr[pb]], writes=[xT_res[tt // 4]])
            else:
                P.add("act", lambda e, dst=dst, s=ptr[pb]: e.copy(dst, s[:]), reads=[r_ptr[pb]], writes=[xT_res[tt // 4]])


def load_vec_fm(g, ph, vec, n, dst, dst_res, pt, pt_res, parts=128):
    P = g.P
    nfull = n // parts
    rem = n - nfull * parts
    nch = nfull + (1 if rem else 0)
    tmp = ph.sb([128, parts], F32, "vtmp")
    r_tmp = Res("vtmp")
    P.add("pool", lambda e: e.memset(tmp[:], 0.0), writes=[r_tmp])
    if nfull:
        P.dma(tmp[0:nfull, :], vec[0:nfull * parts].rearrange("(c p) -> c p", p=parts), writes=[r_tmp])
    if rem:
        P.dma(tmp[nfull:nfull + 1, 0:rem], vec[nfull * parts:n].rearrange("(c p) -> c p", c=1), writes=[r_tmp])
    P.tr(pt[0:parts, 0:nch], tmp[0:nch, 0:parts], g.ident_f[0:nch, 0:nch], reads=[r_tmp, g.r_ident], writes=[pt_res])
    P.add("dve", lambda e: e.tensor_copy(dst, pt[0:parts, 0:nch]), reads=[pt_res], writes=[dst_res])
    return nch


def phase_l0_inproj(g):
    P, A, R, nc = g.P, g.A, g.R, g.nc
    with Phase(g) as ph:
        xT = ph.sb([128, 16, S], BF16, "xT")
        xT_res = [Res("xT%d" % i) for i in range(4)]
        w = [ph.sb([128, 16, 512], BF16, "w") for _ in range(2)]
        r_w = [Res("w0"), Res("w1")]
        P.dma(w[0][:, :, 0:512], A["even_w_in"][:, 0:512].rearrange("(kc p) c -> p kc c", p=128), writes=[r_w[0]], q="pool")
        with Phase(g) as ph2:
            make_xT(g, ph2, "x", xT, xT_res)
        pacc = [ph.ps([128, 512], F32, "pacc") for _ in range(4)]
        r_pacc = [Res("pacc%d" % i) for i in range(4)]
        mu = ph.sb([128, 27], F32, "mu")
        omu = ph.sb([128, 27], F32, "omu")
        r_mu = Res("mu")
        ptv = ph.ps([128, 128], F32, "ptv")
        r_ptv = Res("ptv")
        load_vec_fm(g, ph, A["even_shift_mu"], 3360, mu[:, 0:27], r_mu, ptv, r_ptv)
        P.add("dve", lambda e: e.tensor_scalar(omu[:], mu[:], -1.0, 1.0, ALU.mult, ALU.add), reads=[r_mu], writes=[r_mu])
        stage_b = [ph.sb([128, S], BF16, "stb") for _ in range(2)]
        r_stb = [Res("stb0"), Res("stb1")]
        stage_v = [ph.sb([128, 512], BF16, "stv") for _ in range(2)]
        r_stv = [Res("stv0"), Res("stv1")]
        ubuf = [ph.sb([128, S + 1], F32, "ubuf") for _ in range(2)]
        r_ub = [Res("ub0"), Res("ub1")]
        shf = [ph.sb([128, S], F32, "shf") for _ in range(2)]
        r_shf = [Res("shf0"), Res("shf1")]
        for b in range(2):
            P.add("pool", lambda e, b=b: e.memset(ubuf[b][:, 0:1], 0.0), writes=[r_ub[b]])
        W = A["even_w_in"]
        ngroups = (6432 + 511) // 512
        cnt = {"pacc": 0, "chunk": 0, "v": 0}

        def load_w(gi):
            c0 = gi * 512
            gw = min(512, 6432 - c0)
            b = gi % 2
            P.dma(w[b][:, :, 0:gw], W[:, c0:c0 + gw].rearrange("(kc p) c -> p kc c", p=128), writes=[r_w[b]], q="pool")

        for gi in range(ngroups):
            c0 = gi * 512
            gw = min(512, 6432 - c0)
            b = gi % 2
            if gi + 1 < ngroups:
                load_w(gi + 1)
            if 2048 <= c0 < 3072:
                for tt in range(16):
                    pi = cnt["pacc"] % 4
                    cnt["pacc"] += 1
                    for kc in range(16):
                        P.mm(pacc[pi][:, :], xT[:, kc, tt * 128:(tt + 1) * 128], w[b][:, kc, 0:512], start=(kc == 0), stop=(kc == 15),
                             reads=[xT_res[tt // 4], r_w[b]], writes=[r_pacc[pi]])
                    sv = cnt["v"] % 2
                    cnt["v"] += 1
                    P.add("act", lambda e, sv=sv, pi=pi: e.copy(stage_v[sv][:], pacc[pi][:]), reads=[r_pacc[pi]], writes=[r_stv[sv]])
                    P.dma(A["v_s"][tt * 128:(tt + 1) * 128, c0 - 2048:c0 - 2048 + 512], stage_v[sv][:], reads=[r_stv[sv]], writes=[R["v_s"]])
                continue
            nchunk = (gw + 127) // 128
            for j in range(nchunk):
                cw = min(128, gw - j * 128)
                col = c0 + j * 128
                ci = cnt["chunk"] % 2
                cnt["chunk"] += 1
                for qt in range(4):
                    pi = cnt["pacc"] % 4
                    cnt["pacc"] += 1
                    for kc in range(16):
                        P.mm(pacc[pi][0:cw, :], w[b][:, kc, j * 128:j * 128 + cw], xT[:, kc, qt * 512:(qt + 1) * 512], start=(kc == 0), stop=(kc == 15),
                             reads=[xT_res[qt], r_w[b]], writes=[r_pacc[pi]])
                    if col < 2048:
                        sc = (128.0 ** -0.5) if col < 1024 else 1.0
                        P.add("act", lambda e, ci=ci, pi=pi, qt=qt, sc=sc: e.activation(stage_b[ci][:, qt * 512:(qt + 1) * 512], pacc[pi][:], AF.Copy, scale=sc),
                              reads=[r_pacc[pi]], writes=[r_stb[ci]])
                    else:
                        P.add("act", lambda e, ci=ci, pi=pi, qt=qt, cw=cw: e.copy(ubuf[ci][0:cw, 1 + qt * 512:1 + (qt + 1) * 512], pacc[pi][0:cw, :]),
                              reads=[r_pacc[pi]], writes=[r_ub[ci]])
                if col < 2048:
                    dst = A["qT_s"] if col < 1024 else A["kT_s"]
                    dn = "qT_s" if col < 1024 else "kT_s"
                    r0 = col % 1024
                    P.dma(dst[r0:r0 + 128, :], stage_b[ci][:], reads=[r_stb[ci]], writes=[R[dn]])
                else:
                    rc = (col - 3072) // 128
                    P.add("dve", lambda e, ci=ci, rc=rc, cw=cw: e.tensor_scalar(shf[ci][0:cw, :], ubuf[ci][0:cw, 0:S], mu[0:cw, rc:rc + 1], None, ALU.mult),
                          reads=[r_ub[ci], r_mu], writes=[r_shf[ci]])
                    P.add("dve", lambda e, ci=ci, rc=rc, cw=cw: e.scalar_tensor_tensor(shf[ci][0:cw, :], ubuf[ci][0:cw, 1:S + 1], omu[0:cw, rc:rc + 1], shf[ci][0:cw, :], ALU.mult, ALU.add),
                          reads=[r_ub[ci], r_mu, r_shf[ci]], writes=[r_shf[ci]])
                    P.dma(A["rwT_s"][col - 3072:col - 3072 + cw, :], shf[ci][0:cw, :], reads=[r_shf[ci]], writes=[R["rwT_s"]])


def phase_sb_attn(g, NH=8):
    P, A, R, nc = g.P, g.A, g.R, g.nc
    NA = 3
    with Phase(g) as ph:
        qT = [ph.sb([128, S], BF16, "qT") for _ in range(2)]
        kT = [ph.sb([128, S], BF16, "kT") for _ in range(2)]
        vv = [ph.sb([128, 16, 128], BF16, "vv") for _ in range(2)]
        r_in = [Res("sbin0"), Res("sbin1")]
        tri = ph.sb([128, 128], F32, "tri")
        ones = ph.sb([128, 128], F32, "ones")
        r_c = Res("sbconst")
        P.add("pool", lambda e: e.memset(tri[:], 1.0), writes=[r_c])
        P.add("pool", lambda e: e.memset(ones[:], 1.0), writes=[r_c])
        P.add("pool", lambda e: e.affine_select(tri[:], tri[:], [[-1, 128]], ALU.is_ge, 0.0, base=0, channel_multiplier=1),
              reads=[r_c], writes=[r_c])
        pz = [ph.ps([128, 512], F32, "pz") for _ in range(NA)]
        r_pz = [Res("pz%d" % i) for i in range(NA)]
        zs = [ph.sb([128, 512], F32, "zs") for _ in range(NA)]
        r_zs = [Res("zs%d" % i) for i in range(NA)]
        sp = [ph.sb([128, 512], F32, "sp") for _ in range(NA)]
        r_sp = [Res("sp%d" % i) for i in range(NA)]
        pt = [ph.ps([128, 512], F32, "pt") for _ in range(2)]
        r_pt = [Res("pt0"), Res("pt1")]
        po = [ph.ps([128, 512], F32, "po") for _ in range(2)]
        r_po = [Res("po0"), Res("po1")]
        sacc = [ph.sb([128, 512], F32, "sacc") for _ in range(2)]
        r_sacc = [Res("sacc0"), Res("sacc1")]
        ee = [ph.sb([128, 512], F32, "ee") for _ in range(2)]
        r_ee = [Res("ee0"), Res("ee1")]
        ww = [ph.sb([128, 512], BF16, "ww") for _ in range(3)]
        r_ww = [Res("ww0"), Res("ww1"), Res("ww2")]
        osb = [ph.sb([128, S], BF16, "osb") for _ in range(2)]
        r_osb = [Res("osb0"), Res("osb1")]

        def load(h):
            b = h % 2
            P.dma(qT[b][:], A["qT_s"][h * 128:(h + 1) * 128, :], reads=[R["qT_s"]], writes=[r_in[b]])
            P.dma(kT[b][:], A["kT_s"][h * 128:(h + 1) * 128, :], reads=[R["kT_s"]], writes=[r_in[b]])
            P.dma(vv[b][:], A["v_s"][:, h * 128:(h + 1) * 128].rearrange("(t p) d -> p t d", p=128), reads=[R["v_s"]], writes=[r_in[b]])

        steps = []
        gq = 0
        for h in range(NH):
            for qt in range(4):
                kbs = list(range(4 * qt + 3, -1, -1))
                for ki, kb in enumerate(kbs):
                    steps.append((h, qt, ki, kb, len(kbs), gq))
                gq += 1

        def stageA(i):
            h, qt, ki, kb, nk, gq = steps[i]
            a = i % NA
            b = h % 2
            t0, s0 = qt * 512, kb * 128
            if i == 0:
                load(0)
            P.mm(pz[a][:], kT[b][:, s0:s0 + 128], qT[b][:, t0:t0 + 512], reads=[r_in[b]], writes=[r_pz[a]])
            P.act(sp[a][:], pz[a][:], AF.Exp, reads=[r_pz[a]], writes=[r_sp[a]])
            P.add("act", lambda e: e.copy(zs[a][:], pz[a][:]), reads=[r_pz[a]], writes=[r_zs[a]])
            P.act(sp[a][:], sp[a][:], AF.Ln, reads=[r_sp[a]], writes=[r_sp[a]], bias=1.0)
            if kb >= 4 * qt:
                P.add("pool", lambda e: e.affine_select(sp[a][:], sp[a][:], [[1, 512]], ALU.is_gt, 0.0, base=t0 - s0, channel_multiplier=-1),
                      reads=[r_sp[a]], writes=[r_sp[a]])

        def stageB(i):
            h, qt, ki, kb, nk, gq = steps[i]
            a = i % NA
            i2 = i % 2
            b = h % 2
            t0, s0 = qt * 512, kb * 128
            ob = gq % 2
            sa, r_sa = sacc[gq % 2], r_sacc[gq % 2]
            P.mm(pt[i2][:], tri[:], sp[a][:], start=True, stop=(ki == 0), reads=[r_sp[a], r_c], writes=[r_pt[i2]])
            if ki > 0:
                P.mm(pt[i2][:], ones[:], sa[:], start=False, stop=True, reads=[r_sa, r_c], writes=[r_pt[i2]])
            P.add("dve", lambda e: e.tensor_tensor(ee[i2][:], zs[a][:], pt[i2][:], ALU.subtract), reads=[r_zs[a], r_pt[i2]], writes=[r_ee[i2]])
            i3 = i % 3
            P.act(ww[i3][:], ee[i2][:], AF.Exp, reads=[r_ee[i2]], writes=[r_ww[i3]])
            if kb >= 4 * qt:
                P.add("pool", lambda e: e.affine_select(ww[i3][:], ww[i3][:], [[1, 512]], ALU.is_gt, 0.0, base=t0 - s0, channel_multiplier=-1),
                      reads=[r_ww[i3]], writes=[r_ww[i3]])
            if ki == 0:
                P.add("pool", lambda e: e.tensor_copy(sa[:], sp[a][:]), reads=[r_sp[a]], writes=[r_sa])
            elif ki + 1 < nk:
                P.add("pool", lambda e: e.tensor_tensor(sa[:], sa[:], sp[a][:], ALU.add), reads=[r_sp[a], r_sa], writes=[r_sa])

        def stageC(i):
            h, qt, ki, kb, nk, gq = steps[i]
            i3 = i % 3
            b = h % 2
            t0 = qt * 512
            ob = gq % 2
            P.mm(po[ob][:], vv[b][:, kb, :], ww[i3][:], start=(ki == 0), stop=(ki == nk - 1), reads=[r_ww[i3], r_in[b]], writes=[r_po[ob]])
            if ki == nk - 1:
                P.add("act", lambda e: e.copy(osb[b][:, t0:t0 + 512], po[ob][:]), reads=[r_po[ob]], writes=[r_osb[b]])
                if qt == 3:
                    P.dma(A["oT_s"][h * 128:(h + 1) * 128, :], osb[b][:], reads=[r_osb[b]], writes=[R["oT_s"]])

        for i in range(len(steps) + 2):
            if i < len(steps):
                stageA(i)
            if 1 <= i <= len(steps):
                stageB(i - 1)
            if i >= 2:
                stageC(i - 2)
            j = i - 1
            if 0 <= j < len(steps) and steps[j][1] == 0 and steps[j][2] == 0 and steps[j][0] + 1 < NH:
                load(steps[j][0] + 1)


def bcast_vec(g, ph, vec, n, name):
    t = ph.sb([128, n], F32, name)
    r = Res(name)
    g.P.dma(t[:], vec.partition_broadcast(128), writes=[r])
    return t, r


def layer_norm_tile(g, pre, r_pre, gam, bet, r_gb, out, r_out, stats, mv, rstd, r_tmp, negh, n=D, eps=LN_EPS):
    P = g.P
    nch = n // 512
    for c in range(nch):
        P.add("dve", lambda e, c=c: e.bn_stats(stats[:, c, :], pre[:, c * 512:(c + 1) * 512]), reads=[r_pre], writes=[r_tmp])
    P.add("dve", lambda e: e.bn_aggr(mv[:], stats[:, 0:nch, :]), reads=[r_tmp], writes=[r_tmp])
    P.add("pool", lambda e: e.tensor_scalar(rstd[:], mv[:, 1:2], eps, None, ALU.add), reads=[r_tmp], writes=[r_tmp])
    P.add("pool", lambda e: e.tensor_tensor(rstd[:], rstd[:], negh, ALU.pow), reads=[r_tmp, r_gb], writes=[r_tmp])
    P.add("dve", lambda e: e.tensor_scalar(out, pre, mv[:, 0:1], rstd[:, 0:1], ALU.subtract, ALU.mult), reads=[r_pre, r_tmp], writes=[r_out])
    P.add("dve", lambda e: e.tensor_tensor(out, out, gam, ALU.mult), reads=[r_gb, r_out], writes=[r_out])
    P.add("pool", lambda e: e.tensor_tensor(out, out, bet, ALU.add), reads=[r_gb, r_out], writes=[r_out])


def phase_proj_ln(g, a_name, KC, Wd, resid_name, ln_g, ln_b, out_name, wb=None):
    P, A, R, nc = g.P, g.A, g.R, g.nc
    CG = 512
    ncg = D // CG
    resident = KC <= 16
    with Phase(g) as ph:
        naT = 2 if resident else 1
        aT = [ph.sb([128, KC, 512], BF16, "aT") for _ in range(naT)]
        r_aT = [Res("aT%d" % i) for i in range(naT)]
        nw = ncg if resident else 2
        w = [ph.sb([128, KC, CG], BF16, "wp") for _ in range(nw)]
        r_w = [Res("wp%d" % i) for i in range(nw)]
        npre = 2 if resident else 1
        pre2 = [ph.sb([128, 4, D], F32, "pre") for _ in range(npre)]
        r_pre2 = [[Res("pre%d_%d" % (j, i)) for i in range(4)] for j in range(npre)]
        gam, r_g = bcast_vec(g, ph, ln_g, D, "gam")
        bet, r_b = bcast_vec(g, ph, ln_b, D, "bet")
        r_gb = Res("gb")
        negh = ph.sb([128, 1], F32, "negh")
        P.add("pool", lambda e: e.memset(negh[:], -0.5), reads=[r_g, r_b], writes=[r_gb])
        P.add("dve", None, reads=[r_g, r_b, r_gb])
        stats = [ph.sb([128, 4, 6], F32, "stats") for _ in range(4)]
        mv = [ph.sb([128, 2], F32, "mv") for _ in range(4)]
        rstd = [ph.sb([128, 1], F32, "rstd") for _ in range(4)]
        r_tmp = [Res("lntmp%d" % i) for i in range(4)]
        pacc = [ph.ps([128, 512], F32, "pp") for _ in range(4)]
        r_pacc = [Res("pp%d" % i) for i in range(4)]
        cnt = 0

        def load_w(i):
            cg = i % ncg
            b = cg if resident else i % 2
            if wb is None:
                P.dma(w[b][:], Wd[:, cg * CG:(cg + 1) * CG].rearrange("(kc p) c -> p kc c", p=128), writes=[r_w[b]], q="pool")
            else:
                P.dma(w[b][:], wb[0][cg], reads=[wb[1]], writes=[r_w[b]], q="act")

        def load_a(tg):
            if a_name == "hffT_s":
                P.dma(aT[tg % naT][:], A[a_name][tg], reads=[R[a_name]], writes=[r_aT[tg % naT]])
            else:
                P.dma(aT[tg % naT][:], A[a_name][:, tg * 512:(tg + 1) * 512].rearrange("(kc p) t -> p kc t", p=128), reads=[R[a_name]], writes=[r_aT[tg % naT]])

        total = 4 * ncg
        if resident:
            for i in range(ncg):
                load_w(i)
        else:
            load_w(0)
        load_a(0)
        for tg in range(4):
            ab = tg % naT
            pre = pre2[tg % npre]
            r_pre = r_pre2[tg % npre]
            if naT > 1 and tg + 1 < 4:
                load_a(tg + 1)
            for cg in range(ncg):
                i = tg * ncg + cg
                b = cg if resident else i % 2
                if not resident and i + 1 < total:
                    load_w(i + 1)
                for tt in range(4):
                    pi = cnt % 4
                    cnt += 1
                    for kc in range(KC):
                        P.mm(pacc[pi][:, 0:CG], aT[ab][:, kc, tt * 128:(tt + 1) * 128], w[b][:, kc, :], start=(kc == 0), stop=(kc == KC - 1),
                             reads=[r_aT[ab], r_w[b]], writes=[r_pacc[pi]])
                    P.add("act", lambda e: e.activation(pre[:, tt, cg * CG:(cg + 1) * CG], pacc[pi][:, 0:CG], AF.Copy, scale=1.0 / ALPHA),
                          reads=[r_pacc[pi]], writes=[r_pre[tt]])
            if naT == 1 and tg + 1 < 4:
                load_a(tg + 1)
            for tt in range(4):
                tok = tg * 4 + tt
                P.add("pool", lambda e: e.dma_start(out=pre[:, tt, :], in_=A[resid_name][tok * 128:(tok + 1) * 128, :], accum_op=ALU.add),
                      reads=[R[resid_name]], writes=[r_pre[tt]], dma=True)
            for tt in range(4):
                tok = tg * 4 + tt
                layer_norm_tile(g, pre[:, tt, :], r_pre[tt], gam[:], bet[:], r_gb, pre[:, tt, :], r_pre[tt], stats[tt], mv[tt], rstd[tt], r_tmp[tt], negh[:],
                                eps=LN_EPS / (ALPHA * ALPHA))
                P.dma(A[out_name][tok * 128:(tok + 1) * 128, :], pre[:, tt, :], reads=[r_pre[tt]], writes=[R[out_name]], q="pool")


def phase_ffn_up(g, layer, h_name):
    P, A, R, nc = g.P, g.A, g.R, g.nc
    Wu = A["ffn_w_up"][layer]
    with Phase(g) as ph:
        xT = ph.sb([128, 16, S], BF16, "hT")
        xT_res = [Res("hT%d" % i) for i in range(4)]
        with Phase(g) as ph2:
            make_xT(g, ph2, h_name, xT, xT_res)
        for cg in range(4):
            P.dma(A["wdn_b"][layer, cg], A["ffn_w_down"][layer][:, cg * 512:(cg + 1) * 512].rearrange("(kc p) c -> p kc c", p=128),
                  writes=[R["wdn_b"]], q="pool")
        w = [ph.sb([128, 16, 256], BF16, "wu") for _ in range(2)]
        r_w = [Res("wu0"), Res("wu1")]
        cw = ph.sb([128, 3, 44], F32, "cw")
        cb = ph.sb([128, 44], F32, "cb")
        r_cw = Res("cw")
        ptv = ph.ps([128, 128], F32, "ptv")
        r_ptv = Res("ptv")
        for j in range(3):
            load_vec_fm(g, ph, A["ffn_conv_w"][layer, j], DFF, cw[:, j, :], r_cw, ptv, r_ptv)
        load_vec_fm(g, ph, A["ffn_conv_b"][layer], DFF, cb[:, :], r_cw, ptv, r_ptv)
        pg = [ph.ps([128, 512], F32, "pg") for _ in range(2)]
        r_pg = [Res("pg0"), Res("pg1")]
        pu = [ph.ps([128, 512], F32, "pu") for _ in range(2)]
        r_pu = [Res("pu0"), Res("pu1")]
        gbuf = [ph.sb([128, S + 2], F32, "gbuf") for _ in range(2)]
        r_gb = [Res("gbuf0"), Res("gbuf1")]
        ubuf = [ph.sb([128, S], F32, "ubuf") for _ in range(2)]
        r_ub = [Res("fub0"), Res("fub1")]
        t1 = [ph.sb([128, S], F32, "t1") for _ in range(2)]
        r_t1 = [Res("t10"), Res("t11")]
        hf = [ph.sb([128, S], BF16, "hf") for _ in range(2)]
        r_hf = [Res("hf0"), Res("hf1")]
        for b in range(2):
            P.add("pool", lambda e, b=b: e.memset(gbuf[b][:, 0:2], 0.0), writes=[r_gb[b]])

        def load_w(c):
            b = c % 2
            P.dma(w[b][:, :, 0:128], Wu[:, c * 128:(c + 1) * 128].rearrange("(kc p) c -> p kc c", p=128), writes=[r_w[b]], q="pool")
            P.dma(w[b][:, :, 128:256], Wu[:, DFF + c * 128:DFF + (c + 1) * 128].rearrange("(kc p) c -> p kc c", p=128), writes=[r_w[b]], q="pool")

        load_w(0)
        k = 0
        for c in range(44):
            b = c % 2
            if c + 1 < 44:
                load_w(c + 1)
            for qt in range(4):
                pi = k % 2
                k += 1
                for kc in range(16):
                    P.mm(pg[pi][:], w[b][:, kc, 0:128], xT[:, kc, qt * 512:(qt + 1) * 512], start=(kc == 0), stop=(kc == 15),
                         reads=[xT_res[qt], r_w[b]], writes=[r_pg[pi]])
                for kc in range(16):
                    P.mm(pu[pi][:], w[b][:, kc, 128:256], xT[:, kc, qt * 512:(qt + 1) * 512], start=(kc == 0), stop=(kc == 15),
                         reads=[xT_res[qt], r_w[b]], writes=[r_pu[pi]])
                P.add("act", lambda e, b=b, pi=pi, qt=qt: e.copy(gbuf[b][:, 2 + qt * 512:2 + (qt + 1) * 512], pg[pi][:]), reads=[r_pg[pi]], writes=[r_gb[b]])
                P.add("dve", lambda e, b=b, pi=pi, qt=qt: e.tensor_copy(ubuf[b][:, qt * 512:(qt + 1) * 512], pu[pi][:]), reads=[r_pu[pi]], writes=[r_ub[b]])
            P.add("pool", lambda e, b=b, c=c: e.tensor_scalar(t1[b][:], gbuf[b][:, 0:S], cw[:, 0, c:c + 1], cb[:, c:c + 1], ALU.mult, ALU.add),
                  reads=[r_gb[b], r_cw], writes=[r_t1[b]])
            P.add("dve", lambda e, b=b, c=c: e.scalar_tensor_tensor(t1[b][:], gbuf[b][:, 1:S + 1], cw[:, 1, c:c + 1], t1[b][:], ALU.mult, ALU.add),
                  reads=[r_gb[b], r_cw, r_t1[b]], writes=[r_t1[b]])
            P.add("dve", lambda e, b=b, c=c: e.scalar_tensor_tensor(t1[b][:], gbuf[b][:, 2:S + 2], cw[:, 2, c:c + 1], t1[b][:], ALU.mult, ALU.add),
                  reads=[r_gb[b], r_cw, r_t1[b]], writes=[r_t1[b]])
            P.act(t1[b][:], t1[b][:], AF.Gelu, reads=[r_t1[b]], writes=[r_t1[b]])
            P.add("pool", lambda e, b=b: e.tensor_tensor(hf[b][:], t1[b][:], ubuf[b][:], ALU.mult), reads=[r_t1[b], r_ub[b]], writes=[r_hf[b]])
            for tg in range(4):
                P.dma(A["hffT_s"][tg, :, c, :], hf[b][:, tg * 512:(tg + 1) * 512], reads=[r_hf[b]], writes=[R["hffT_s"]])


def phase_l1_inproj(g):
    P, A, R, nc = g.P, g.A, g.R, g.nc
    W = A["odd_w_in"]
    groups = [(c, 512, "fm") for c in range(0, 2048, 512)] + [(c, 512, "v") for c in range(2048, 4096, 512)] \
        + [(4096, 16, "dg")] + [(c, 512, "r") for c in range(4112, 6160, 512)]
    with Phase(g) as ph:
        xT = ph.sb([128, 16, S], BF16, "xT1")
        xT_res = [Res("xT1%d" % i) for i in range(4)]
        w = [ph.sb([128, 16, 512], BF16, "w1") for _ in range(2)]
        r_w = [Res("w10"), Res("w11")]
        P.dma(w[0][:, :, 0:512], W[:, 0:512].rearrange("(kc p) c -> p kc c", p=128), writes=[r_w[0]], q="pool")
        with Phase(g) as ph2:
            make_xT(g, ph2, "h_b", xT, xT_res)
        pacc = [ph.ps([128, 512], F32, "pacc") for _ in range(4)]
        r_pacc = [Res("pacc%d" % i) for i in range(4)]
        stf = [ph.sb([128, S], F32, "stf") for _ in range(2)]
        r_stf = [Res("stf0"), Res("stf1")]
        stv = [ph.sb([128, 512], BF16, "stv") for _ in range(2)]
        r_stv = [Res("stv0"), Res("stv1")]
        strr = [ph.sb([128, 512], F32, "str") for _ in range(2)]
        r_str = [Res("str0"), Res("str1")]
        cnt = {"pacc": 0, "chunk": 0, "v": 0}

        def load_w(gi):
            c0, gw, _ = groups[gi]
            b = gi % 2
            P.dma(w[b][:, :, 0:gw], W[:, c0:c0 + gw].rearrange("(kc p) c -> p kc c", p=128), writes=[r_w[b]], q="pool")

        for gi, (c0, gw, kind) in enumerate(groups):
            b = gi % 2
            if gi + 1 < len(groups):
                load_w(gi + 1)
            if kind in ("v", "r"):
                for tt in range(16):
                    pi = cnt["pacc"] % 4
                    cnt["pacc"] += 1
                    for kc in range(16):
                        P.mm(pacc[pi][:, :], xT[:, kc, tt * 128:(tt + 1) * 128], w[b][:, kc, 0:512], start=(kc == 0), stop=(kc == 15),
                             reads=[xT_res[tt // 4], r_w[b]], writes=[r_pacc[pi]])
                    sv = cnt["v"] % 2
                    cnt["v"] += 1
                    if kind == "v":
                        P.add("act", lambda e, sv=sv, pi=pi: e.copy(stv[sv][:], pacc[pi][:]), reads=[r_pacc[pi]], writes=[r_stv[sv]])
                        P.dma(A["v1_s"][tt * 128:(tt + 1) * 128, c0 - 2048:c0 - 2048 + 512], stv[sv][:], reads=[r_stv[sv]], writes=[R["v1_s"]])
                    else:
                        P.add("dve", lambda e, sv=sv, pi=pi: e.tensor_copy(strr[sv][:], pacc[pi][:]), reads=[r_pacc[pi]], writes=[r_str[sv]])
                        P.dma(A["r1_s"][tt * 128:(tt + 1) * 128, c0 - 4112:c0 - 4112 + 512], strr[sv][:], reads=[r_str[sv]], writes=[R["r1_s"]])
                continue
            nchunk = (gw + 127) // 128
            for j in range(nchunk):
                cw = min(128, gw - j * 128)
                col = c0 + j * 128
                ci = cnt["chunk"] % 2
                cnt["chunk"] += 1
                for qt in range(4):
                    pi = cnt["pacc"] % 4
                    cnt["pacc"] += 1
                    for kc in range(16):
                        P.mm(pacc[pi][0:cw, :], w[b][:, kc, j * 128:j * 128 + cw], xT[:, kc, qt * 512:(qt + 1) * 512], start=(kc == 0), stop=(kc == 15),
                             reads=[xT_res[qt], r_w[b]], writes=[r_pacc[pi]])
                    P.add("act", lambda e, ci=ci, pi=pi, qt=qt, cw=cw: e.copy(stf[ci][0:cw, qt * 512:(qt + 1) * 512], pacc[pi][0:cw, :]),
                          reads=[r_pacc[pi]], writes=[r_stf[ci]])
                if kind == "fm":
                    P.dma(A["qk1_s"][col:col + 128, :], stf[ci][:], reads=[r_stf[ci]], writes=[R["qk1_s"]])
                else:
                    P.dma(A["dg1_s"][0:16, :], stf[ci][0:16, :], reads=[r_stf[ci]], writes=[R["dg1_s"]])


def phase_gla(g):
    P, A, R, nc = g.P, g.A, g.R, g.nc
    SC = 1.0 / 16.0
    with Phase(g) as ph:
        gw2 = ph.sb([16, 1024], F32, "gw2")
        dgT = ph.sb([16, S], F32, "dgT")
        r_c = Res("glac")
        P.dma(gw2[:], A["gla_gate_w2"], writes=[r_c])
        P.dma(dgT[:], A["dg1_s"], reads=[R["dg1_s"]], writes=[r_c])
        ngb = ph.sb([128, 8], F32, "ngb")
        ptv = ph.ps([128, 128], F32, "ptv")
        r_ptv = Res("ptv")
        load_vec_fm(g, ph, A["gla_gate_b"], 1024, ngb[:, :], r_c, ptv, r_ptv)
        P.add("dve", lambda e: e.tensor_scalar(ngb[:], ngb[:], -1.0, None, ALU.mult), reads=[r_c], writes=[r_c])
        msk = ph.sb([128, 128], F32, "msk")
        P.add("pool", lambda e: e.memset(msk[:], 1.0), writes=[r_c])
        P.add("pool", lambda e: e.affine_select(msk[:], msk[:], [[1, 128]], ALU.is_ge, 0.0, base=0, channel_multiplier=-1),
              reads=[r_c], writes=[r_c])
        gnegh = ph.sb([128, 1], F32, "gnegh")
        P.add("pool", lambda e: e.memset(gnegh[:], -0.5), writes=[r_c])
        rst = ph.sb([128, 16, 128], F32, "rst")
        P.add("pool", lambda e: e.memset(rst[:], 1.0), writes=[r_c])
        P.add("pool", lambda e: e.memset(rst[:, :, 0:1], 0.0), reads=[r_c], writes=[r_c])
        rstf = rst[:].rearrange("p c l -> p (c l)")
        qf = ph.sb([128, S], F32, "qf")
        kf = ph.sb([128, S], F32, "kf")
        r_qk = Res("qkf")
        csp = ph.sb([128, 16, 128], F32, "csp")
        cspf = csp[:].rearrange("p c l -> p (c l)")
        r_csp = Res("csp")
        tmp = ph.sb([128, 16, 128], F32, "gtmp")
        tmpf = tmp[:].rearrange("p c l -> p (c l)")
        r_tmp = Res("gtmp")
        qd = ph.sb([128, 2, S], BF16, "qd")
        kinv = ph.sb([128, 2, S], BF16, "kinv")
        kendT = ph.sb([128, 2, S], BF16, "kendT")
        r_qd, r_kinv, r_kendT = Res("qd"), Res("kinv"), Res("kendT")
        kend = ph.sb([128, 16, 256], BF16, "kend")
        r_kend = Res("kend")
        dec = ph.sb([128, 2, 16], F32, "dec")
        r_dec = Res("dec")
        vv = ph.sb([128, 16, 512], BF16, "vv1")
        r_vv = Res("vv1")
        S_f = ph.sb([128, 2, 512], F32, "S_f")
        S_b = ph.sb([128, 2, 512], BF16, "S_b")
        r_Sf, r_Sb = Res("S_f"), Res("S_b")
        oT_h = ph.sb([128, 4, S], BF16, "oT_h")
        r_oTh = Res("oT_h")
        gng = ph.sb([128, 512], F32, "gng")
        gnb = ph.sb([128, 512], F32, "gnb")
        rbb = ph.sb([128, 512], F32, "rbb")
        r_hc = Res("headc")
        pA = [ph.ps([128, 512], F32, "pA") for _ in range(2)]
        r_pA = [Res("pA0"), Res("pA1")]
        patt = ph.ps([128, 128], F32, "patt")
        r_patt = Res("patt")
        po = [ph.ps([128, 512], F32, "po1") for _ in range(2)]
        r_po = [Res("po10"), Res("po11")]
        ptr = ph.ps([128, 8, 128], BF16, "ptr1")
        r_ptr = Res("ptr1")
        att_b = ph.sb([128, 128], BF16, "att_b")
        r_att = Res("att_b")
        NSL = 4
        o_sb = [ph.sb([128, 512], F32, "o_sb") for _ in range(NSL)]
        r_osb = [Res("o_sb%d" % i) for i in range(NSL)]
        rt = [ph.sb([128, 512], F32, "rt") for _ in range(NSL)]
        r_rt = [Res("rt%d" % i) for i in range(NSL)]
        og = [ph.sb([128, 512], BF16, "og") for _ in range(NSL)]
        r_og = [Res("og%d" % i) for i in range(NSL)]
        stats = [ph.sb([128, 1, 6], F32, "gstats") for _ in range(NSL)]
        mv = [ph.sb([128, 2], F32, "gmv") for _ in range(NSL)]
        rstd = [ph.sb([128, 1], F32, "grstd") for _ in range(NSL)]
        r_st = [Res("gst%d" % i) for i in range(NSL)]
        k = 0
        for h in range(4):
            P.dma(gng[:], A["gla_gn_g"][h * 512:(h + 1) * 512].partition_broadcast(128), writes=[r_hc])
            P.dma(gnb[:], A["gla_gn_b"][h * 512:(h + 1) * 512].partition_broadcast(128), writes=[r_hc])
            P.dma(rbb[:], A["gla_r_b"][h * 512:(h + 1) * 512].partition_broadcast(128), writes=[r_hc])
            P.dma(vv[:], A["v1_s"][:, h * 512:(h + 1) * 512].rearrange("(c p) v -> p c v", p=128), reads=[R["v1_s"]], writes=[r_vv])
            for dc in range(2):
                ch = h * 2 + dc
                P.dma(qf[:], A["qk1_s"][ch * 128:(ch + 1) * 128, :], reads=[R["qk1_s"]], writes=[r_qk])
                P.dma(kf[:], A["qk1_s"][1024 + ch * 128:1024 + (ch + 1) * 128, :], reads=[R["qk1_s"]], writes=[r_qk])
                for qt in range(4):
                    pi = k % 2
                    k += 1
                    P.mm(pA[pi][:], gw2[0:16, ch * 128:(ch + 1) * 128], dgT[0:16, qt * 512:(qt + 1) * 512], reads=[r_c], writes=[r_pA[pi]])
                    P.act(tmpf[:, qt * 512:(qt + 1) * 512], pA[pi][:], AF.Exp, reads=[r_pA[pi], r_c], writes=[r_tmp], scale=-1.0, bias=ngb[:, ch:ch + 1])
                P.act(tmpf, tmpf, AF.Ln, reads=[r_tmp], writes=[r_tmp], bias=1.0)
                P.add("dve", lambda e: e.tensor_tensor_scan(cspf, rstf, tmpf, 0.0, ALU.mult, ALU.add), reads=[r_tmp, r_c], writes=[r_csp])
                P.act(tmpf, cspf, AF.Exp, reads=[r_csp], writes=[r_tmp], scale=-SC)
                P.add("dve", lambda e, dc=dc: e.scalar_tensor_tensor(qd[:, dc, :], tmpf, SC, qf[:], ALU.mult, ALU.mult), reads=[r_tmp, r_qk], writes=[r_qd])
                P.act(dec[:, dc, :], csp[:, :, 127], AF.Exp, reads=[r_csp], writes=[r_dec], scale=-SC)
                P.act(tmpf, cspf, AF.Exp, reads=[r_csp], writes=[r_tmp], scale=SC)
                P.add("dve", lambda e, dc=dc: e.tensor_tensor(kinv[:, dc, :], tmpf, kf[:], ALU.mult), reads=[r_tmp, r_qk], writes=[r_kinv])
                P.add("dve", lambda e: e.tensor_tensor(tmp[:], csp[:], csp[:, :, 127:128].to_broadcast([128, 16, 128]), ALU.subtract),
                      reads=[r_csp], writes=[r_tmp])
                P.act(tmpf, tmpf, AF.Exp, reads=[r_tmp], writes=[r_tmp], scale=SC)
                P.add("dve", lambda e, dc=dc: e.tensor_tensor(kendT[:, dc, :], tmpf, kf[:], ALU.mult), reads=[r_tmp, r_qk], writes=[r_kendT])
                for half in range(2):
                    for j in range(8):
                        c = half * 8 + j
                        P.tr(ptr[:, j, :], kendT[:, dc, c * 128:(c + 1) * 128], g.ident_b[:], reads=[r_kendT, g.r_ident], writes=[r_ptr])
                    P.add("dve", lambda e, dc=dc, half=half: e.tensor_copy(kend[:, half * 8:(half + 1) * 8, dc * 128:(dc + 1) * 128], ptr[:]),
                          reads=[r_ptr], writes=[r_kend])
            kctr = [0]

            def gla_stage_a(c):
                cs = slice(c * 128, (c + 1) * 128)
                ob = c % 2
                for dc in range(2):
                    P.mm(patt[:], kinv[:, dc, cs], qd[:, dc, cs], start=(dc == 0), stop=(dc == 1), reads=[r_kinv, r_qd], writes=[r_patt])
                P.add("dve", lambda e: e.tensor_tensor(att_b[:], patt[:], msk[:], ALU.mult), reads=[r_patt, r_c], writes=[r_att])
                P.mm(po[ob][:], att_b[:], vv[:, c, :], start=True, stop=(c == 0), reads=[r_att, r_vv], writes=[r_po[ob]])
                if c > 0:
                    for dc in range(2):
                        P.mm(po[ob][:], qd[:, dc, cs], S_b[:, dc, :], start=False, stop=(dc == 1), reads=[r_qd, r_Sb], writes=[r_po[ob]])
                if c < 15:
                    for dc in range(2):
                        pi = kctr[0] % 2
                        kctr[0] += 1
                        P.mm(pA[pi][:], kend[:, c, dc * 128:(dc + 1) * 128], vv[:, c, :], reads=[r_kend, r_vv], writes=[r_pA[pi]])
                        if c == 0:
                            P.add("dve", lambda e: e.tensor_copy(S_b[:, dc, :], pA[pi][:]), reads=[r_pA[pi]], writes=[r_Sb])
                            P.add("dve", lambda e: e.tensor_copy(S_f[:, dc, :], pA[pi][:]), reads=[r_pA[pi]], writes=[r_Sf])
                        else:
                            P.add("dve", lambda e: e.scalar_tensor_tensor(S_b[:, dc, :], S_f[:, dc, :], dec[:, dc, c:c + 1], pA[pi][:], ALU.mult, ALU.add),
                                  reads=[r_pA[pi], r_Sf, r_dec], writes=[r_Sb])
                            P.add("dve", lambda e: e.scalar_tensor_tensor(S_f[:, dc, :], S_f[:, dc, :], dec[:, dc, c:c + 1], pA[pi][:], ALU.mult, ALU.add),
                                  reads=[r_pA[pi], r_Sf, r_dec], writes=[r_Sf])

            def gla_s0(c):
                cs = slice(c * 128, (c + 1) * 128)
                ob = c % 2
                sl = c % NSL
                P.add("dve", lambda e: e.tensor_copy(o_sb[sl][:], po[ob][:]), reads=[r_po[ob]], writes=[r_osb[sl]])
                P.dma(rt[sl][:], A["r1_s"][cs, h * 512:(h + 1) * 512], reads=[R["r1_s"]], writes=[r_rt[sl]])
                P.add("pool", lambda e: e.tensor_tensor(rt[sl][:], rt[sl][:], rbb[:], ALU.add), reads=[r_rt[sl], r_hc], writes=[r_rt[sl]])
                P.act(rt[sl][:], rt[sl][:], AF.Silu, reads=[r_rt[sl]], writes=[r_rt[sl]])
                P.add("dve", lambda e: e.bn_stats(stats[sl][:, 0, :], o_sb[sl][:]), reads=[r_osb[sl]], writes=[r_st[sl]])
                P.add("dve", lambda e: e.bn_aggr(mv[sl][:], stats[sl][:, 0:1, :]), reads=[r_st[sl]], writes=[r_st[sl]])

            def gla_s1(c):
                sl = c % NSL
                P.add("pool", lambda e: e.tensor_scalar(rstd[sl][:], mv[sl][:, 1:2], LN_EPS, None, ALU.add), reads=[r_st[sl]], writes=[r_st[sl]])
                P.add("pool", lambda e: e.tensor_tensor(rstd[sl][:], rstd[sl][:], gnegh[:], ALU.pow), reads=[r_st[sl], r_c], writes=[r_st[sl]])
                P.add("dve", lambda e: e.tensor_scalar(o_sb[sl][:], o_sb[sl][:], mv[sl][:, 0:1], rstd[sl][:, 0:1], ALU.subtract, ALU.mult),
                      reads=[r_st[sl], r_osb[sl]], writes=[r_osb[sl]])

            def gla_s2(c):
                sl = c % NSL
                P.add("pool", lambda e: e.tensor_tensor(o_sb[sl][:], o_sb[sl][:], gng[:], ALU.mult), reads=[r_hc, r_osb[sl]], writes=[r_osb[sl]])
                P.add("pool", lambda e: e.tensor_tensor(o_sb[sl][:], o_sb[sl][:], gnb[:], ALU.add), reads=[r_hc, r_osb[sl]], writes=[r_osb[sl]])
                P.add("dve", lambda e: e.tensor_tensor(og[sl][:], o_sb[sl][:], rt[sl][:], ALU.mult), reads=[r_osb[sl], r_rt[sl]], writes=[r_og[sl]])

            def gla_s3(c):
                cs = slice(c * 128, (c + 1) * 128)
                sl = c % NSL
                for vc in range(4):
                    P.tr(ptr[:, vc, :], og[sl][:, vc * 128:(vc + 1) * 128], g.ident_b[:], reads=[r_og[sl], g.r_ident], writes=[r_ptr])
                P.add("dve", lambda e: e.tensor_copy(oT_h[:, :, cs], ptr[:, 0:4, :]), reads=[r_ptr], writes=[r_oTh])

            for c in range(16 + 4):
                if c < 16:
                    gla_stage_a(c)
                for k_, fn_ in enumerate((gla_s0, gla_s1, gla_s2, gla_s3)):
                    cc_ = c - 1 - k_
                    if 0 <= cc_ < 16:
                        fn_(cc_)
            P.dma(A["oT_s"][h * 512:(h + 1) * 512, :].rearrange("(vc p) t -> p vc t", p=128), oT_h[:], reads=[r_oTh], writes=[R["oT_s"]])


def phase_rwkv(g):
    P, A, R, nc = g.P, g.A, g.R, g.nc
    H = 8
    HP = 4
    RW = A["rwT_s"]
    with Phase(g) as ph:
        T = {}

        def mk(name, shape, dt=F32):
            t = ph.sb(shape, dt, name)
            T[name] = (t, Res(name))
            return t

        def tl(name):
            return T[name][0]

        def rs(*names):
            return [T[n][1] for n in names]

        def op(eng, fn, reads, writes):
            P.add(eng, fn, rs(*reads), rs(*writes))

        def hsl(h):
            return slice((h % 2) * 64, (h % 2) * 64 + 64), h // 2

        mk("w2", [64, 1024]); mk("a2", [64, 1024]); mk("g2a", [128, 1024]); mk("g2b", [32, 1024])
        P.dma(tl("w2")[:], A["rw_w2"], writes=rs("w2"))
        P.dma(tl("a2")[:], A["rw_a2"], writes=rs("a2"))
        P.dma(tl("g2a")[:], A["rw_g2"][0:128, :], writes=rs("g2a"))
        P.dma(tl("g2b")[:], A["rw_g2"][128:160, :], writes=rs("g2b"))
        psA = [ph.ps([128, HP, 128], F32, "psA") for _ in range(2)]
        r_psA = [Res("psA0"), Res("psA1")]
        for n in ("rw_w0", "rw_a0", "rw_k_k", "rw_k_a", "rw_r_k", "rw_gn_g", "rw_gn_b"):
            mk(n, [128, 8])
            load_vec_fm(g, ph, A[n], 1024, tl(n)[:, :], T[n][1], psA[0][:, 0, :], r_psA[0], parts=128)
        mk("ones", [128, 128]); mk("onesm", [128, 128]); mk("mS", [128, 64]); mk("mI", [128, 64]); mk("mL", [128, 64])
        mk("identS", [128, 64])
        mk("rst", [128, HP, 2, 64])
        for nm, val in (("ones", 1.0), ("onesm", 1.0 / 64.0)):
            op("pool", lambda e: e.memset(tl(nm)[:], 0.0), [], [nm])
            for h2 in range(2):
                q = slice(h2 * 64, h2 * 64 + 64)
                op("pool", lambda e: e.memset(tl(nm)[q, q], val), [nm], [nm])
        for nm, cmp_, cm, st_ in (("mS", ALU.is_gt, -1, 1), ("mI", ALU.is_ge, -1, 1), ("mL", ALU.is_gt, 1, -1)):
            op("pool", lambda e: e.memset(tl(nm)[:], 1.0), [], [nm])
            for h2 in range(2):
                q = slice(h2 * 64, h2 * 64 + 64)
                op("pool", lambda e: e.affine_select(tl(nm)[q, :], tl(nm)[q, :], [[st_, 64]], cmp_, 0.0, base=0, channel_multiplier=cm), [nm], [nm])
        for h2 in range(2):
            q = slice(h2 * 64, h2 * 64 + 64)
            P.add("pool", lambda e: e.tensor_copy(tl("identS")[q, :], g.ident_f[q, q]), reads=[g.r_ident], writes=rs("identS"))
        op("pool", lambda e: e.memset(tl("rst")[:], 1.0), [], ["rst"])
        op("pool", lambda e: e.memset(tl("rst")[:, :, :, 0:1], 0.0), ["rst"], ["rst"])

        def bcm(name):
            return tl(name)[:].unsqueeze(1).to_broadcast([128, HP, 64])

        for n in ("r_", "k_", "v_", "lw", "cl", "a_", "g_", "ep", "en", "ex", "kk", "kt", "bh", "t1", "bonus",
                  "yT", "t2"):
            mk(n, [128, HP, 128])
        for n in ("RG", "AG", "BI", "KI", "Bend", "Kend", "vb16"):
            mk(n, [128, HP, 128], BF16)
        mk("dwT", [64, 128]); mk("daT", [64, 128]); mk("dgT", [128, 128]); mk("dgT2", [32, 128])
        mk("gL", [128, HP, 2])
        mk("ob", [128, HP, 128], BF16)
        for cc in range(2):
            for n in ("Vt", "BeT", "KeT", "M", "N", "Aak", "Arb", "Ark", "TT", "M2", "N2"):
                mk("%s%d" % (n, cc), [128, HP, 64], BF16)
        mk("P", [128, HP, 64]); mk("Xs", [128, HP, 64], BF16); mk("Us", [128, HP, 64], BF16); mk("Pt", [128, HP, 64])
        mk("Pb", [128, HP, 64], BF16)
        psB = [ph.ps([128, HP, 64], F32, "psB") for _ in range(4)]
        r_psB = [Res("psB%d" % i) for i in range(4)]
        psC = ph.ps([128, HP * 128], F32, "psC")
        r_psC = Res("psC")
        cnt = {"A": 0, "B": 0}

        def nextB():
            i = cnt["B"] % 4
            cnt["B"] += 1
            return psB[i], r_psB[i]

        def flat(name):
            return tl(name)[:].rearrange("p h t -> p (h t)")

        def ones_reduce(lhs_name, src_name):
            P.mm(psC[:, :], tl(lhs_name)[:], flat(src_name), reads=rs(lhs_name, src_name), writes=[r_psC])

        psC3 = psC[:].rearrange("p (h t) -> p h t", h=HP)

        for hg in range(2):
            prs = slice(hg * HP, (hg + 1) * HP)

            def vb(name, n=128):
                return tl(name)[:, prs].unsqueeze(2).to_broadcast([128, HP, n])

            op("pool", lambda e: e.memset(tl("P")[:], 0.0), [], ["P"])
            op("pool", lambda e: e.memset(tl("Pb")[:], 0.0), [], ["Pb"])
            for blk in range(16):
                t0 = blk * 128
                ts_ = slice(t0, t0 + 128)
                for nm, base in (("r_", 0), ("k_", 1024), ("v_", 2048)):
                    P.dma(tl(nm)[:], RW[base + hg * 512:base + (hg + 1) * 512, ts_].rearrange("(hp p) t -> p hp t", p=128),
                          reads=[R["rwT_s"]], writes=rs(nm))
                P.dma(tl("dwT")[:], RW[3072:3136, ts_], reads=[R["rwT_s"]], writes=rs("dwT"))
                P.dma(tl("daT")[:], RW[3136:3200, ts_], reads=[R["rwT_s"]], writes=rs("daT"))
                P.dma(tl("dgT")[:], RW[3200:3328, ts_], reads=[R["rwT_s"]], writes=rs("dgT"))
                P.dma(tl("dgT2")[:], RW[3328:3360, ts_], reads=[R["rwT_s"]], writes=rs("dgT2"))
                P.act(tl("dwT")[:], tl("dwT")[:], AF.Tanh, reads=rs("dwT"), writes=rs("dwT"))
                P.act(tl("dgT")[:], tl("dgT")[:], AF.Sigmoid, reads=rs("dgT"), writes=rs("dgT"))
                P.act(tl("dgT2")[:], tl("dgT2")[:], AF.Sigmoid, reads=rs("dgT2"), writes=rs("dgT2"))
                for (wn, src, dst, bn) in (("w2", "dwT", "lw", "rw_w0"), ("a2", "daT", "a_", "rw_a0"), ("g2", "dgT", "g_", None)):
                    ai = cnt["A"] % 2
                    cnt["A"] += 1
                    for hp in range(HP):
                        pr = hg * HP + hp
                        cols = slice(pr * 128, (pr + 1) * 128)
                        if wn == "g2":
                            P.mm(psA[ai][:, hp, :], tl("g2a")[:, cols], tl("dgT")[:], start=True, stop=False, reads=rs("g2a", "dgT"), writes=[r_psA[ai]])
                            P.mm(psA[ai][:, hp, :], tl("g2b")[:, cols], tl("dgT2")[:], start=False, stop=True, reads=rs("g2b", "dgT2"), writes=[r_psA[ai]])
                        else:
                            P.mm(psA[ai][:, hp, :], tl(wn)[:, cols], tl(src)[:], reads=rs(wn, src), writes=[r_psA[ai]])
                    if bn is None:
                        P.add("dve", lambda e: e.tensor_copy(tl(dst)[:], psA[ai][:]), reads=[r_psA[ai]], writes=rs(dst))
                    else:
                        for hp in range(HP):
                            pr = hg * HP + hp
                            P.act(tl(dst)[:, hp, :], psA[ai][:, hp, :], AF.Sigmoid, reads=[r_psA[ai]] + rs(bn), writes=rs(dst), bias=tl(bn)[:, pr:pr + 1])
                op("dve", lambda e: e.tensor_scalar(tl("lw")[:], tl("lw")[:], -EXPM05, None, ALU.mult), ["lw"], ["lw"])
                op("dve", lambda e: e.tensor_tensor_scan(flat("cl"), tl("rst")[:].rearrange("p h c l -> p (h c l)"), flat("lw"), 0.0, ALU.mult, ALU.add),
                   ["lw", "rst"], ["cl"])
                P.act(tl("ep")[:], tl("cl")[:], AF.Exp, reads=rs("cl"), writes=rs("ep"))
                P.act(tl("en")[:], tl("cl")[:], AF.Exp, reads=rs("cl"), writes=rs("en"), scale=-1.0)
                op("dve", lambda e: e.tensor_tensor(tl("ex")[:], tl("cl")[:], tl("lw")[:], ALU.subtract), ["cl", "lw"], ["ex"])
                P.act(tl("ex")[:], tl("ex")[:], AF.Exp, reads=rs("ex"), writes=rs("ex"))
                op("pool", lambda e: e.tensor_copy(tl("gL")[:], tl("ep")[:].rearrange("p h (c l) -> p h c l", c=2)[:, :, :, 63]), ["ep"], ["gL"])
                op("dve", lambda e: e.tensor_tensor(tl("kk")[:], tl("k_")[:], vb("rw_k_k"), ALU.mult), ["k_", "rw_k_k"], ["kk"])
                op("pool", lambda e: e.tensor_tensor(tl("t1")[:], tl("kk")[:], tl("kk")[:], ALU.mult), ["kk"], ["t1"])
                ones_reduce("ones", "t1")
                P.add("dve", lambda e: e.tensor_scalar_max(tl("t1")[:], psC3, 1e-24), reads=[r_psC], writes=rs("t1"))
                P.act(tl("t1")[:], tl("t1")[:], AF.Sqrt, reads=rs("t1"), writes=rs("t1"))
                op("dve", lambda e: e.reciprocal(tl("t1")[:], tl("t1")[:]), ["t1"], ["t1"])
                op("dve", lambda e: e.tensor_tensor(tl("kk")[:], tl("kk")[:], tl("t1")[:], ALU.mult), ["kk", "t1"], ["kk"])
                op("dve", lambda e: e.scalar_tensor_tensor(tl("t2")[:], tl("a_")[:], -1.0, vb("rw_k_a"), ALU.add, ALU.mult), ["a_", "rw_k_a"], ["t2"])
                op("dve", lambda e: e.scalar_tensor_tensor(tl("kt")[:], tl("t2")[:], 1.0, tl("k_")[:], ALU.add, ALU.mult), ["t2", "k_"], ["kt"])
                op("pool", lambda e: e.tensor_tensor(tl("bh")[:], tl("kk")[:], tl("a_")[:], ALU.mult), ["kk", "a_"], ["bh"])
                op("pool", lambda e: e.tensor_tensor(tl("t2")[:], tl("r_")[:], tl("kt")[:], ALU.mult), ["r_", "kt"], ["t2"])
                op("pool", lambda e: e.tensor_tensor(tl("t2")[:], tl("t2")[:], vb("rw_r_k"), ALU.mult), ["t2", "rw_r_k"], ["t2"])
                ones_reduce("ones", "t2")
                P.add("dve", lambda e: e.tensor_tensor(tl("bonus")[:], psC3, tl("v_")[:], ALU.mult), reads=[r_psC] + rs("v_"), writes=rs("bonus"))
                op("pool", lambda e: e.tensor_tensor(tl("RG")[:], tl("r_")[:], tl("ep")[:], ALU.mult), ["r_", "ep"], ["RG"])
                op("dve", lambda e: e.scalar_tensor_tensor(tl("AG")[:], tl("kk")[:], -1.0, tl("ex")[:], ALU.mult, ALU.mult), ["kk", "ex"], ["AG"])
                gLb = tl("gL")[:].unsqueeze(3).to_broadcast([128, HP, 2, 64])
                op("pool", lambda e: e.tensor_tensor(tl("BI")[:], tl("bh")[:], tl("en")[:], ALU.mult), ["bh", "en"], ["BI"])
                op("pool", lambda e: e.tensor_tensor(tl("KI")[:], tl("kt")[:], tl("en")[:], ALU.mult), ["kt", "en"], ["KI"])
                op("dve", lambda e: e.tensor_tensor(tl("Bend")[:].rearrange("p h (c l) -> p h c l", c=2), tl("BI")[:].rearrange("p h (c l) -> p h c l", c=2), gLb, ALU.mult),
                   ["BI", "gL"], ["Bend"])
                op("dve", lambda e: e.tensor_tensor(tl("Kend")[:].rearrange("p h (c l) -> p h c l", c=2), tl("KI")[:].rearrange("p h (c l) -> p h c l", c=2), gLb, ALU.mult),
                   ["KI", "gL"], ["Kend"])
                op("act", lambda e: e.copy(tl("vb16")[:], tl("v_")[:]), ["v_"], ["vb16"])
                for cc in range(2):
                    c_ = slice(cc * 64, (cc + 1) * 64)
                    sfx = str(cc)
                    for k3, (src, dst) in enumerate((("vb16", "Vt"), ("Bend", "BeT"), ("Kend", "KeT"))):
                        pb, r_pb = nextB()
                        for h in range(H):
                            q, hp = hsl(h)
                            P.mm(pb[q, hp, :], tl(src)[q, hp, c_], g.ident_b[q, q], reads=rs(src) + [g.r_ident], writes=[r_pb])
                        if k3 % 2 == 0:
                            P.add("act", lambda e: e.copy(tl(dst + sfx)[:], pb[:]), reads=[r_pb], writes=rs(dst + sfx))
                        else:
                            P.add("dve", lambda e: e.tensor_copy(tl(dst + sfx)[:], pb[:]), reads=[r_pb], writes=rs(dst + sfx))
                    for (lh, rh, dst, msk) in (("BI", "AG", "M", "mS"), ("KI", "AG", "Aak", "mS"), ("BI", "RG", "Arb", "mI"), ("KI", "RG", "Ark", "mI"),
                                              ("AG", "BI", "N", "mL")):
                        pb, r_pb = nextB()
                        for h in range(H):
                            q, hp = hsl(h)
                            P.mm(pb[q, hp, :], tl(lh)[q, hp, c_], tl(rh)[q, hp, c_], reads=rs(lh, rh), writes=[r_pb])
                        P.add("dve", lambda e: e.tensor_tensor(tl(dst + sfx)[:], pb[:], bcm(msk), ALU.mult), reads=[r_pb] + rs(msk), writes=rs(dst + sfx))
                    op("pool", lambda e: e.tensor_tensor(tl("TT" + sfx)[:], tl("M" + sfx)[:], bcm("identS"), ALU.add), ["M" + sfx, "identS"], ["TT" + sfx])
                    cur_m, cur_n = "M" + sfx, "N" + sfx
                    oth_m, oth_n = "M2" + sfx, "N2" + sfx
                    for lvl in range(5):
                        last = (lvl == 4)
                        pn, r_pn = nextB()
                        for h in range(H):
                            q, hp = hsl(h)
                            P.mm(pn[q, hp, :], tl(cur_m)[q, hp, :], tl(cur_n)[q, hp, :], reads=rs(cur_m, cur_n), writes=[r_pn])
                        if not last:
                            pm, r_pm = nextB()
                            for h in range(H):
                                q, hp = hsl(h)
                                P.mm(pm[q, hp, :], tl(cur_n)[q, hp, :], tl(cur_m)[q, hp, :], reads=rs(cur_m, cur_n), writes=[r_pm])
                        P.add("act", lambda e: e.copy(tl(oth_n)[:], pn[:]), reads=[r_pn], writes=rs(oth_n))
                        if not last:
                            P.add("dve", lambda e: e.tensor_copy(tl(oth_m)[:], pm[:]), reads=[r_pm], writes=rs(oth_m))
                        pt_, r_pt = nextB()
                        for h in range(H):
                            q, hp = hsl(h)
                            P.mm(pt_[q, hp, :], tl(oth_n)[q, hp, :], tl("TT" + sfx)[q, hp, :], reads=rs(oth_n, "TT" + sfx), writes=[r_pt])
                        P.add("dve", lambda e: e.tensor_tensor(tl("TT" + sfx)[:], tl("TT" + sfx)[:], pt_[:], ALU.add), reads=[r_pt] + rs("TT" + sfx), writes=rs("TT" + sfx))
                        cur_m, oth_m = oth_m, cur_m
                        cur_n, oth_n = oth_n, cur_n
                for cc in range(2):
                    c_ = slice(cc * 64, (cc + 1) * 64)
                    sfx = str(cc)
                    px, r_px = nextB()
                    for h in range(H):
                        q, hp = hsl(h)
                        P.mm(px[q, hp, :], tl("AG")[q, hp, c_], tl("Pb")[q, hp, :], start=True, stop=False, reads=rs("AG", "Pb"), writes=[r_px])
                        P.mm(px[q, hp, :], tl("Aak" + sfx)[q, hp, :], tl("Vt" + sfx)[q, hp, :], start=False, stop=True, reads=rs("Aak" + sfx, "Vt" + sfx), writes=[r_px])
                    P.add("act", lambda e: e.copy(tl("Xs")[:], px[:]), reads=[r_px], writes=rs("Xs"))
                    pu, r_pu = nextB()
                    for h in range(H):
                        q, hp = hsl(h)
                        P.mm(pu[q, hp, :], tl("TT" + sfx)[q, hp, :], tl("Xs")[q, hp, :], reads=rs("TT" + sfx, "Xs"), writes=[r_pu])
                    P.add("dve", lambda e: e.tensor_copy(tl("Us")[:], pu[:]), reads=[r_pu], writes=rs("Us"))
                    py, r_py = nextB()
                    for h in range(H):
                        q, hp = hsl(h)
                        P.mm(py[q, hp, :], tl("Pb")[q, hp, :], tl("RG")[q, hp, c_], start=True, stop=False, reads=rs("Pb", "RG"), writes=[r_py])
                        P.mm(py[q, hp, :], tl("Us")[q, hp, :], tl("Arb" + sfx)[q, hp, :], start=False, stop=False, reads=rs("Us", "Arb" + sfx), writes=[r_py])
                        P.mm(py[q, hp, :], tl("Vt" + sfx)[q, hp, :], tl("Ark" + sfx)[q, hp, :], start=False, stop=True, reads=rs("Vt" + sfx, "Ark" + sfx), writes=[r_py])
                    P.add("act", lambda e: e.copy(tl("yT")[:, :, c_], py[:]), reads=[r_py], writes=rs("yT"))
                    pp, r_pp = nextB()
                    for h in range(H):
                        q, hp = hsl(h)
                        P.mm(pp[q, hp, :], tl("BeT" + sfx)[q, hp, :], tl("Us")[q, hp, :], start=True, stop=False, reads=rs("BeT" + sfx, "Us"), writes=[r_pp])
                        P.mm(pp[q, hp, :], tl("KeT" + sfx)[q, hp, :], tl("Vt" + sfx)[q, hp, :], start=False, stop=True, reads=rs("KeT" + sfx, "Vt" + sfx), writes=[r_pp])
                    op("pool", lambda e: e.tensor_tensor(tl("Pt")[:], tl("P")[:], tl("gL")[:, :, cc:cc + 1].to_broadcast([128, HP, 64]), ALU.mult), ["P", "gL"], ["Pt"])
                    P.add("dve", lambda e: e.tensor_tensor(tl("P")[:], tl("Pt")[:], pp[:], ALU.add), reads=[r_pp] + rs("Pt"), writes=rs("P"))
                    op("act", lambda e: e.copy(tl("Pb")[:], tl("P")[:]), ["P"], ["Pb"])
                ones_reduce("onesm", "yT")
                P.add("dve", lambda e: e.tensor_tensor(tl("yT")[:], tl("yT")[:], psC3, ALU.subtract), reads=[r_psC] + rs("yT"), writes=rs("yT"))
                op("pool", lambda e: e.tensor_tensor(tl("t1")[:], tl("yT")[:], tl("yT")[:], ALU.mult), ["yT"], ["t1"])
                ones_reduce("onesm", "t1")
                P.act(tl("t1")[:], psC3, AF.Sqrt, reads=[r_psC], writes=rs("t1"), bias=64e-5)
                op("dve", lambda e: e.reciprocal(tl("t1")[:], tl("t1")[:]), ["t1"], ["t1"])
                op("dve", lambda e: e.tensor_tensor(tl("yT")[:], tl("yT")[:], tl("t1")[:], ALU.mult), ["yT", "t1"], ["yT"])
                op("pool", lambda e: e.tensor_tensor(tl("yT")[:], tl("yT")[:], vb("rw_gn_g"), ALU.mult), ["yT", "rw_gn_g"], ["yT"])
                op("pool", lambda e: e.tensor_tensor(tl("yT")[:], tl("yT")[:], vb("rw_gn_b"), ALU.add), ["yT", "rw_gn_b"], ["yT"])
                op("dve", lambda e: e.tensor_tensor(tl("yT")[:], tl("yT")[:], tl("bonus")[:], ALU.add), ["yT", "bonus"], ["yT"])
                op("dve", lambda e: e.tensor_tensor(tl("ob")[:], tl("yT")[:], tl("g_")[:], ALU.mult), ["yT", "g_"], ["ob"])
                P.dma(A["oT_s"][1024 + hg * 512:1024 + (hg + 1) * 512, ts_].rearrange("(hp p) t -> p hp t", p=128), tl("ob")[:],
                      reads=rs("ob"), writes=[R["oT_s"]])


N_CORES = 4
_NC_CACHE = {}


def _core_inputs(inp, b):
    m = {"x": np.ascontiguousarray(inp["x"][b], dtype=np.float32)}
    for k, v in inp.items():
        if k == "x":
            continue
        v = np.asarray(v, dtype=np.float32)
        if k.startswith("ffn_") or k.startswith("ln_"):
            m[k] = np.ascontiguousarray(v)
        else:
            v0 = v[0]
            if k == "rw_r_k":
                v0 = v0.reshape(-1)
            m[k] = np.ascontiguousarray(v0)
    return m


def kernel(**inputs):
    if "nc" not in _NC_CACHE:
        _NC_CACHE["nc"] = build_program()
    nc = _NC_CACHE["nc"]
    in_maps = [_core_inputs(inputs, b) for b in range(N_CORES)]
    res = run_bass_kernel_spmd(nc, in_maps, core_ids=list(range(N_CORES)))
    out = np.stack([np.asarray(res.results[b]["y"], dtype=np.float32) for b in range(N_CORES)], axis=0)
    return out
```
